# Optimizing a Trainium2 kernel written in Bass

```python
import jax, jax.numpy as jnp
from jax import lax
import numpy as np

D_MODEL = 1024
BATCH = 8
SEQ = 4096
DEPTH = 2

PLE_DIM = 256
EPS = 1e-6

MLA_HEADS = 4
MLA_Q_RANK = 384
MLA_KV_RANK = 256
MLA_NOPE = 128
MLA_ROPE = 64
MLA_V = 128
MLA_WIDTH = MLA_HEADS * MLA_V
ROPE_THETA = 10000.0
Q_BLOCK = 128

GLA_HEADS = 4
GLA_DK = 64
GLA_DV = 128
GLA_WIDTH = GLA_HEADS * GLA_DV
GLA_GATE_RANK = 16
GLA_TAU = 16.0
GLA_CHUNK = 64

D_MIX = MLA_WIDTH + GLA_WIDTH

IN_SPLITS = (
    MLA_Q_RANK,
    MLA_KV_RANK,
    MLA_ROPE,
    MLA_WIDTH,
    GLA_HEADS * GLA_DK,
    GLA_HEADS * GLA_DK,
    GLA_WIDTH,
    GLA_GATE_RANK,
    GLA_GATE_RANK,
    GLA_WIDTH,
)
D_IN = sum(IN_SPLITS)

kernel_name = "hymba_mla_gla_bidir_encoder"


def rmsnorm(x, g):
    xf = x.astype(jnp.float32)
    y = xf * lax.rsqrt(jnp.mean(xf * xf, axis=-1, keepdims=True) + EPS)
    return (y * g.astype(jnp.float32)).astype(x.dtype)


def split_cols(u, sizes):
    out, start = [], 0
    for s in sizes:
        out.append(u[..., start:start + s])
        start += s
    return out


def apply_rope(x, pos):
    half = MLA_ROPE // 2
    inv = ROPE_THETA ** (-jnp.arange(half, dtype=jnp.float32) / half)
    ang = pos.astype(jnp.float32)[..., None] * inv
    cos, sin = jnp.cos(ang), jnp.sin(ang)
    if x.ndim == 4:
        cos, sin = cos[:, :, None, :], sin[:, :, None, :]
    xf = x.astype(jnp.float32)
    x1, x2 = xf[..., :half], xf[..., half:]
    return jnp.concatenate([x1 * cos - x2 * sin, x1 * sin + x2 * cos], axis=-1).astype(x.dtype)


def mla_branch(c_q, c_kv, k_rope, pos, q_norm, w_uq, kv_norm, w_ukv):
    B, S, _ = c_q.shape
    q = (rmsnorm(c_q, q_norm) @ w_uq).reshape(B, S, MLA_HEADS, MLA_NOPE + MLA_ROPE)
    q_nope = q[..., :MLA_NOPE]
    q_rope = apply_rope(q[..., MLA_NOPE:], pos)
    kv = (rmsnorm(c_kv, kv_norm) @ w_ukv).reshape(B, S, MLA_HEADS, MLA_NOPE + MLA_V)
    k_nope, v = kv[..., :MLA_NOPE], kv[..., MLA_NOPE:]
    k_r = apply_rope(k_rope, pos)
    scale = (MLA_NOPE + MLA_ROPE) ** -0.5
    nb = S // Q_BLOCK
    qn_b = q_nope.reshape(B, nb, Q_BLOCK, MLA_HEADS, MLA_NOPE).transpose(1, 0, 2, 3, 4)
    qr_b = q_rope.reshape(B, nb, Q_BLOCK, MLA_HEADS, MLA_ROPE).transpose(1, 0, 2, 3, 4)

    def block(args):
        qn, qr = args
        s = (jnp.einsum('bqhd,bkhd->bhqk', qn, k_nope)
             + jnp.einsum('bqhr,bkr->bhqk', qr, k_r))
        prob = jax.nn.softmax(s.astype(jnp.float32) * scale, axis=-1).astype(v.dtype)
        return jnp.einsum('bhqk,bkhd->bqhd', prob, v)

    o = lax.map(block, (qn_b, qr_b))
    return o.transpose(1, 0, 2, 3, 4).reshape(B, S, MLA_WIDTH)


def gla_chunked(q, k, v, log_a):
    B, S, H, DK = q.shape
    DV = v.shape[-1]
    C = GLA_CHUNK
    N = S // C

    def to_chunks(t):
        return t.astype(jnp.float32).reshape(B, N, C, H, -1).transpose(1, 0, 3, 2, 4)

    qc, kc, vc, gc = to_chunks(q), to_chunks(k), to_chunks(v), to_chunks(log_a)
    b = jnp.cumsum(gc, axis=-2)
    ref = b[..., C // 2 - 1:C // 2, :]
    q_in = qc * jnp.exp(b - ref)
    k_in = kc * jnp.exp(ref - b)
    mask = jnp.tril(jnp.ones((C, C), dtype=bool))
    A = jnp.where(mask, jnp.einsum('nbhid,nbhjd->nbhij', q_in, k_in), 0.0)
    o_intra = jnp.einsum('nbhij,nbhjv->nbhiv', A, vc)

    b_last = b[..., -1:, :]
    q_inter = qc * jnp.exp(b)
    k_state = kc * jnp.exp(b_last - b)
    decay = jnp.exp(b_last[..., 0, :])

    def step(state, inp):
        qi, ki, vi, di = inp
        o = jnp.einsum('bhid,bhdv->bhiv', qi, state)
        state = state * di[..., None] + jnp.einsum('bhjd,bhjv->bhdv', ki, vi)
        return state, o

    s0 = jnp.zeros((B, H, DK, DV), jnp.float32)
    _, o_inter = lax.scan(step, s0, (q_inter, k_state, vc, decay))
    o = o_intra + o_inter
    return o.transpose(1, 0, 3, 2, 4).reshape(B, S, H, DV).astype(v.dtype)


def gla_branch(q, k, v, lr_f, lr_b, w_g_f, b_g_f, w_g_b, b_g_b, out_norm):
    B, S, _ = q.shape
    q = (q * GLA_DK ** -0.5).reshape(B, S, GLA_HEADS, GLA_DK)
    k = k.reshape(B, S, GLA_HEADS, GLA_DK)
    v = v.reshape(B, S, GLA_HEADS, GLA_DV)
    la_f = (jax.nn.log_sigmoid((lr_f @ w_g_f + b_g_f).astype(jnp.float32)) / GLA_TAU)
    la_b = (jax.nn.log_sigmoid((lr_b @ w_g_b + b_g_b).astype(jnp.float32)) / GLA_TAU)
    la_f = la_f.reshape(B, S, GLA_HEADS, GLA_DK)
    la_b = la_b.reshape(B, S, GLA_HEADS, GLA_DK)
    o_f = gla_chunked(q, k, v, la_f)
    flip = lambda t: jnp.flip(t, axis=1)
    o_b = flip(gla_chunked(flip(q), flip(k), flip(v), flip(la_b)))
    o = rmsnorm(o_f + o_b, out_norm)
    return o.reshape(B, S, GLA_WIDTH)


def setup_inputs(seed: int = 0) -> dict:
    key = jax.random.key(seed)
    ks = jax.random.split(key, 24)
    nrm = lambda k, shape, fan_in: jax.random.normal(k, shape, jnp.float32) * fan_in ** -0.5
    gain = lambda k, shape: 1.0 + 0.02 * jax.random.normal(k, shape, jnp.float32)
    L = DEPTH
    x = jax.random.normal(ks[0], (BATCH, SEQ, D_MODEL), jnp.float32)
    p = jax.random.normal(ks[1], (DEPTH, BATCH, SEQ, PLE_DIM), jnp.float32)
    offs = jax.random.randint(ks[2], (BATCH, 1), 0, 1024, dtype=jnp.int32)
    positions = offs + jnp.arange(SEQ, dtype=jnp.int32)[None, :]
    return {
        "x": x,
        "p": p,
        "positions": positions,
        "ln_mix": gain(ks[3], (L, D_MODEL)),
        "w_in": nrm(ks[4], (L, D_MODEL, D_IN), D_MODEL),
        "mla_q_norm": gain(ks[5], (L, MLA_Q_RANK)),
        "w_uq": nrm(ks[6], (L, MLA_Q_RANK, MLA_HEADS * (MLA_NOPE + MLA_ROPE)), MLA_Q_RANK),
        "mla_kv_norm": gain(ks[7], (L, MLA_KV_RANK)),
        "w_ukv": nrm(ks[8], (L, MLA_KV_RANK, MLA_HEADS * (MLA_NOPE + MLA_V)), MLA_KV_RANK),
        "gla_w_gate_fwd": nrm(ks[9], (L, GLA_GATE_RANK, GLA_HEADS * GLA_DK), GLA_GATE_RANK),
        "gla_b_gate_fwd": 0.1 * jax.random.normal(ks[10], (L, GLA_HEADS * GLA_DK), jnp.float32),
        "gla_w_gate_bwd": nrm(ks[11], (L, GLA_GATE_RANK, GLA_HEADS * GLA_DK), GLA_GATE_RANK),
        "gla_b_gate_bwd": 0.1 * jax.random.normal(ks[12], (L, GLA_HEADS * GLA_DK), jnp.float32),
        "gla_out_norm": gain(ks[13], (L, GLA_DV)),
        "w_out": nrm(ks[14], (L, D_MIX, D_MODEL), D_MIX),
        "ple_norm": gain(ks[15], (L, D_MODEL)),
        "w_ple_gate": nrm(ks[16], (L, D_MODEL, D_MODEL), D_MODEL),
        "w_ple_proj": nrm(ks[17], (L, PLE_DIM, D_MODEL), PLE_DIM),
        "final_norm": gain(ks[18], (D_MODEL,)),
    }


def reference(x, p, positions, ln_mix, w_in, mla_q_norm, w_uq, mla_kv_norm, w_ukv,
              gla_w_gate_fwd, gla_b_gate_fwd, gla_w_gate_bwd, gla_b_gate_bwd,
              gla_out_norm, w_out, ple_norm, w_ple_gate, w_ple_proj, final_norm):
    h = x
    for i in range(DEPTH):
        u = rmsnorm(h, ln_mix[i]) @ w_in[i]
        (c_q, c_kv, k_rope, gate_a, gq, gk, gv, lr_f, lr_b, gate_g) = split_cols(u, IN_SPLITS)
        y_mla = mla_branch(c_q, c_kv, k_rope, positions,
                           mla_q_norm[i], w_uq[i], mla_kv_norm[i], w_ukv[i]) * jax.nn.silu(gate_a)
        y_gla = gla_branch(gq, gk, gv, lr_f, lr_b,
                           gla_w_gate_fwd[i], gla_b_gate_fwd[i],
                           gla_w_gate_bwd[i], gla_b_gate_bwd[i],
                           gla_out_norm[i]) * jax.nn.silu(gate_g)
        h = h + jnp.concatenate([y_mla, y_gla], axis=-1) @ w_out[i]
        gate = jax.nn.sigmoid(rmsnorm(h, ple_norm[i]) @ w_ple_gate[i])
        h = h + gate * (p[i] @ w_ple_proj[i])
    return rmsnorm(h, final_norm)
```

```python
import contextlib
import numpy as np
import concourse.bass as bass
import concourse.mybir as mybir
from concourse.bass_utils import run_bass_kernel_spmd

F32 = mybir.dt.float32
BF16 = mybir.dt.bfloat16
I32 = mybir.dt.int32
AF = mybir.ActivationFunctionType
ALU = mybir.AluOpType
AX = mybir.AxisListType

S = 4096
D = 1024
NCH = 8
CH = 512
L = 2
EPS = 1e-6
SCALE = float((128 + 64) ** -0.5)
SAME_ENGINE_SYNC = True
import os
SKIP = set(os.environ.get('DBG_SKIP', '').split(','))


class T:
    __slots__ = ("ap", "name", "w", "r", "psum")

    def __init__(self, ap, name, psum=False):
        self.ap = ap
        self.name = name
        self.w = None
        self.r = []
        self.psum = psum

    def __getitem__(self, i):
        return self.ap[i]


class Ctx:
    def __init__(self, nc, stack):
        self.nc = nc
        self.stack = stack
        self.eng = {}
        for name, h in [("pe", nc.tensor), ("act", nc.scalar), ("dve", nc.vector),
                        ("pool", nc.gpsimd), ("sp", nc.sync)]:
            sem = stack.enter_context(nc.semaphore("s_" + name))
            self.eng[name] = dict(h=h, sem=sem, cnt=0, waited={}, name=name)
        self.dsem = {}
        self.n_inst = 0

    def _sem(self, key):
        if key in self.eng:
            return self.eng[key]["sem"]
        return self.dsem[key][0]

    def _need(self, e, stamp, acc):
        key, val = stamp
        if key == e["name"]:
            if key == "pe" or not SAME_ENGINE_SYNC:
                return
        if e["waited"].get(key, 0) >= val:
            return
        if acc.get(key, 0) < val:
            acc[key] = val

    def _deps(self, e, reads, writes):
        acc = {}
        for t in reads:
            if t.w is not None:
                self._need(e, t.w, acc)
            if t.psum:
                for st in t.r:
                    self._need(e, st, acc)
        for t in writes:
            if t.w is not None:
                self._need(e, t.w, acc)
            for st in t.r:
                self._need(e, st, acc)
        for key, val in acc.items():
            e["h"].wait_ge(self._sem(key), val)
            e["waited"][key] = val

    def _mark(self, stamp, reads, writes):
        for t in reads:
            if t.psum:
                t.w = stamp
                t.r = []
                continue
            t.r.append(stamp)
            if len(t.r) > 64:
                best = {}
                for k, v in t.r:
                    if best.get(k, 0) < v:
                        best[k] = v
                t.r = list(best.items())
        for t in writes:
            t.w = stamp
            t.r = []

    def op(self, engname, fn, reads=(), writes=(), inc=True):
        e = self.eng[engname]
        self._deps(e, reads, writes)
        ins = fn()
        self.n_inst += 1
        if inc:
            e["cnt"] += 1
            ins.then_inc(e["sem"], 1)
            stamp = (engname, e["cnt"])
        else:
            stamp = (engname, e["cnt"] + 1)
        self._mark(stamp, reads, writes)
        return ins

    def dma(self, q, out, in_, key, reads=(), writes=()):
        e = self.eng[q]
        if q == "pool":
            key = key + "_sw"
        self._deps(e, reads, writes)
        if key not in self.dsem:
            sem = self.stack.enter_context(self.nc.semaphore("d_" + key))
            self.dsem[key] = [sem, 0]
        d = self.dsem[key]
        d[1] += 16
        e["h"].dma_start(out=out, in_=in_).then_inc(d[0], 16)
        self.n_inst += 1
        self._mark((key, d[1]), reads, writes)

    def barrier(self):
        for en, e in self.eng.items():
            for xn, x in self.eng.items():
                if xn != en and x["cnt"] > 0 and e["waited"].get(xn, 0) < x["cnt"]:
                    e["h"].wait_ge(x["sem"], x["cnt"])
                    e["waited"][xn] = x["cnt"]
            for key, d in self.dsem.items():
                if d[1] > 0 and e["waited"].get(key, 0) < d[1]:
                    e["h"].wait_ge(d[0], d[1])
                    e["waited"][key] = d[1]


def build(n_layers=L, debug=False, stop_after=None):
    nc = bass.Bass("TRN2", target_bir_lowering=False)
    dt = nc.dram_tensor

    def din(name, shape, dtype=F32):
        return dt(name, list(shape), dtype, kind="ExternalInput").ap()

    x_d = din("x", [S, D])
    p_d = din("p", [L, S, 256])
    pos_d = din("pos", [1, S], I32)
    wa_d = din("wa", [L, 128, 8, 512])
    wb_d = din("wb", [L, 128, 8, 896])
    wg_d = din("wg", [L, 128, 8, 1824])
    wuq_d = din("wuq", [L, 128, 3, 1024])
    wukv_d = din("wukv", [L, 128, 2, 1024])
    wout_d = din("wout", [L, 128, 8, 1024])
    wpg_d = din("wpg", [L, 128, 8, 1024])
    wpp_d = din("wpp", [L, 128, 2, 1024])
    gmix_d = din("gmix", [128, L, 8])
    gq_d = din("gq", [128, L, 3])
    gkv_d = din("gkv", [128, L, 2])
    goutn_d = din("goutn", [128, L])
    gple_d = din("gple", [128, L, 8])
    gfin_d = din("gfin", [128, 8])
    wgf_d = din("wgf", [16, L, 256])
    wgb_d = din("wgb", [16, L, 256])
    bgf_d = din("bgf", [1, L * 256])
    bgb_d = din("bgb", [1, L * 256])
    ident_d = din("ident", [128, 128])
    tri_d = din("tri", [128, 6, 128])
    mask_d = din("mask", [128, 2, 256])
    invf_d = din("invf", [128, 1])

    out_d = dt("out", [S, D], F32, kind="ExternalOutput").ap()
    skind = "ExternalOutput" if debug else "Internal"
    hT_d = dt("hT_s", [D, S], F32, kind=skind).ap()
    xnT_d = dt("xnT_s", [D, S], BF16, kind=skind).ap()
    yT_d = dt("yT_s", [D, S], BF16, kind=skind).ap()
    cs_d = dt("cs_s", [2, 128, S], F32, kind=skind).ap()

    hT_v = hT_d.rearrange("(k p) t -> p k t", p=128)
    xnT_v = xnT_d.rearrange("(k p) t -> p k t", p=128)
    yT_v = yT_d.rearrange("(k p) t -> p k t", p=128)

    with contextlib.ExitStack() as stack:
        cx = Ctx(nc, stack)
        op = cx.op

        uid = [0]

        def sb(st, name, shape, dtype):
            uid[0] += 1
            return st.enter_context(nc.sbuf_tensor("sb%d_%s" % (uid[0], name), list(shape), dtype))

        ident_f = T(sb(stack, "ident_f", [128, 128], F32), "ident_f")
        ident_b = T(sb(stack, "ident_b", [128, 128], BF16), "ident_b")
        ones_b = T(sb(stack, "ones_b", [128, 128], BF16), "ones_b")
        ones_f = T(sb(stack, "ones_f", [128, 128], F32), "ones_f")
        tri = T(sb(stack, "tri", [128, 6, 128], F32), "tri")
        mask = T(sb(stack, "mask", [128, 2, 256], F32), "mask")
        gmix = T(sb(stack, "gmix", [128, L, 8], F32), "gmix")
        gq = T(sb(stack, "gq", [128, L, 3], F32), "gq")
        gkv = T(sb(stack, "gkv", [128, L, 2], F32), "gkv")
        goutn = T(sb(stack, "goutn", [128, L], F32), "goutn")
        gple = T(sb(stack, "gple", [128, L, 8], F32), "gple")
        gfin = T(sb(stack, "gfin", [128, 8], F32), "gfin")
        invf = T(sb(stack, "invf", [128, 1], F32), "invf")
        wgs = T(sb(stack, "wgs", [48, L, 256], F32), "wgs")
        wg_b = T(sb(stack, "wg_b", [48, L, 256], BF16), "wg_b")
        bgs = [T(sb(stack, "bgs%d" % d, [1, L * 256], F32), "bgs%d" % d) for d in range(2)]
        bgh = [T(sb(stack, "bgh%d" % d, [1, 2, L * 256], BF16), "bgh%d" % d) for d in range(2)]
        bgt = T(sb(stack, "bgt", [1, L * 256], F32), "bgt")
        epsb = T(sb(stack, "epsb", [128, 1], F32), "epsb")
        PS = [T(stack.enter_context(nc.psum_tensor("ps%d" % i, [128, 512], F32)), "ps%d" % i, psum=True)
              for i in range(8)]

        for t, d in [(ident_f, ident_d), (tri, tri_d), (mask, mask_d), (gmix, gmix_d), (gq, gq_d),
                     (gkv, gkv_d), (goutn, goutn_d), (gple, gple_d), (gfin, gfin_d), (invf, invf_d),
]:
            cx.dma("sp", t.ap[:], d, "const", writes=[t])
        cx.dma("sp", wgs.ap[0:16], wgf_d, "const", writes=[wgs])
        cx.dma("sp", wgs.ap[32:48], wgb_d, "const", writes=[wgs])
        cx.dma("sp", bgs[0].ap[:], bgf_d, "const", writes=[bgs[0]])
        cx.dma("sp", bgs[1].ap[:], bgb_d, "const", writes=[bgs[1]])
        cx.barrier()
        op("dve", lambda: nc.vector.memset(ones_b.ap[:], 1.0), writes=[ones_b])
        op("dve", lambda: nc.vector.memset(ones_f.ap[:], 1.0), writes=[ones_f])
        for d in range(2):
            op("dve", lambda d=d: nc.vector.tensor_copy(bgh[d].ap[0:1, 0, :], bgs[d].ap[:, :]), reads=[bgs[d]], writes=[bgh[d]])
            op("dve", lambda d=d: nc.vector.tensor_tensor(out=bgt.ap[:, :], in0=bgs[d].ap[:, :], in1=bgh[d].ap[0:1, 0, :], op=ALU.subtract),
               reads=[bgs[d], bgh[d]], writes=[bgt])
            op("dve", lambda d=d: nc.vector.tensor_copy(bgh[d].ap[0:1, 1, :], bgt.ap[:, :]), reads=[bgt], writes=[bgh[d]])
        op("dve", lambda: nc.vector.memset(epsb.ap[:], EPS), writes=[epsb])
        op("dve", lambda: nc.vector.tensor_copy(ident_b.ap[:], ident_f.ap[:]), reads=[ident_f], writes=[ident_b])
        op("dve", lambda: nc.vector.tensor_copy(wg_b.ap[0:16], wgs.ap[0:16]), reads=[wgs], writes=[wg_b])
        op("dve", lambda: nc.vector.tensor_copy(wg_b.ap[32:48], wgs.ap[32:48]), reads=[wgs], writes=[wg_b])
        cx.barrier()

        hT_c = [T(None, "hTd%d" % c) for c in range(NCH)]
        xnT_c = [T(None, "xnTd%d" % c) for c in range(NCH)]
        yT_c = [T(None, "yTd%d" % c) for c in range(NCH)]
        cs_c = [T(None, "csd%d" % c) for c in range(NCH)]

        def rstd_from_psum(ps_t, ps_ap, n_feat, sq_t, rs_t, parts=128, cols=CH):
            op("act", lambda: nc.scalar.activation(sq_t.ap[0:parts, 0:cols], ps_ap, AF.Ln,
                                                    bias=epsb.ap[0:parts, :], scale=1.0 / n_feat),
               reads=[ps_t], writes=[sq_t])
            op("act", lambda: nc.scalar.activation(rs_t.ap[0:parts, 0:cols], sq_t.ap[0:parts, 0:cols], AF.Exp, scale=-0.5),
               reads=[sq_t], writes=[rs_t])

        def rms_fm(h_t, nk, sq_t, tmp_t, rs_t, ps_t, n_feat):
            op("pool", lambda: nc.gpsimd.tensor_tensor(out=sq_t.ap[:, 0:nk, :], in0=h_t.ap[:, 0:nk, :],
                                                       in1=h_t.ap[:, 0:nk, :], op=ALU.mult),
               reads=[h_t], writes=[sq_t])
            for k in range(nk):
                op("pe", lambda k=k: nc.tensor.matmul(ps_t.ap[:, :], ones_b.ap[:, :], sq_t.ap[:, k, :],
                                                      start=(k == 0), stop=(k == nk - 1)),
                   reads=[sq_t], writes=[ps_t], inc=(k == nk - 1))
            rstd_from_psum(ps_t, ps_t.ap[:, :], n_feat, tmp_t, rs_t)

        def load_weight(st_list, w_dram_l, nk, ncols, dst, gain_ap_fn, neg_cols=None, q="sp"):
            for k in range(nk):
                stg = st_list[k % len(st_list)]
                cx.dma(q, stg.ap[:, 0:ncols], w_dram_l[:, k, :], stg.name, writes=[stg])
                g = gain_ap_fn(k)
                if g is None:
                    op("dve", lambda k=k, stg=stg: nc.vector.tensor_copy(dst.ap[:, k, :], stg.ap[:, 0:ncols]),
                       reads=[stg], writes=[dst])
                elif k % 2 == 1:
                    op("act", lambda k=k, stg=stg, g=g: nc.scalar.activation(
                        dst.ap[:, k, :], stg.ap[:, 0:ncols], AF.Copy, scale=g),
                       reads=[stg], writes=[dst])
                    if neg_cols is not None:
                        for (a, b) in neg_cols:
                            op("dve", lambda k=k, a=a, b=b: nc.vector.tensor_scalar(
                                dst.ap[:, k, a:b], dst.ap[:, k, a:b], -1.0, None, ALU.mult),
                               reads=[dst], writes=[dst])
                else:
                    op("dve", lambda k=k, stg=stg, g=g: nc.vector.tensor_scalar(
                        dst.ap[:, k, :], stg.ap[:, 0:ncols], g, None, ALU.mult),
                       reads=[stg], writes=[dst])
                    if neg_cols is not None:
                        for (a, b) in neg_cols:
                            op("dve", lambda k=k, a=a, b=b: nc.vector.tensor_scalar(
                                dst.ap[:, k, a:b], dst.ap[:, k, a:b], -1.0, None, ALU.mult),
                               reads=[dst], writes=[dst])

        with contextlib.ExitStack() as ph:
            posi = T(sb(ph, "posi", [128, S], I32), "posi")
            posf = T(sb(ph, "posf", [128, S], F32), "posf")
            tri_i = T(sb(ph, "tri_i", [128, S], I32), "tri_i")
            frac = T(sb(ph, "frac", [128, S], F32), "frac")
            tab = T(sb(ph, "tab", [128, S], F32), "tab")
            cx.dma("sp", posi.ap[:], pos_d.partition_broadcast(128), "posi", writes=[posi])
            op("dve", lambda: nc.vector.tensor_copy(posf.ap[:], posi.ap[:]), reads=[posi], writes=[posf])
            op("dve", lambda: nc.vector.tensor_scalar(posf.ap[:], posf.ap[:], invf.ap[:, 0:1], None, ALU.mult),
               reads=[posf, invf], writes=[posf])
            op("dve", lambda: nc.vector.tensor_copy(tri_i.ap[:], posf.ap[:]), reads=[posf], writes=[tri_i])
            op("dve", lambda: nc.vector.tensor_copy(frac.ap[:], tri_i.ap[:]), reads=[tri_i], writes=[frac])
            op("dve", lambda: nc.vector.tensor_tensor(out=frac.ap[:], in0=posf.ap[:], in1=frac.ap[:], op=ALU.subtract),
               reads=[posf, frac], writes=[frac])
            for which, shift in ((0, 0.25), (1, 0.0)):
                op("dve", lambda shift=shift: nc.vector.tensor_scalar(tab.ap[:], frac.ap[:], shift, None, ALU.add),
                   reads=[frac], writes=[tab])
                op("dve", lambda: nc.vector.tensor_single_scalar(posf.ap[:], tab.ap[:], 0.5, ALU.is_gt),
                   reads=[tab], writes=[posf])
                op("dve", lambda: nc.vector.tensor_tensor(out=tab.ap[:], in0=tab.ap[:], in1=posf.ap[:], op=ALU.subtract),
                   reads=[tab, posf], writes=[tab])
                op("dve", lambda: nc.vector.scalar_tensor_tensor(tab.ap[:], tab.ap[:], -0.5, tab.ap[:], ALU.is_lt, ALU.add),
                   reads=[tab], writes=[tab])
                op("act", lambda: nc.scalar.activation(tab.ap[:], tab.ap[:], AF.Sin, scale=6.283185),
                   reads=[tab], writes=[tab])
                cx.dma("pool", cs_d[which], tab.ap[:], "tab", reads=[tab], writes=cs_c)
            cx.barrier()

        def emit_xn(h_t, sq_t, tmp_t, rs_t, ps_t, xn_t, c, store_h):
            if store_h:
                cx.dma("pool", hT_v[:, :, c * CH:(c + 1) * CH], h_t.ap[:], h_t.name, reads=[h_t], writes=[hT_c[c]])
            rms_fm(h_t, 8, sq_t, tmp_t, rs_t, ps_t, D)
            for k in range(8):
                op("dve", lambda k=k: nc.vector.tensor_tensor(out=xn_t.ap[:, k, :], in0=h_t.ap[:, k, :],
                                                              in1=rs_t.ap[:, :], op=ALU.mult),
                   reads=[h_t, rs_t], writes=[xn_t])
            cx.dma("pool", xnT_v[:, :, c * CH:(c + 1) * CH], xn_t.ap[:], xn_t.name, reads=[xn_t], writes=[xnT_c[c]])

        def run_staggered(gen_fn, n):
            active = {0: gen_fn(0)}
            nxt = 1
            want = False
            while active:
                for i_ in sorted(active):
                    try:
                        if next(active[i_]) == "half":
                            want = True
                    except StopIteration:
                        del active[i_]
                if want and nxt < n and (nxt - 2) not in active:
                    active[nxt] = gen_fn(nxt)
                    nxt += 1
                    want = False

        with contextlib.ExitStack() as ph:
            Z2 = range(2)
            xin = [[T(sb(ph, "xin%d_%d" % (z, i), [128, D], F32), "xin%d_%d" % (z, i)) for i in range(2)] for z in Z2]
            hb = [T(sb(ph, "hb%d" % i, [128, 8, CH], F32), "hb%d" % i) for i in Z2]
            sqb = [T(sb(ph, "sqb%d" % i, [128, 8, CH], BF16), "sqb%d" % i) for i in Z2]
            tmpb = [T(sb(ph, "tmpb%d" % i, [128, CH], F32), "tmpb%d" % i) for i in Z2]
            rsb = [T(sb(ph, "rsb%d" % i, [128, CH], F32), "rsb%d" % i) for i in Z2]
            xnb = [T(sb(ph, "xnb%d" % i, [128, 8, CH], BF16), "xnb%d" % i) for i in Z2]

            def pro_gen(c):
                z = c % 2
                banks = [PS[4 * z + i] for i in range(4)]
                h_t = hb[z]
                for j in range(4):
                    tl = c * 4 + j
                    xi = xin[z][j % 2]
                    cx.dma("sp", xi.ap[:], x_d[tl * 128:(tl + 1) * 128, :], xi.name, writes=[xi])
                    yield
                    for half in range(2):
                        pt = banks[(j * 2 + half) % 3]
                        for q4 in range(4):
                            k = half * 4 + q4
                            op("pe", lambda: nc.tensor.transpose(
                                pt.ap[:, q4 * 128:(q4 + 1) * 128], xi.ap[:, k * 128:(k + 1) * 128], ident_f.ap[:, :]),
                               reads=[xi], writes=[pt], inc=(q4 == 3))
                        yield
                        src = pt.ap[:, :].rearrange("p (k t) -> p k t", k=4)
                        dst = h_t.ap[:, half * 4:(half + 1) * 4, j * 128:(j + 1) * 128]
                        if half == 0:
                            op("act", lambda: nc.scalar.copy(dst, src), reads=[pt], writes=[h_t])
                        else:
                            op("dve", lambda: nc.vector.tensor_copy(dst, src), reads=[pt], writes=[h_t])
                        yield
                yield "half"
                cx.dma("pool", hT_v[:, :, c * CH:(c + 1) * CH], h_t.ap[:], h_t.name, reads=[h_t], writes=[hT_c[c]])
                op("act", lambda: nc.scalar.activation(sqb[z].ap[:, 0:4, :], h_t.ap[:, 0:4, :], AF.Square),
                   reads=[h_t], writes=[sqb[z]])
                yield
                op("pool", lambda: nc.gpsimd.tensor_tensor(out=sqb[z].ap[:, 4:8, :], in0=h_t.ap[:, 4:8, :],
                                                           in1=h_t.ap[:, 4:8, :], op=ALU.mult),
                   reads=[h_t], writes=[sqb[z]])
                for _ in range(5):
                    yield
                for k in range(8):
                    op("pe", lambda: nc.tensor.matmul(banks[3].ap[:, :], ones_b.ap[:, :], sqb[z].ap[:, k, :],
                                                      start=(k == 0), stop=(k == 7)),
                       reads=[sqb[z]], writes=[banks[3]], inc=(k == 7))
                yield
                op("act", lambda: nc.scalar.activation(tmpb[z].ap[:, :], banks[3].ap[:, :], AF.Ln,
                                                        bias=epsb.ap[:, :], scale=1.0 / D),
                   reads=[banks[3]], writes=[tmpb[z]])
                yield
                op("act", lambda: nc.scalar.activation(rsb[z].ap[:, :], tmpb[z].ap[:, :], AF.Exp, scale=-0.5),
                   reads=[tmpb[z]], writes=[rsb[z]])
                yield
                for k in range(8):
                    op("dve", lambda: nc.vector.tensor_tensor(out=xnb[z].ap[:, k, :], in0=h_t.ap[:, k, :],
                                                              in1=rsb[z].ap[:, :], op=ALU.mult),
                       reads=[h_t, rsb[z]], writes=[xnb[z]])
                    yield
                cx.dma("pool", xnT_v[:, :, c * CH:(c + 1) * CH], xnb[z].ap[:], xnb[z].name, reads=[xnb[z]], writes=[xnT_c[c]])
                yield

            run_staggered(pro_gen, NCH)
            cx.barrier()

        if stop_after == "prologue":
            n_layers = 0

        for l in range(n_layers):
            last = (l == L - 1)
            with contextlib.ExitStack() as p1:
                KT = [T(sb(p1, "KT%d" % c, [128, 4, CH], BF16), "KT%d" % c) for c in range(NCH)]
                krT = [T(sb(p1, "krT%d" % c, [128, CH], BF16), "krT%d" % c) for c in range(NCH)]
                Vt = [T(sb(p1, "V%d" % c, [128, 4, 512], BF16), "V%d" % c) for c in range(NCH)]
                xnb = [T(sb(p1, "xnc%d" % i, [128, 8, CH], BF16), "xnc%d" % i) for i in range(2)]
                csb = [[T(sb(p1, "cs%d_%d" % (w, i), [128, CH], F32), "cs%d_%d" % (w, i)) for i in range(2)]
                       for w in range(2)]
                with contextlib.ExitStack() as p1a:
                    stg = [T(sb(p1a, "stgA%d" % i, [128, 1024], F32), "stgA%d" % i) for i in range(2)]
                    wa = T(sb(p1a, "wa", [128, 8, 512], BF16), "wa")
                    wukv = T(sb(p1a, "wukv", [128, 2, 1024], BF16), "wukv")
                    ckv = T(sb(p1a, "ckv", [128, 2, CH], BF16), "ckv")
                    sq = T(sb(p1a, "sqkv", [128, 2, CH], BF16), "sqkv")
                    tmp = T(sb(p1a, "tmpkv", [128, CH], F32), "tmpkv")
                    rs = T(sb(p1a, "rskv", [128, CH], F32), "rskv")
                    tmpt = T(sb(p1a, "tmpt", [128, 4], F32), "tmpt")
                    rst = T(sb(p1a, "rst", [128, 4], F32), "rst")
                    t1 = T(sb(p1a, "t1", [128, CH], F32), "t1")
                    t2 = T(sb(p1a, "t2", [128, CH], F32), "t2")
                    load_weight(stg, wa_d[l], 8, 512, wa, lambda k: gmix.ap[:, l, k:k + 1],
                                neg_cols=[(384, 416), (448, 480)])
                    load_weight(stg, wukv_d[l], 2, 1024, wukv, lambda k: gkv.ap[:, l, k:k + 1])
                    for c in (range(NCH) if 'loop' not in SKIP else []):
                        xn = xnb[c % 2]
                        cs0, cs1 = csb[0][c % 2], csb[1][c % 2]
                        cx.dma("sp", xn.ap[:], xnT_v[:, :, c * CH:(c + 1) * CH], xn.name, reads=[xnT_c[c]], writes=[xn])
                        cx.dma("sp", cs0.ap[:], cs_d[0, :, c * CH:(c + 1) * CH], cs0.name, reads=[cs_c[c]], writes=[cs0])
                        cx.dma("sp", cs1.ap[:], cs_d[1, :, c * CH:(c + 1) * CH], cs1.name, reads=[cs_c[c]], writes=[cs1])
                        if 'ckv' in SKIP:
                            continue
                        for m in range(2):
                            pt = PS[m]
                            for k in range(8):
                                op("pe", lambda m=m, k=k, pt=pt: nc.tensor.matmul(
                                    pt.ap[:, :], wa.ap[:, k, m * 128:(m + 1) * 128], xn.ap[:, k, :],
                                    start=(k == 0), stop=(k == 7)), reads=[wa, xn], writes=[pt], inc=(k == 7))
                            if 'cp' not in SKIP:
                                op("dve", lambda m=m, pt=pt: nc.vector.tensor_copy(ckv.ap[:, m, :], pt.ap[:, :]), reads=[pt], writes=[ckv])
                            if 'sq' not in SKIP:
                                op("act", lambda m=m, pt=pt: nc.scalar.activation(sq.ap[:, m, :], pt.ap[:, :], AF.Square),
                                   reads=[pt], writes=[sq])
                        if 'ss' in SKIP:
                            continue
                        for m in range(2):
                            op("pe", lambda m=m: nc.tensor.matmul(PS[2].ap[:, :], ones_b.ap[:, :], sq.ap[:, m, :],
                                                                  start=(m == 0), stop=(m == 1)),
                               reads=[sq], writes=[PS[2]], inc=(m == 1))
                        rstd_from_psum(PS[2], PS[2].ap[:, :], 256, tmp, rs)
                        for j in (range(4) if 'rst' not in SKIP else []):
                            for m in range(2):
                                op("pe", lambda m=m, j=j: nc.tensor.matmul(
                                    PS[3].ap[:, j:j + 1], sq.ap[:, m, j * 128:(j + 1) * 128], ones_b.ap[:, 0:1],
                                    start=(m == 0), stop=(m == 1)), reads=[sq], writes=[PS[3]],
                                   inc=(m == 1 and j == 3))
                        if 'rst' not in SKIP:
                            rstd_from_psum(PS[3], PS[3].ap[:, 0:4], 256, tmpt, rst, cols=4)
                        for h in (range(4) if 'kn' not in SKIP else []):
                            pt = PS[4 + (h % 2)]
                            for m in range(2):
                                op("pe", lambda m=m, h=h, pt=pt: nc.tensor.matmul(
                                    pt.ap[:, :], wukv.ap[:, m, h * 128:(h + 1) * 128], ckv.ap[:, m, :],
                                    start=(m == 0), stop=(m == 1)), reads=[wukv, ckv], writes=[pt], inc=(m == 1))
                            op("dve", lambda h=h, pt=pt: nc.vector.tensor_tensor(
                                out=KT[c].ap[:, h, :], in0=pt.ap[:, :], in1=rs.ap[:, :], op=ALU.mult),
                               reads=[pt, rs], writes=[KT[c]])
                        for j in (range(4) if 'v' not in SKIP else []):
                            pt = PS[6 + (j % 2)]
                            for m in range(2):
                                op("pe", lambda m=m, j=j, pt=pt: nc.tensor.matmul(
                                    pt.ap[:, :], ckv.ap[:, m, j * 128:(j + 1) * 128], wukv.ap[:, m, 512:1024],
                                    start=(m == 0), stop=(m == 1)), reads=[wukv, ckv], writes=[pt], inc=(m == 1))
                            op("act", lambda j=j, pt=pt: nc.scalar.activation(
                                Vt[c].ap[:, j, :], pt.ap[:, :], AF.Copy, scale=rst.ap[:, j:j + 1]),
                               reads=[pt, rst], writes=[Vt[c]])
                        for w in (range(2) if 'kr' not in SKIP else []):
                            pt = PS[w]
                            for k in range(8):
                                op("pe", lambda w=w, k=k, pt=pt: nc.tensor.matmul(
                                    pt.ap[:, :], wa.ap[:, k, 256 + w * 128:256 + (w + 1) * 128], xn.ap[:, k, :],
                                    start=(k == 0), stop=(k == 7)), reads=[wa, xn], writes=[pt], inc=(k == 7))
                        if 'kr' not in SKIP:
                            op("dve", lambda: nc.vector.tensor_tensor(out=t1.ap[:, :], in0=PS[0].ap[:, :], in1=cs0.ap[:, :], op=ALU.mult),
                               reads=[PS[0], cs0], writes=[t1])
                            op("dve", lambda: nc.vector.tensor_tensor(out=t2.ap[:, :], in0=PS[1].ap[:, :], in1=cs1.ap[:, :], op=ALU.mult),
                               reads=[PS[1], cs1], writes=[t2])
                            op("pool", lambda: nc.gpsimd.tensor_tensor(out=krT[c].ap[:, :], in0=t1.ap[:, :], in1=t2.ap[:, :], op=ALU.add),
                               reads=[t1, t2], writes=[krT[c]])
                    cx.barrier()

                if stop_after == "p1a":
                    break
                with contextlib.ExitStack() as p1b:
                    wb = T(sb(p1b, "wb", [128, 8, 896], BF16), "wb")
                    wuq = T(sb(p1b, "wuq", [128, 3, 1024], BF16), "wuq")
                    with contextlib.ExitStack() as p1bw:
                        stg = [T(sb(p1bw, "stgB%d" % i, [128, 1024], F32), "stgB%d" % i) for i in range(2)]
                        load_weight(stg, wb_d[l], 8, 896, wb, lambda k: gmix.ap[:, l, k:k + 1])
                        load_weight(stg, wuq_d[l], 3, 1024, wuq, lambda k: gq.ap[:, l, k:k + 1],
                                    neg_cols=[(768, 800), (832, 864), (896, 928), (960, 992)])
                        cx.barrier()
                    Z2 = range(2)
                    cqs = [T(sb(p1b, "cq%d" % z, [128, 3, CH], BF16), "cq%d" % z) for z in Z2]
                    sqs = [T(sb(p1b, "sqq%d" % z, [128, 3, CH], BF16), "sqq%d" % z) for z in Z2]
                    rss = [T(sb(p1b, "rsq%d" % z, [128, CH], F32), "rsq%d" % z) for z in Z2]
                    tmps = rss
                    qns = [T(sb(p1b, "qn%d" % z, [128, 4, CH], BF16), "qn%d" % z) for z in Z2]
                    qrs = [T(sb(p1b, "qr%d" % z, [128, 4, CH], BF16), "qr%d" % z) for z in Z2]
                    gas = [T(sb(p1b, "ga%d" % z, [128, 4, CH], BF16), "ga%d" % z) for z in Z2]
                    t1s = [T(sb(p1b, "t1b%d" % z, [128, CH], F32), "t1b%d" % z) for z in Z2]
                    t2s = [T(sb(p1b, "t2b%d" % z, [128, CH], F32), "t2b%d" % z) for z in Z2]
                    for z in Z2:
                        op("pool", lambda: nc.gpsimd.memset(qrs[z].ap[:], 0.0), writes=[qrs[z]])
                    accd = T(sb(p1b, "accd", [128, CH], F32), "accd")
                    ptb = [T(sb(p1b, "pt%d" % i, [128, CH], BF16), "pt%d" % i) for i in range(4)]
                    rinv = T(sb(p1b, "rinv", [128, CH], F32), "rinv")
                    ym = [T(sb(p1b, "ym%d" % i, [128, 4, CH], BF16), "ym%d" % i) for i in range(2)]

                    def qprep(c):
                        z = c % 2
                        xn = xnb[z]
                        cs0, cs1 = csb[0][z], csb[1][z]
                        cq, sq, tmp, rs, qn, qr, ga, t1, t2 = cqs[z], sqs[z], tmps[z], rss[z], qns[z], qrs[z], gas[z], t1s[z], t2s[z]
                        cx.dma("sp", xn.ap[:], xnT_v[:, :, c * CH:(c + 1) * CH], xn.name, reads=[xnT_c[c]], writes=[xn])
                        cx.dma("sp", cs0.ap[:], cs_d[0, :, c * CH:(c + 1) * CH], cs0.name, reads=[cs_c[c]], writes=[cs0])
                        cx.dma("sp", cs1.ap[:], cs_d[1, :, c * CH:(c + 1) * CH], cs1.name, reads=[cs_c[c]], writes=[cs1])
                        yield
                        for m in range(3):
                            pt = PS[0]
                            for k in range(8):
                                op("pe", lambda: nc.tensor.matmul(
                                    pt.ap[:, :], wb.ap[:, k, m * 128:(m + 1) * 128], xn.ap[:, k, :],
                                    start=(k == 0), stop=(k == 7)), reads=[wb, xn], writes=[pt], inc=(k == 7))
                                if k % 4 == 3:
                                    yield
                            op("dve", lambda: nc.vector.tensor_copy(cq.ap[:, m, :], pt.ap[:, :]), reads=[pt], writes=[cq])
                            yield
                            op("act", lambda: nc.scalar.activation(sq.ap[:, m, :], pt.ap[:, :], AF.Square),
                               reads=[pt], writes=[sq])
                            yield
                        for m in range(3):
                            op("pe", lambda: nc.tensor.matmul(PS[0].ap[:, :], ones_b.ap[:, :], sq.ap[:, m, :],
                                                              start=(m == 0), stop=(m == 2)),
                               reads=[sq], writes=[PS[0]], inc=(m == 2))
                        yield
                        op("act", lambda: nc.scalar.activation(tmp.ap[:, :], PS[0].ap[:, :], AF.Ln, bias=epsb.ap[:, :], scale=1.0 / 384),
                           reads=[PS[0]], writes=[tmp])
                        yield
                        op("act", lambda: nc.scalar.activation(rs.ap[:, :], tmp.ap[:, :], AF.Exp, scale=-0.5), reads=[tmp], writes=[rs])
                        yield
                        for h in range(4):
                            pt = PS[0]
                            for m in range(3):
                                op("pe", lambda: nc.tensor.matmul(
                                    pt.ap[:, :], wuq.ap[:, m, h * 128:(h + 1) * 128], cq.ap[:, m, :],
                                    start=(m == 0), stop=(m == 2)), reads=[wuq, cq], writes=[pt], inc=(m == 2))
                            yield
                            op("dve", lambda: nc.vector.tensor_tensor(
                                out=qn.ap[:, h, :], in0=pt.ap[:, :], in1=rs.ap[:, :], op=ALU.mult),
                               reads=[pt, rs], writes=[qn])
                            yield
                        for pr in range(2):
                            for w in range(2):
                                pt = PS[0]
                                for m in range(3):
                                    col = 512 + w * 256 + pr * 128
                                    op("pe", lambda: nc.tensor.matmul(
                                        pt.ap[:, :], wuq.ap[:, m, col:col + 128], cq.ap[:, m, :],
                                        start=(m == 0), stop=(m == 2)), reads=[wuq, cq], writes=[pt], inc=(m == 2))
                                yield
                                tw, csw = (t1, cs0) if w == 0 else (t2, cs1)
                                op("dve", lambda: nc.vector.tensor_tensor(out=tw.ap[:, :], in0=pt.ap[:, :], in1=csw.ap[:, :], op=ALU.mult),
                                   reads=[pt, csw], writes=[tw])
                                yield
                            op("pool", lambda: nc.gpsimd.tensor_tensor(out=t1.ap[:, :], in0=t1.ap[:, :], in1=t2.ap[:, :], op=ALU.add),
                               reads=[t1, t2], writes=[t1])
                            yield
                            for hh in range(2):
                                hr = slice(hh * 64, (hh + 1) * 64)
                                op("pool", lambda: nc.gpsimd.tensor_tensor(
                                    out=qr.ap[hr, pr * 2 + hh, :], in0=t1.ap[hr, :], in1=rs.ap[hr, :], op=ALU.mult),
                                   reads=[t1, rs], writes=[qr])
                                yield
                        for m in range(4):
                            pt = PS[0]
                            for k in range(8):
                                op("pe", lambda: nc.tensor.matmul(
                                    pt.ap[:, :], wb.ap[:, k, 384 + m * 128:384 + (m + 1) * 128], xn.ap[:, k, :],
                                    start=(k == 0), stop=(k == 7)), reads=[wb, xn], writes=[pt], inc=(k == 7))
                                if k % 4 == 3:
                                    yield
                            op("act", lambda: nc.scalar.activation(ga.ap[:, m, :], pt.ap[:, :], AF.Silu),
                               reads=[pt], writes=[ga])
                            yield

                    accds = [accd, T(sb(p1b, "accd2", [128, CH], F32), "accd2")]
                    rgt = T(sb(p1b, "rgt", [128, CH], F32), "rgt")
                    psum2 = T(sb(p1b, "psum2", [128, CH], BF16), "psum2")

                    def attn_all():
                        tiles = [(c, h, kt) for c in range(NCH) for h in range(4) for kt in range(32)]
                        n = len(tiles)

                        def s_mm(i):
                            c, h, kt = tiles[i]
                            z = c % 2
                            pt = PS[2 + (i % 4)]
                            kc, ko = kt // 4, (kt % 4) * 128
                            op("pe", lambda: nc.tensor.matmul(
                                pt.ap[:, :], KT[kc].ap[:, h, ko:ko + 128], qns[z].ap[:, h, :], start=True, stop=False),
                               reads=[KT[kc], qns[z]], writes=[pt], inc=False)
                            op("pe", lambda: nc.tensor.matmul(
                                pt.ap[:, :], krT[kc].ap[:, ko:ko + 128], qrs[z].ap[:, h, :], start=False, stop=True),
                               reads=[krT[kc], qrs[z]], writes=[pt], inc=True)

                        def epilogue(c, h):
                            z = c % 2
                            g = c * 4 + h
                            po, acc, prs, ymc, ga = PS[6 + (g % 2)], accds[g % 2], PS[1], ym[z], gas[z]
                            op("pe", lambda: nc.tensor.matmul(prs.ap[:, :], ones_f.ap[:, :], acc.ap[:, :], start=True, stop=True),
                               reads=[acc], writes=[prs])
                            op("act", lambda: nc.scalar.activation(rinv.ap[:, :], prs.ap[:, :], AF.Ln), reads=[prs], writes=[rinv])
                            op("act", lambda: nc.scalar.activation(rinv.ap[:, :], rinv.ap[:, :], AF.Exp, scale=-1.0),
                               reads=[rinv], writes=[rinv])
                            op("pool", lambda: nc.gpsimd.tensor_tensor(out=rgt.ap[:, :], in0=rinv.ap[:, :], in1=ga.ap[:, h, :], op=ALU.mult),
                               reads=[rinv, ga], writes=[rgt])
                            op("dve", lambda: nc.vector.tensor_tensor(out=ymc.ap[:, h, :], in0=po.ap[:, :], in1=rgt.ap[:, :], op=ALU.mult),
                               reads=[po, rgt], writes=[ymc])
                            if h == 3:
                                cx.dma("pool", yT_v[:, 0:4, c * CH:(c + 1) * CH], ymc.ap[:], ymc.name, reads=[ymc], writes=[yT_c[c]])

                        pending = []
                        yield ("chunk", 0)
                        s_mm(0)
                        s_mm(1)
                        for i0 in range(0, n, 2):
                            c, h, kt0 = tiles[i0]
                            if h == 0 and kt0 == 0 and c > 0:
                                yield ("chunk", c)
                            g = c * 4 + h
                            po, acc = PS[6 + (g % 2)], accds[g % 2]
                            pbs = [ptb[i0 % 4], ptb[(i0 + 1) % 4]]
                            for u in range(2):
                                pt = PS[2 + ((i0 + u) % 4)]
                                op("act", lambda: nc.scalar.activation(pbs[u].ap[:, :], pt.ap[:, :], AF.Exp, scale=SCALE),
                                   reads=[pt], writes=[pbs[u]])
                            for u in range(2):
                                if i0 + 2 + u < n:
                                    s_mm(i0 + 2 + u)
                            for u in range(2):
                                kt = kt0 + u
                                kc, kj = kt // 4, kt % 4
                                rd = [Vt[kc], pbs[0], pbs[1]] if u == 0 else [Vt[kc], pbs[1]]
                                op("pe", lambda: nc.tensor.matmul(
                                    po.ap[:, :], Vt[kc].ap[:, kj, h * 128:(h + 1) * 128], pbs[u].ap[:, :],
                                    start=(kt == 0), stop=(kt == 31)), reads=rd, writes=[po], inc=True)
                            op("dve", lambda: nc.vector.tensor_tensor(out=psum2.ap[:, :], in0=pbs[0].ap[:, :], in1=pbs[1].ap[:, :], op=ALU.add),
                               reads=[pbs[0], pbs[1]], writes=[psum2])
                            if kt0 == 0:
                                op("dve", lambda: nc.vector.tensor_copy(acc.ap[:, :], psum2.ap[:, :]), reads=[psum2], writes=[acc])
                            else:
                                op("dve", lambda: nc.vector.tensor_tensor(out=acc.ap[:, :], in0=acc.ap[:, :], in1=psum2.ap[:, :], op=ALU.add),
                                   reads=[psum2, acc], writes=[acc])
                            if kt0 + 1 == 31:
                                pending.append((i0 + 4, c, h))
                            while pending and pending[0][0] <= i0:
                                _, c_, h_ = pending.pop(0)
                                epilogue(c_, h_)
                            yield None
                            yield None
                        for _, c_, h_ in pending:
                            epilogue(c_, h_)
                        yield None

                    for _ in qprep(0):
                        pass
                    gens = [attn_all()]
                    while gens:
                        nx = []
                        for g_ in gens:
                            try:
                                tag = next(g_)
                                nx.append(g_)
                                if tag is not None and tag[0] == "chunk" and tag[1] + 1 < NCH:
                                    nx.append(qprep(tag[1] + 1))
                            except StopIteration:
                                pass
                        gens = nx
                    cx.barrier()
            cx.barrier()
            if stop_after == "p1b":
                break
            with contextlib.ExitStack() as p2:
                gqT = [T(sb(p2, "gqT%d" % c, [128, 2, CH], BF16), "gqT%d" % c) for c in range(NCH)]
                gkT = [T(sb(p2, "gkT%d" % c, [128, 2, CH], BF16), "gkT%d" % c) for c in range(NCH)]
                gkt = [T(sb(p2, "gkt%d" % c, [128, 4, 256], BF16), "gkt%d" % c) for c in range(NCH)]
                gvt = [T(sb(p2, "gvt%d" % c, [128, 4, 512], BF16), "gvt%d" % c) for c in range(NCH)]
                ggt = [T(sb(p2, "ggt%d" % c, [128, 4, 512], BF16), "ggt%d" % c) for c in range(NCH)]
                lrT = [T(sb(p2, "lr%d" % c, [48, CH], BF16), "lr%d" % c) for c in range(NCH)]
                with contextlib.ExitStack() as p2a:
                    stg = [T(sb(p2a, "stgG%d" % i, [128, 1824], F32), "stgG%d" % i) for i in range(2)]
                    wg = T(sb(p2a, "wg", [128, 8, 1824], BF16), "wg")
                    xnb = [T(sb(p2a, "xng%d" % i, [128, 8, CH], BF16), "xng%d" % i) for i in range(2)]
                    load_weight(stg, wg_d[l], 8, 1824, wg, lambda k: gmix.ap[:, l, k:k + 1])
                    for c in range(NCH):
                        xn = xnb[c % 2]
                        cx.dma("sp", xn.ap[:], xnT_v[:, :, c * CH:(c + 1) * CH], xn.name, reads=[xnT_c[c]], writes=[xn])
                        for which, dst in ((0, gqT[c]), (1, gkT[c])):
                            for pr in range(2):
                                pt = PS[(which * 2 + pr) % 4]
                                col = which * 256 + pr * 128
                                for k in range(8):
                                    op("pe", lambda k=k, pt=pt, col=col: nc.tensor.matmul(
                                        pt.ap[:, :], wg.ap[:, k, col:col + 128], xn.ap[:, k, :],
                                        start=(k == 0), stop=(k == 7)), reads=[wg, xn], writes=[pt], inc=(k == 7))
                                if which == 0:
                                    op("act", lambda pt=pt, dst=dst, pr=pr: nc.scalar.mul(dst.ap[:, pr, :], pt.ap[:, :], 0.125),
                                       reads=[pt], writes=[dst])
                                else:
                                    op("dve", lambda pt=pt, dst=dst, pr=pr: nc.vector.tensor_copy(dst.ap[:, pr, :], pt.ap[:, :]),
                                       reads=[pt], writes=[dst])
                        for dr in range(2):
                            pt = PS[4 + dr]
                            col = 512 + dr * 16
                            for k in range(8):
                                op("pe", lambda k=k, pt=pt, col=col: nc.tensor.matmul(
                                    pt.ap[32 * dr:32 * dr + 16, :], wg.ap[:, k, col:col + 16], xn.ap[:, k, :],
                                    start=(k == 0), stop=(k == 7)), reads=[wg, xn], writes=[pt], inc=(k == 7))
                            op("dve", lambda pt=pt, dr=dr: nc.vector.tensor_copy(lrT[c].ap[32 * dr:32 * dr + 16, :],
                                                                                 pt.ap[32 * dr:32 * dr + 16, :]),
                               reads=[pt], writes=[lrT[c]])
                        for j in range(4):
                            tk = slice(j * 128, (j + 1) * 128)
                            pk_, pv_, pg_ = PS[6], PS[7], PS[j % 4]
                            for k in range(8):
                                op("pe", lambda k=k: nc.tensor.matmul(
                                    pk_.ap[:, 0:256], xn.ap[:, k, tk], wg.ap[:, k, 544:800],
                                    start=(k == 0), stop=(k == 7)), reads=[wg, xn], writes=[pk_], inc=(k == 7))
                            op("dve", lambda j=j: nc.vector.tensor_copy(gkt[c].ap[:, j, :], pk_.ap[:, 0:256]),
                               reads=[pk_], writes=[gkt[c]])
                            for k in range(8):
                                op("pe", lambda k=k: nc.tensor.matmul(
                                    pv_.ap[:, :], xn.ap[:, k, tk], wg.ap[:, k, 800:1312],
                                    start=(k == 0), stop=(k == 7)), reads=[wg, xn], writes=[pv_], inc=(k == 7))
                            op("act", lambda j=j: nc.scalar.copy(gvt[c].ap[:, j, :], pv_.ap[:, :]),
                               reads=[pv_], writes=[gvt[c]])
                            for k in range(8):
                                op("pe", lambda k=k, pg_=pg_: nc.tensor.matmul(
                                    pg_.ap[:, :], xn.ap[:, k, tk], wg.ap[:, k, 1312:1824],
                                    start=(k == 0), stop=(k == 7)), reads=[wg, xn], writes=[pg_], inc=(k == 7))
                            op("act", lambda j=j, pg_=pg_: nc.scalar.activation(ggt[c].ap[:, j, :], pg_.ap[:, :], AF.Silu),
                               reads=[pg_], writes=[ggt[c]])
                    cx.barrier()
                if stop_after == "p2a":
                    break
                with contextlib.ExitStack() as p2b:
                    opart = [T(sb(p2b, "opart%d" % t, [128, 512], BF16), "opart%d" % t) for t in range(32)]
                    D2 = range(2)
                    S2 = range(2)
                    ez = [T(sb(p2b, "ez%d" % d, [128, 256], F32), "ez%d" % d) for d in D2]
                    spt = ez
                    Eq = [T(sb(p2b, "Eq%d" % d, [128, 2, 128], F32), "Eq%d" % d) for d in D2]
                    Ek = [T(sb(p2b, "Ek%d" % d, [128, 2, 128], F32), "Ek%d" % d) for d in D2]
                    Eqi = [T(sb(p2b, "Eqi%d" % d, [128, 2, 128], F32), "Eqi%d" % d) for d in D2]
                    Es = [T(sb(p2b, "Es%d" % d, [128, 256], F32), "Es%d" % d) for d in D2]
                    dec = [[T(sb(p2b, "dec%d_%d" % (d, z), [128, 2, 2], F32), "dec%d_%d" % (d, z)) for z in S2] for d in D2]
                    qin = [[T(sb(p2b, "qin%d_%d" % (d, z), [128, 2, 128], BF16), "qin%d_%d" % (d, z)) for z in S2] for d in D2]
                    kin = [[T(sb(p2b, "kin%d_%d" % (d, z), [128, 4, 128], BF16), "kin%d_%d" % (d, z)) for z in S2] for d in D2]
                    qint = [[T(sb(p2b, "qint%d_%d" % (d, z), [128, 4, 128], BF16), "qint%d_%d" % (d, z)) for z in S2] for d in D2]
                    kst = [[[T(sb(p2b, "kst%d_%d_%d" % (d, z, cc), [128, 256], BF16), "kst%d_%d_%d" % (d, z, cc))
                             for cc in range(2)] for z in S2] for d in D2]
                    Am = [[T(sb(p2b, "Am%d_%d" % (d, cc), [128, 256], BF16), "Am%d_%d" % (d, cc)) for cc in range(2)] for d in D2]
                    Sf = [T(sb(p2b, "Sf%d" % d, [128, 2, 128], F32), "Sf%d" % d) for d in D2]
                    Sb = [T(sb(p2b, "Sb%d" % d, [128, 2, 128], BF16), "Sb%d" % d) for d in D2]
                    hm = T(sb(p2b, "hm", [128, 2], F32), "hm")
                    ot = T(sb(p2b, "ot", [128, 512], F32), "ot")
                    osq = T(sb(p2b, "osq", [128, 512], F32), "osq")
                    ss = T(sb(p2b, "ssg", [128, 4], F32), "ssg")
                    ssr = T(sb(p2b, "ssr", [128, 4], F32), "ssr")
                    rg = T(sb(p2b, "rg", [128, 4], F32), "rg")
                    yt = T(sb(p2b, "yt", [128, 512], BF16), "yt")
                    ygT = T(sb(p2b, "ygT", [128, 4, 128], BF16), "ygT")
                    op("pool", lambda: nc.gpsimd.memset(hm.ap[:], 0.0), writes=[hm])
                    op("pool", lambda: nc.gpsimd.memset(hm.ap[0:64, 0:1], 1.0), writes=[hm])
                    op("pool", lambda: nc.gpsimd.memset(hm.ap[64:128, 1:2], 1.0), writes=[hm])
                    for d in D2:
                        op("pool", lambda: nc.gpsimd.memset(Sf[d].ap[:], 0.0), writes=[Sf[d]])
                        op("pool", lambda: nc.gpsimd.memset(Sb[d].ap[:], 0.0), writes=[Sb[d]])
                        for cc in range(2):
                            op("pool", lambda: nc.gpsimd.memset(Am[d][cc].ap[:], 0.0), writes=[Am[d][cc]])
                            for z in S2:
                                op("pool", lambda: nc.gpsimd.memset(kst[d][z][cc].ap[:], 0.0), writes=[kst[d][z][cc]])
                    PREP = [PS[0], PS[1]]
                    PA = [PS[2], PS[5]]
                    PO = [PS[3], PS[4]]
                    PKS = [PS[6], PS[7]]

                    def prep(d, t):
                        z = t % 2
                        c, j = t // 4, t % 4
                        tk = slice(j * 128, (j + 1) * 128)
                        lr_rows = slice(32 * d, 32 * d + 16)
                        lastcol = 63 if d == 0 else 0
                        pp = PREP[d]
                        bgx = bgh[d]
                        op("pe", lambda: nc.tensor.matmul(pp.ap[:, 0:256], lrT[c].ap[lr_rows, tk], wg_b.ap[lr_rows, l, :],
                                                          start=True, stop=False),
                           reads=[lrT[c], wg_b], writes=[pp], inc=False)
                        op("pe", lambda: nc.tensor.matmul(pp.ap[:, 0:256], ones_b.ap[0:1, :], bgx.ap[0:1, 0, l * 256:(l + 1) * 256],
                                                          start=False, stop=False),
                           reads=[bgx], writes=[pp], inc=False)
                        op("pe", lambda: nc.tensor.matmul(pp.ap[:, 0:256], ones_b.ap[0:1, :], bgx.ap[0:1, 1, l * 256:(l + 1) * 256],
                                                          start=False, stop=True),
                           reads=[bgx], writes=[pp])
                        yield
                        op("act", lambda: nc.scalar.activation(ez[d].ap[:, :], pp.ap[:, 0:256], AF.Exp, scale=-1.0),
                           reads=[pp], writes=[ez[d]])
                        yield
                        op("act", lambda: nc.scalar.activation(spt[d].ap[:, :], ez[d].ap[:, :], AF.Ln, bias=1.0),
                           reads=[ez[d]], writes=[spt[d]])
                        yield
                        for w in range(2):
                            for pr in range(2):
                                op("pe", lambda: nc.tensor.matmul(
                                    pp.ap[:, w * 256 + pr * 128:w * 256 + (pr + 1) * 128], spt[d].ap[:, pr * 128:(pr + 1) * 128],
                                    tri.ap[:, d * 3 + w, :], start=True, stop=True),
                                   reads=[spt[d], tri], writes=[pp], inc=(w == 1 and pr == 1))
                        yield
                        op("pe", lambda: nc.tensor.matmul(PA[d].ap[:, 256:512], tri.ap[:, d * 3 + 2, :], spt[d].ap[:, :],
                                                          start=True, stop=True),
                           reads=[spt[d], tri], writes=[PA[d]])
                        yield
                        op("act", lambda: nc.scalar.activation(
                            Eq[d].ap[:], pp.ap[:, 256:512].rearrange("p (a i) -> p a i", a=2), AF.Exp),
                           reads=[pp], writes=[Eq[d]])
                        yield
                        op("act", lambda: nc.scalar.activation(
                            Ek[d].ap[:], pp.ap[:, 256:512].rearrange("p (a i) -> p a i", a=2), AF.Exp, scale=-1.0),
                           reads=[pp], writes=[Ek[d]])
                        yield
                        op("act", lambda: nc.scalar.activation(
                            Eqi[d].ap[:], pp.ap[:, 0:256].rearrange("p (a i) -> p a i", a=2), AF.Exp),
                           reads=[pp], writes=[Eqi[d]])
                        yield
                        op("act", lambda: nc.scalar.activation(
                            dec[d][z].ap[:].rearrange("p a c -> p (a c)").rearrange("p (q o) -> p q o", o=1),
                            pp.ap[:, 0:256].rearrange("p (q i) -> p q i", i=64)[:, :, lastcol:lastcol + 1], AF.Exp),
                           reads=[pp], writes=[dec[d][z]])
                        yield
                        op("act", lambda: nc.scalar.activation(Es[d].ap[:, :], PA[d].ap[:, 256:512], AF.Exp),
                           reads=[PA[d]], writes=[Es[d]])
                        yield
                        op("dve", lambda: nc.vector.tensor_tensor(out=qin[d][z].ap[:], in0=gqT[c].ap[:, :, tk], in1=Eq[d].ap[:], op=ALU.mult),
                           reads=[gqT[c], Eq[d]], writes=[qin[d][z]])
                        yield
                        for h in range(4):
                            pr, hh = h // 2, h % 2
                            op("dve", lambda: nc.vector.scalar_tensor_tensor(
                                kin[d][z].ap[:, h, :], gkT[c].ap[:, pr, tk], hm.ap[:, hh:hh + 1], Ek[d].ap[:, pr, :],
                                ALU.mult, ALU.mult), reads=[gkT[c], Ek[d], hm], writes=[kin[d][z]])
                            yield
                            op("dve", lambda: nc.vector.scalar_tensor_tensor(
                                qint[d][z].ap[:, h, :], gqT[c].ap[:, pr, tk], hm.ap[:, hh:hh + 1], Eqi[d].ap[:, pr, :],
                                ALU.mult, ALU.mult), reads=[gqT[c], Eqi[d], hm], writes=[qint[d][z]])
                            yield
                        for cc in range(2):
                            rws = slice(cc * 64, (cc + 1) * 64)
                            op("pool", lambda: nc.gpsimd.tensor_tensor(
                                out=kst[d][z][cc].ap[rws, :], in0=gkt[c].ap[rws, j, :], in1=Es[d].ap[rws, :], op=ALU.mult),
                               reads=[gkt[c], Es[d]], writes=[kst[d][z][cc]])
                            yield

                    def scan(d, t, s_):
                        z = t % 2
                        c, j = t // 4, t % 4
                        po, pa = PO[d], PA[d]
                        PK = PKS[d]
                        PX = PKS[d]
                        for cc in ((0, 1) if d == 0 else (1, 0)):
                            rows = slice(cc * 64, (cc + 1) * 64)
                            cols = slice(cc * 64, (cc + 1) * 64)
                            for h in range(4):
                                pr = h // 2
                                op("pe", lambda: nc.tensor.matmul(
                                    pa.ap[rows, h * 64:(h + 1) * 64], kin[d][z].ap[:, h, cols], qin[d][z].ap[:, pr, cols],
                                    start=True, stop=True), reads=[kin[d][z], qin[d][z]], writes=[pa], inc=(h == 3))
                            yield
                            op("dve", lambda: nc.vector.tensor_tensor(out=Am[d][cc].ap[rows, :], in0=pa.ap[rows, 0:256],
                                                                      in1=mask.ap[rows, d, :], op=ALU.mult),
                               reads=[pa, mask], writes=[Am[d][cc]])
                            yield
                            for h in range(4):
                                pr = h // 2
                                op("pe", lambda: nc.tensor.matmul(
                                    po.ap[rows, h * 128:(h + 1) * 128], Am[d][cc].ap[:, h * 64:(h + 1) * 64],
                                    gvt[c].ap[:, j, h * 128:(h + 1) * 128], start=True, stop=False),
                                   reads=[Am[d][cc], gvt[c]], writes=[po], inc=False)
                                op("pe", lambda: nc.tensor.matmul(
                                    po.ap[rows, h * 128:(h + 1) * 128], qint[d][z].ap[:, h, cols],
                                    Sb[d].ap[:, pr, :], start=False, stop=True),
                                   reads=[qint[d][z], Sb[d]], writes=[po], inc=(h == 3))
                            yield
                            for h in range(4):
                                pr, hh = h // 2, h % 2
                                hr = slice(hh * 64, (hh + 1) * 64)
                                op("pe", lambda: nc.tensor.matmul(
                                    PK.ap[hr, pr * 128:(pr + 1) * 128], kst[d][z][cc].ap[:, h * 64:(h + 1) * 64],
                                    gvt[c].ap[:, j, h * 128:(h + 1) * 128], start=True, stop=True),
                                   reads=[kst[d][z][cc], gvt[c]], writes=[PK], inc=(h == 3))
                            yield
                            for pr in range(2):
                                op("dve", lambda: nc.vector.scalar_tensor_tensor(
                                    Sf[d].ap[:, pr, :], Sf[d].ap[:, pr, :], dec[d][z].ap[:, pr, cc:cc + 1],
                                    PK.ap[:, pr * 128:(pr + 1) * 128], ALU.mult, ALU.add),
                                   reads=[Sf[d], dec[d][z], PK], writes=[Sf[d]])
                            yield
                            op("act", lambda: nc.scalar.copy(Sb[d].ap[:], Sf[d].ap[:]), reads=[Sf[d]], writes=[Sb[d]])
                            yield
                        if s_ < 16:
                            op("act", lambda: nc.scalar.copy(opart[t].ap[:, :], po.ap[:, :]), reads=[po], writes=[opart[t]])
                            yield
                            return
                        op("dve", lambda: nc.vector.tensor_tensor(out=ot.ap[:, :], in0=po.ap[:, :], in1=opart[t].ap[:, :], op=ALU.add),
                           reads=[po, opart[t]], writes=[ot])
                        op("pool", lambda: nc.gpsimd.tensor_tensor(out=osq.ap[:, :], in0=ot.ap[:, :], in1=ot.ap[:, :], op=ALU.mult),
                           reads=[ot], writes=[osq])
                        op("dve", lambda: nc.vector.reduce_sum(ss.ap[:, :], osq.ap[:, :].rearrange("p (h v) -> p h v", h=4), axis=AX.X),
                           reads=[osq], writes=[ss])
                        op("act", lambda: nc.scalar.activation(ssr.ap[:, :], ss.ap[:, :], AF.Ln, bias=epsb.ap[:, :], scale=1.0 / 128),
                           reads=[ss], writes=[ssr])
                        op("act", lambda: nc.scalar.activation(rg.ap[:, :], ssr.ap[:, :], AF.Exp, scale=-0.5), reads=[ssr], writes=[rg])
                        for h in range(4):
                            hs = slice(h * 128, (h + 1) * 128)
                            op("dve", lambda: nc.vector.scalar_tensor_tensor(
                                yt.ap[:, hs], ot.ap[:, hs], rg.ap[:, h:h + 1], ggt[c].ap[:, j, hs], ALU.mult, ALU.mult),
                               reads=[ot, rg, ggt[c]], writes=[yt])
                            pview = PX.ap[:, :].bitcast(BF16)
                        for h in range(4):
                            op("pe", lambda: nc.tensor.transpose(pview[:, h * 128:(h + 1) * 128],
                                                                 yt.ap[:, h * 128:(h + 1) * 128], ident_b.ap[:, :]),
                               reads=[yt], writes=[PX], inc=(h == 3))
                        op("act", lambda: nc.scalar.copy(ygT.ap[:].rearrange("p h t -> p (h t)"), pview[:, 0:512]),
                           reads=[PX], writes=[ygT])
                        cx.dma("pool", yT_v[:, 4:8, t * 128:(t + 1) * 128], ygT.ap[:], ygT.name, reads=[ygT], writes=[yT_c[c]])
                        yield

                    def run_zipped(gens, weights=None):
                        gens = list(gens)
                        weights = list(weights) if weights else [1] * len(gens)
                        while gens:
                            nxt, nw = [], []
                            for g, w in zip(gens, weights):
                                alive = True
                                for _ in range(w):
                                    try:
                                        next(g)
                                    except StopIteration:
                                        alive = False
                                        break
                                if alive:
                                    nxt.append(g)
                                    nw.append(w)
                            gens, weights = nxt, nw

                    run_zipped([prep(0, 0), prep(1, 31)])
                    for s_ in range(32):
                        gens = [scan(0, s_, s_), scan(1, 31 - s_, s_)]
                        wts = [1, 1]
                        if s_ + 1 < 32:
                            gens += [prep(0, s_ + 1), prep(1, 30 - s_)]
                            wts += [2, 2]
                        run_zipped(gens, wts)
                    cx.barrier()
            cx.barrier()
            if stop_after == "p2b":
                break
            with contextlib.ExitStack() as p3:
                stg = [T(sb(p3, "stgO%d" % i, [128, 1024], F32), "stgO%d" % i) for i in range(2)]
                wout = T(sb(p3, "wout", [128, 8, 1024], BF16), "wout")
                wpg = T(sb(p3, "wpg", [128, 8, 1024], BF16), "wpg")
                wpp = T(sb(p3, "wpp", [128, 2, 1024], BF16), "wpp")
                Z2 = range(2)
                yb = [T(sb(p3, "yb%d" % i, [128, 8, CH], BF16), "yb%d" % i) for i in Z2]
                hb = [T(sb(p3, "hb3_%d" % i, [128, 8, CH], F32), "hb3_%d" % i) for i in range(3)]
                pin = [[T(sb(p3, "pin%d_%d" % (z, i), [128, 256], F32), "pin%d_%d" % (z, i)) for i in range(4)] for z in Z2]
                pTb = [T(sb(p3, "pT%d" % z, [128, 2, CH], BF16), "pT%d" % z) for z in Z2]
                sqbs = [T(sb(p3, "sqb3_%d" % z, [128, 8, CH], BF16), "sqb3_%d" % z) for z in Z2]
                tmpbs = [T(sb(p3, "tmpb3_%d" % z, [128, CH], F32), "tmpb3_%d" % z) for z in Z2]
                rsbs = [T(sb(p3, "rsb3_%d" % z, [128, CH], F32), "rsb3_%d" % z) for z in Z2]
                hns = [T(sb(p3, "hn%d" % z, [128, 8, CH], BF16), "hn%d" % z) for z in Z2]
                sigs = [[T(sb(p3, "sig%d_%d" % (z, i), [128, CH], F32), "sig%d_%d" % (z, i)) for i in range(2)] for z in Z2]
                if not last:
                    xnbs = [T(sb(p3, "xno%d" % i, [128, 8, CH], BF16), "xno%d" % i) for i in Z2]
                else:
                    otiles = [[T(sb(p3, "otile%d_%d" % (z, i), [128, D], F32), "otile%d_%d" % (z, i)) for i in range(2)] for z in Z2]
                load_weight(stg, wout_d[l], 8, 1024, wout, lambda k: (None if k < 4 else goutn.ap[:, l:l + 1]))
                load_weight(stg, wpg_d[l], 8, 1024, wpg, lambda k: gple.ap[:, l, k:k + 1])
                load_weight(stg, wpp_d[l], 2, 1024, wpp, lambda k: None)

                def rms_gen(h_, sq_t, tmp_t, rs_t, ps_t):
                    op("act", lambda: nc.scalar.activation(sq_t.ap[:, 0:4, :], h_.ap[:, 0:4, :], AF.Square),
                       reads=[h_], writes=[sq_t])
                    yield
                    op("act", lambda: nc.scalar.activation(sq_t.ap[:, 4:8, :], h_.ap[:, 4:8, :], AF.Square),
                       reads=[h_], writes=[sq_t])
                    for _ in range(5):
                        yield
                    for k in range(8):
                        op("pe", lambda: nc.tensor.matmul(ps_t.ap[:, :], ones_b.ap[:, :], sq_t.ap[:, k, :],
                                                          start=(k == 0), stop=(k == 7)),
                           reads=[sq_t], writes=[ps_t], inc=(k == 7))
                    yield
                    op("act", lambda: nc.scalar.activation(tmp_t.ap[:, :], ps_t.ap[:, :], AF.Ln,
                                                            bias=epsb.ap[:, :], scale=1.0 / D),
                       reads=[ps_t], writes=[tmp_t])
                    yield
                    op("act", lambda: nc.scalar.activation(rs_t.ap[:, :], tmp_t.ap[:, :], AF.Exp, scale=-0.5),
                       reads=[tmp_t], writes=[rs_t])
                    yield

                def chunk_gen(c):
                    z = c % 2
                    B0, B1, B2, B3 = PS[4 * z], PS[4 * z + 1], PS[4 * z + 2], PS[4 * z + 3]
                    y_, h_, pT_, hn = yb[z], hb[c % 3], pTb[z], hns[z]
                    sqb, tmpb, rsb = sqbs[z], tmpbs[z], rsbs[z]

                    def load_h(cc):
                        cx.dma("sp", hb[cc % 3].ap[:], hT_v[:, :, cc * CH:(cc + 1) * CH], hb[cc % 3].name,
                               reads=[hT_c[cc]], writes=[hb[cc % 3]])

                    def loads(cc):
                        cx.dma("sp", yb[cc % 2].ap[:], yT_v[:, :, cc * CH:(cc + 1) * CH], yb[cc % 2].name,
                               reads=[yT_c[cc]], writes=[yb[cc % 2]])
                        for jt in range(4):
                            tl = cc * 4 + jt
                            pt_ = pin[cc % 2][jt]
                            cx.dma("sp", pt_.ap[:], p_d[l, tl * 128:(tl + 1) * 128, :], pt_.name, writes=[pt_])

                    if c < 2:
                        loads(c)
                    if c == 0:
                        load_h(0)
                        load_h(1)
                        load_h(2)
                    yield
                    for m in range(8):
                        pt = B0 if m % 2 == 0 else B1
                        for k in range(8):
                            op("pe", lambda: nc.tensor.matmul(
                                pt.ap[:, :], wout.ap[:, k, m * 128:(m + 1) * 128], y_.ap[:, k, :],
                                start=(k == 0), stop=(k == 7)), reads=[wout, y_], writes=[pt], inc=(k == 7))
                        yield
                        op("dve", lambda: nc.vector.tensor_tensor(out=h_.ap[:, m, :], in0=pt.ap[:, :],
                                                                  in1=h_.ap[:, m, :], op=ALU.add),
                           reads=[pt, h_], writes=[h_])
                        yield
                    for fc in range(2):
                        for jt in range(4):
                            op("pe", lambda: nc.tensor.transpose(
                                B2.ap[:, jt * 128:(jt + 1) * 128], pin[z][jt].ap[:, fc * 128:(fc + 1) * 128], ident_f.ap[:, :]),
                               reads=[pin[z][jt]], writes=[B2], inc=(jt == 3))
                        yield
                        op("act", lambda: nc.scalar.copy(pT_.ap[:, fc, :], B2.ap[:, :]), reads=[B2], writes=[pT_])
                        yield
                    if c + 2 < NCH:
                        loads(c + 2)
                    yield "half"
                    yield from rms_gen(h_, sqb, tmpb, rsb, B3)
                    for k in range(8):
                        op("dve", lambda: nc.vector.tensor_tensor(out=hn.ap[:, k, :], in0=h_.ap[:, k, :],
                                                                  in1=rsb.ap[:, :], op=ALU.mult),
                           reads=[h_, rsb], writes=[hn])
                        yield
                    for m in range(8):
                        pg_ = B0 if m % 2 == 0 else B1
                        pp_ = B2
                        sg = sigs[z][m % 2]
                        for k in range(8):
                            op("pe", lambda: nc.tensor.matmul(
                                pg_.ap[:, :], wpg.ap[:, k, m * 128:(m + 1) * 128], hn.ap[:, k, :],
                                start=(k == 0), stop=(k == 7)), reads=[wpg, hn], writes=[pg_], inc=(k == 7))
                        yield
                        op("act", lambda: nc.scalar.activation(sg.ap[:, :], pg_.ap[:, :], AF.Sigmoid),
                           reads=[pg_], writes=[sg])
                        yield
                        for fc in range(2):
                            op("pe", lambda: nc.tensor.matmul(
                                pp_.ap[:, :], wpp.ap[:, fc, m * 128:(m + 1) * 128], pT_.ap[:, fc, :],
                                start=(fc == 0), stop=(fc == 1)), reads=[wpp, pT_], writes=[pp_], inc=(fc == 1))
                        yield
                        op("dve", lambda: nc.vector.tensor_tensor(out=sg.ap[:, :], in0=pp_.ap[:, :],
                                                                  in1=sg.ap[:, :], op=ALU.mult),
                           reads=[pp_, sg], writes=[sg])
                        yield
                        op("dve", lambda: nc.vector.tensor_tensor(out=h_.ap[:, m, :], in0=h_.ap[:, m, :],
                                                                  in1=sg.ap[:, :], op=ALU.add),
                           reads=[h_, sg], writes=[h_])
                        yield
                    if not last:
                        xn_t = xnbs[z]
                        cx.dma("pool", hT_v[:, :, c * CH:(c + 1) * CH], h_.ap[:], h_.name, reads=[h_], writes=[hT_c[c]])
                        yield from rms_gen(h_, sqb, tmpb, rsb, B3)
                        for k in range(8):
                            op("dve", lambda: nc.vector.tensor_tensor(out=xn_t.ap[:, k, :], in0=h_.ap[:, k, :],
                                                                      in1=rsb.ap[:, :], op=ALU.mult),
                               reads=[h_, rsb], writes=[xn_t])
                            yield
                        cx.dma("pool", xnT_v[:, :, c * CH:(c + 1) * CH], xn_t.ap[:], xn_t.name, reads=[xn_t], writes=[xnT_c[c]])
                        yield
                    else:
                        yield from rms_gen(h_, sqb, tmpb, rsb, B3)
                        for k in range(8):
                            op("dve", lambda: nc.vector.scalar_tensor_tensor(
                                h_.ap[:, k, :], h_.ap[:, k, :], gfin.ap[:, k:k + 1], rsb.ap[:, :], ALU.mult, ALU.mult),
                               reads=[h_, gfin, rsb], writes=[h_])
                            yield
                        banks = [B0, B1, B2, B3]
                        for jt in range(4):
                            tl = c * 4 + jt
                            ot_ = otiles[z][jt % 2]
                            for half in range(2):
                                pt = banks[(jt * 2 + half) % 4]
                                for q4 in range(4):
                                    k = half * 4 + q4
                                    op("pe", lambda: nc.tensor.transpose(
                                        pt.ap[:, q4 * 128:(q4 + 1) * 128], h_.ap[:, k, jt * 128:(jt + 1) * 128], ident_f.ap[:, :]),
                                       reads=[h_], writes=[pt], inc=(q4 == 3))
                                yield
                                if half == 0:
                                    op("act", lambda: nc.scalar.copy(ot_.ap[:, 0:512], pt.ap[:, :]), reads=[pt], writes=[ot_])
                                else:
                                    op("dve", lambda: nc.vector.tensor_copy(ot_.ap[:, 512:1024], pt.ap[:, :]), reads=[pt], writes=[ot_])
                                yield
                            cx.dma("pool", out_d[tl * 128:(tl + 1) * 128, :], ot_.ap[:], ot_.name, reads=[ot_])
                            yield

                    if c + 3 < NCH:
                        load_h(c + 3)
                        yield

                active = {0: chunk_gen(0)}
                nxt_c = 1
                want_start = False
                while active:
                    for c_ in sorted(active):
                        try:
                            tag = next(active[c_])
                            if tag == "half":
                                want_start = True
                        except StopIteration:
                            del active[c_]
                    if want_start and nxt_c < NCH and (nxt_c - 2) not in active:
                        active[nxt_c] = chunk_gen(nxt_c)
                        nxt_c += 1
                        want_start = False
                cx.barrier()

        cx.barrier()
    return nc, cx.n_inst


def _kchunk(w):
    K, C = w.shape
    return np.ascontiguousarray(w.reshape(K // 128, 128, C).transpose(1, 0, 2))


def _rot_cols(w):
    return np.concatenate([w[:, 32:64], w[:, 0:32]], axis=1)


def prep_shared(inp):
    f = lambda a: np.asarray(a, dtype=np.float32)
    w_in = f(inp["w_in"])
    w_uq = f(inp["w_uq"])
    w_ukv = f(inp["w_ukv"])
    sh = {}
    o = np.cumsum([0, 384, 256, 64, 512, 256, 256, 512, 16, 16, 512])
    wa, wb, wg, wuq, wukv = [], [], [], [], []
    for l in range(L):
        W = w_in[l]
        cq, ckv, kr, gate_a, gq_, gk_, gv_, lrf, lrb, gate_g = [W[:, o[i]:o[i + 1]] for i in range(10)]
        krr = _rot_cols(kr)
        wa.append(_kchunk(np.concatenate([ckv, kr, kr, krr, krr], axis=1)))
        wb.append(_kchunk(np.concatenate([cq, gate_a], axis=1)))
        wg.append(_kchunk(np.concatenate([gq_, gk_, lrf, lrb, gk_, gv_, gate_g], axis=1)))
        U = w_uq[l].reshape(384, 4, 192)
        nope = U[:, :, :128].reshape(384, 512)
        rope = U[:, :, 128:]
        rope_cat = rope.reshape(384, 256)
        rot_cat = np.concatenate([_rot_cols(rope[:, h, :]) for h in range(4)], axis=1)
        wuq.append(_kchunk(np.concatenate([nope, rope_cat, rot_cat], axis=1)))
        KV = w_ukv[l].reshape(256, 4, 256)
        kn = KV[:, :, :128].reshape(256, 512)
        vv = KV[:, :, 128:].reshape(256, 512)
        wukv.append(_kchunk(np.concatenate([kn, vv], axis=1)))
    sh["wa"] = np.stack(wa)
    sh["wb"] = np.stack(wb)
    sh["wg"] = np.stack(wg)
    sh["wuq"] = np.stack(wuq)
    sh["wukv"] = np.stack(wukv)
    sh["wout"] = np.stack([_kchunk(f(inp["w_out"])[l]) for l in range(L)])
    sh["wpg"] = np.stack([_kchunk(f(inp["w_ple_gate"])[l]) for l in range(L)])
    sh["wpp"] = np.stack([_kchunk(f(inp["w_ple_proj"])[l]) for l in range(L)])

    def pk(g, nk):
        return np.ascontiguousarray(g.reshape(L, nk, 128).transpose(2, 0, 1))
    sh["gmix"] = pk(f(inp["ln_mix"]), 8)
    sh["gq"] = pk(f(inp["mla_q_norm"]), 3)
    sh["gkv"] = pk(f(inp["mla_kv_norm"]), 2)
    sh["goutn"] = np.ascontiguousarray(f(inp["gla_out_norm"]).T)
    sh["gple"] = pk(f(inp["ple_norm"]), 8)
    sh["gfin"] = np.ascontiguousarray(f(inp["final_norm"]).reshape(8, 128).T)
    sh["wgf"] = np.ascontiguousarray(f(inp["gla_w_gate_fwd"]).transpose(1, 0, 2))
    sh["wgb"] = np.ascontiguousarray(f(inp["gla_w_gate_bwd"]).transpose(1, 0, 2))
    sh["bgf"] = np.ascontiguousarray(f(inp["gla_b_gate_fwd"]).reshape(1, L * 256))
    sh["bgb"] = np.ascontiguousarray(f(inp["gla_b_gate_bwd"]).reshape(1, L * 256))
    sh["ident"] = np.eye(128, dtype=np.float32)
    j = np.arange(128)[:, None]
    i = np.arange(128)[None, :]
    same = (j // 64) == (i // 64)
    v = np.float32(-1.0 / 16.0)
    tri = np.zeros((128, 6, 128), np.float32)
    Tf = np.where(same & (j <= i), v, 0).astype(np.float32)
    Tb = np.where(same & (j >= i), v, 0).astype(np.float32)
    reff = (np.arange(128) // 64) * 64 + 31
    refb = (np.arange(128) // 64) * 64 + 32
    tri[:, 0, :] = Tf
    tri[:, 1, :] = Tf - Tf[:, reff]
    tri[:, 2, :] = np.where(same & (j > i), v, 0)
    tri[:, 3, :] = Tb
    tri[:, 4, :] = Tb - Tb[:, refb]
    tri[:, 5, :] = np.where(same & (j < i), v, 0)
    sh["tri"] = tri
    jl = (np.arange(128) % 64)[:, None]
    il = np.arange(64)[None, :]
    mk = np.zeros((128, 2, 4, 64), np.float32)
    mk[:, 0, :, :] = (jl <= il).astype(np.float32)[:, None, :]
    mk[:, 1, :, :] = (jl >= il).astype(np.float32)[:, None, :]
    sh["mask"] = mk.reshape(128, 2, 256)
    half = 32
    inv = (10000.0 ** (-np.arange(half, dtype=np.float32) / half)).astype(np.float32)
    invf = (inv.astype(np.float64) / (2 * np.pi)).astype(np.float32)
    sh["invf"] = np.ascontiguousarray(np.tile(invf, 4).reshape(128, 1))
    return sh


def make_in_maps(inp):
    sh = prep_shared(inp)
    x = np.asarray(inp["x"], dtype=np.float32)
    p = np.asarray(inp["p"], dtype=np.float32)
    pos = np.asarray(inp["positions"], dtype=np.int32)
    maps = []
    for b in range(8):
        m = dict(sh)
        m["x"] = np.ascontiguousarray(x[b])
        m["p"] = np.ascontiguousarray(p[:, b])
        m["pos"] = np.ascontiguousarray(pos[b:b + 1])
        maps.append(m)
    return maps


def kernel(**inputs):
    nc, _ = build()
    maps = make_in_maps(inputs)
    res = run_bass_kernel_spmd(nc, maps, core_ids=list(range(8)))
    return np.stack([np.asarray(r["out"], dtype=np.float32) for r in res.results], axis=0)
```

```python
import contextlib
import numpy as np
import concourse.bass as bass
import concourse.mybir as mybir
from concourse.bass_utils import run_bass_kernel_spmd

F32 = mybir.dt.float32
BF16 = mybir.dt.bfloat16
I32 = mybir.dt.int32
AF = mybir.ActivationFunctionType
ALU = mybir.AluOpType
AX = mybir.AxisListType

S = 4096
D = 1024
NCH = 8
CH = 512
L = 2
EPS = 1e-6
SCALE = float((128 + 64) ** -0.5)
SAME_ENGINE_SYNC = True
import os
SKIP = set(os.environ.get('DBG_SKIP', '').split(','))


class T:
    __slots__ = ("ap", "name", "w", "r", "psum")

    def __init__(self, ap, name, psum=False):
        self.ap = ap
        self.name = name
        self.w = None
        self.r = []
        self.psum = psum

    def __getitem__(self, i):
        return self.ap[i]


class Ctx:
    def __init__(self, nc, stack):
        self.nc = nc
        self.stack = stack
        self.eng = {}
        for name, h in [("pe", nc.tensor), ("act", nc.scalar), ("dve", nc.vector),
                        ("pool", nc.gpsimd), ("sp", nc.sync)]:
            sem = stack.enter_context(nc.semaphore("s_" + name))
            self.eng[name] = dict(h=h, sem=sem, cnt=0, waited={}, name=name)
        self.dsem = {}
        self.n_inst = 0

    def _sem(self, key):
        if key in self.eng:
            return self.eng[key]["sem"]
        return self.dsem[key][0]

    def _need(self, e, stamp, acc):
        key, val = stamp
        if key == e["name"]:
            if key == "pe" or not SAME_ENGINE_SYNC:
                return
        if e["waited"].get(key, 0) >= val:
            return
        if acc.get(key, 0) < val:
            acc[key] = val

    def _deps(self, e, reads, writes):
        acc = {}
        for t in reads:
            if t.w is not None:
                self._need(e, t.w, acc)
            if t.psum:
                for st in t.r:
                    self._need(e, st, acc)
        for t in writes:
            if t.w is not None:
                self._need(e, t.w, acc)
            for st in t.r:
                self._need(e, st, acc)
        for key, val in acc.items():
            e["h"].wait_ge(self._sem(key), val)
            e["waited"][key] = val

    def _mark(self, stamp, reads, writes):
        for t in reads:
            if t.psum:
                t.w = stamp
                t.r = []
                continue
            t.r.append(stamp)
            if len(t.r) > 64:
                best = {}
                for k, v in t.r:
                    if best.get(k, 0) < v:
                        best[k] = v
                t.r = list(best.items())
        for t in writes:
            t.w = stamp
            t.r = []

    def op(self, engname, fn, reads=(), writes=(), inc=True):
        e = self.eng[engname]
        self._deps(e, reads, writes)
        ins = fn()
        self.n_inst += 1
        if inc:
            e["cnt"] += 1
            ins.then_inc(e["sem"], 1)
            stamp = (engname, e["cnt"])
        else:
            stamp = (engname, e["cnt"] + 1)
        self._mark(stamp, reads, writes)
        return ins

    def dma(self, q, out, in_, key, reads=(), writes=()):
        e = self.eng[q]
        if q == "pool":
            key = key + "_sw"
        self._deps(e, reads, writes)
        if key not in self.dsem:
            sem = self.stack.enter_context(self.nc.semaphore("d_" + key))
            self.dsem[key] = [sem, 0]
        d = self.dsem[key]
        d[1] += 16
        e["h"].dma_start(out=out, in_=in_).then_inc(d[0], 16)
        self.n_inst += 1
        self._mark((key, d[1]), reads, writes)

    def barrier(self):
        for en, e in self.eng.items():
            for xn, x in self.eng.items():
                if xn != en and x["cnt"] > 0 and e["waited"].get(xn, 0) < x["cnt"]:
                    e["h"].wait_ge(x["sem"], x["cnt"])
                    e["waited"][xn] = x["cnt"]
            for key, d in self.dsem.items():
                if d[1] > 0 and e["waited"].get(key, 0) < d[1]:
                    e["h"].wait_ge(d[0], d[1])
                    e["waited"][key] = d[1]


def build(n_layers=L, debug=False, stop_after=None):
    nc = bass.Bass("TRN2", target_bir_lowering=False)
    dt = nc.dram_tensor

    def din(name, shape, dtype=F32):
        return dt(name, list(shape), dtype, kind="ExternalInput").ap()

    x_d = din("x", [S, D])
    p_d = din("p", [L, S, 256])
    pos_d = din("pos", [1, S], I32)
    wa_d = din("wa", [L, 128, 8, 512])
    wb_d = din("wb", [L, 128, 8, 896])
    wg_d = din("wg", [L, 128, 8, 1824])
    wuq_d = din("wuq", [L, 128, 3, 1024])
    wukv_d = din("wukv", [L, 128, 2, 1024])
    wout_d = din("wout", [L, 128, 8, 1024])
    wpg_d = din("wpg", [L, 128, 8, 1024])
    wpp_d = din("wpp", [L, 128, 2, 1024])
    gmix_d = din("gmix", [128, L, 8])
    gq_d = din("gq", [128, L, 3])
    gkv_d = din("gkv", [128, L, 2])
    goutn_d = din("goutn", [128, L])
    gple_d = din("gple", [128, L, 8])
    gfin_d = din("gfin", [128, 8])
    wgf_d = din("wgf", [16, L, 256])
    wgb_d = din("wgb", [16, L, 256])
    bgf_d = din("bgf", [1, L * 256])
    bgb_d = din("bgb", [1, L * 256])
    ident_d = din("ident", [128, 128])
    tri_d = din("tri", [128, 6, 128])
    mask_d = din("mask", [128, 2, 256])
    invf_d = din("invf", [128, 1])

    out_d = dt("out", [S, D], F32, kind="ExternalOutput").ap()
    skind = "ExternalOutput" if debug else "Internal"
    hT_d = dt("hT_s", [D, S], F32, kind=skind).ap()
    xnT_d = dt("xnT_s", [D, S], BF16, kind=skind).ap()
    yT_d = dt("yT_s", [D, S], BF16, kind=skind).ap()
    cs_d = dt("cs_s", [2, 128, S], F32, kind=skind).ap()

    hT_v = hT_d.rearrange("(k p) t -> p k t", p=128)
    xnT_v = xnT_d.rearrange("(k p) t -> p k t", p=128)
    yT_v = yT_d.rearrange("(k p) t -> p k t", p=128)

    with contextlib.ExitStack() as stack:
        cx = Ctx(nc, stack)
        op = cx.op

        uid = [0]

        def sb(st, name, shape, dtype):
            uid[0] += 1
            return st.enter_context(nc.sbuf_tensor("sb%d_%s" % (uid[0], name), list(shape), dtype))

        ident_f = T(sb(stack, "ident_f", [128, 128], F32), "ident_f")
        ident_b = T(sb(stack, "ident_b", [128, 128], BF16), "ident_b")
        ones_b = T(sb(stack, "ones_b", [128, 128], BF16), "ones_b")
        ones_f = T(sb(stack, "ones_f", [128, 128], F32), "ones_f")
        tri = T(sb(stack, "tri", [128, 6, 128], F32), "tri")
        mask = T(sb(stack, "mask", [128, 2, 256], F32), "mask")
        gmix = T(sb(stack, "gmix", [128, L, 8], F32), "gmix")
        gq = T(sb(stack, "gq", [128, L, 3], F32), "gq")
        gkv = T(sb(stack, "gkv", [128, L, 2], F32), "gkv")
        goutn = T(sb(stack, "goutn", [128, L], F32), "goutn")
        gple = T(sb(stack, "gple", [128, L, 8], F32), "gple")
        gfin = T(sb(stack, "gfin", [128, 8], F32), "gfin")
        invf = T(sb(stack, "invf", [128, 1], F32), "invf")
        wgs = T(sb(stack, "wgs", [48, L, 256], F32), "wgs")
        wg_b = T(sb(stack, "wg_b", [48, L, 256], BF16), "wg_b")
        bgs = [T(sb(stack, "bgs%d" % d, [1, L * 256], F32), "bgs%d" % d) for d in range(2)]
        bgh = [T(sb(stack, "bgh%d" % d, [1, 2, L * 256], BF16), "bgh%d" % d) for d in range(2)]
        bgt = T(sb(stack, "bgt", [1, L * 256], F32), "bgt")
        epsb = T(sb(stack, "epsb", [128, 1], F32), "epsb")
        PS = [T(stack.enter_context(nc.psum_tensor("ps%d" % i, [128, 512], F32)), "ps%d" % i, psum=True)
              for i in range(8)]

        for t, d in [(ident_f, ident_d), (tri, tri_d), (mask, mask_d), (gmix, gmix_d), (gq, gq_d),
                     (gkv, gkv_d), (goutn, goutn_d), (gple, gple_d), (gfin, gfin_d), (invf, invf_d),
]:
            cx.dma("sp", t.ap[:], d, "const", writes=[t])
        cx.dma("sp", wgs.ap[0:16], wgf_d, "const", writes=[wgs])
        cx.dma("sp", wgs.ap[32:48], wgb_d, "const", writes=[wgs])
        cx.dma("sp", bgs[0].ap[:], bgf_d, "const", writes=[bgs[0]])
        cx.dma("sp", bgs[1].ap[:], bgb_d, "const", writes=[bgs[1]])
        cx.barrier()
        op("dve", lambda: nc.vector.memset(ones_b.ap[:], 1.0), writes=[ones_b])
        op("dve", lambda: nc.vector.memset(ones_f.ap[:], 1.0), writes=[ones_f])
        for d in range(2):
            op("dve", lambda d=d: nc.vector.tensor_copy(bgh[d].ap[0:1, 0, :], bgs[d].ap[:, :]), reads=[bgs[d]], writes=[bgh[d]])
            op("dve", lambda d=d: nc.vector.tensor_tensor(out=bgt.ap[:, :], in0=bgs[d].ap[:, :], in1=bgh[d].ap[0:1, 0, :], op=ALU.subtract),
               reads=[bgs[d], bgh[d]], writes=[bgt])
            op("dve", lambda d=d: nc.vector.tensor_copy(bgh[d].ap[0:1, 1, :], bgt.ap[:, :]), reads=[bgt], writes=[bgh[d]])
        op("dve", lambda: nc.vector.memset(epsb.ap[:], EPS), writes=[epsb])
        op("dve", lambda: nc.vector.tensor_copy(ident_b.ap[:], ident_f.ap[:]), reads=[ident_f], writes=[ident_b])
        op("dve", lambda: nc.vector.tensor_copy(wg_b.ap[0:16], wgs.ap[0:16]), reads=[wgs], writes=[wg_b])
        op("dve", lambda: nc.vector.tensor_copy(wg_b.ap[32:48], wgs.ap[32:48]), reads=[wgs], writes=[wg_b])
        cx.barrier()

        hT_c = [T(None, "hTd%d" % c) for c in range(NCH)]
        xnT_c = [T(None, "xnTd%d" % c) for c in range(NCH)]
        yT_c = [T(None, "yTd%d" % c) for c in range(NCH)]
        cs_c = [T(None, "csd%d" % c) for c in range(NCH)]

        def rstd_from_psum(ps_t, ps_ap, n_feat, sq_t, rs_t, parts=128, cols=CH):
            op("act", lambda: nc.scalar.activation(sq_t.ap[0:parts, 0:cols], ps_ap, AF.Ln,
                                                    bias=epsb.ap[0:parts, :], scale=1.0 / n_feat),
               reads=[ps_t], writes=[sq_t])
            op("act", lambda: nc.scalar.activation(rs_t.ap[0:parts, 0:cols], sq_t.ap[0:parts, 0:cols], AF.Exp, scale=-0.5),
               reads=[sq_t], writes=[rs_t])

        def rms_fm(h_t, nk, sq_t, tmp_t, rs_t, ps_t, n_feat):
            op("pool", lambda: nc.gpsimd.tensor_tensor(out=sq_t.ap[:, 0:nk, :], in0=h_t.ap[:, 0:nk, :],
                                                       in1=h_t.ap[:, 0:nk, :], op=ALU.mult),
               reads=[h_t], writes=[sq_t])
            for k in range(nk):
                op("pe", lambda k=k: nc.tensor.matmul(ps_t.ap[:, :], ones_b.ap[:, :], sq_t.ap[:, k, :],
                                                      start=(k == 0), stop=(k == nk - 1)),
                   reads=[sq_t], writes=[ps_t], inc=(k == nk - 1))
            rstd_from_psum(ps_t, ps_t.ap[:, :], n_feat, tmp_t, rs_t)

        def load_weight(st_list, w_dram_l, nk, ncols, dst, gain_ap_fn, neg_cols=None, q="sp"):
            for k in range(nk):
                stg = st_list[k % len(st_list)]
                cx.dma(q, stg.ap[:, 0:ncols], w_dram_l[:, k, :], stg.name, writes=[stg])
                g = gain_ap_fn(k)
                if g is None:
                    op("dve", lambda k=k, stg=stg: nc.vector.tensor_copy(dst.ap[:, k, :], stg.ap[:, 0:ncols]),
                       reads=[stg], writes=[dst])
                elif k % 2 == 1:
                    op("act", lambda k=k, stg=stg, g=g: nc.scalar.activation(
                        dst.ap[:, k, :], stg.ap[:, 0:ncols], AF.Copy, scale=g),
                       reads=[stg], writes=[dst])
                    if neg_cols is not None:
                        for (a, b) in neg_cols:
                            op("dve", lambda k=k, a=a, b=b: nc.vector.tensor_scalar(
                                dst.ap[:, k, a:b], dst.ap[:, k, a:b], -1.0, None, ALU.mult),
                               reads=[dst], writes=[dst])
                else:
                    op("dve", lambda k=k, stg=stg, g=g: nc.vector.tensor_scalar(
                        dst.ap[:, k, :], stg.ap[:, 0:ncols], g, None, ALU.mult),
                       reads=[stg], writes=[dst])
                    if neg_cols is not None:
                        for (a, b) in neg_cols:
                            op("dve", lambda k=k, a=a, b=b: nc.vector.tensor_scalar(
                                dst.ap[:, k, a:b], dst.ap[:, k, a:b], -1.0, None, ALU.mult),
                               reads=[dst], writes=[dst])

        with contextlib.ExitStack() as ph:
            posi = T(sb(ph, "posi", [128, S], I32), "posi")
            posf = T(sb(ph, "posf", [128, S], F32), "posf")
            tri_i = T(sb(ph, "tri_i", [128, S], I32), "tri_i")
            frac = T(sb(ph, "frac", [128, S], F32), "frac")
            tab = T(sb(ph, "tab", [128, S], F32), "tab")
            cx.dma("sp", posi.ap[:], pos_d.partition_broadcast(128), "posi", writes=[posi])
            op("dve", lambda: nc.vector.tensor_copy(posf.ap[:], posi.ap[:]), reads=[posi], writes=[posf])
            op("dve", lambda: nc.vector.tensor_scalar(posf.ap[:], posf.ap[:], invf.ap[:, 0:1], None, ALU.mult),
               reads=[posf, invf], writes=[posf])
            op("dve", lambda: nc.vector.tensor_copy(tri_i.ap[:], posf.ap[:]), reads=[posf], writes=[tri_i])
            op("dve", lambda: nc.vector.tensor_copy(frac.ap[:], tri_i.ap[:]), reads=[tri_i], writes=[frac])
            op("dve", lambda: nc.vector.tensor_tensor(out=frac.ap[:], in0=posf.ap[:], in1=frac.ap[:], op=ALU.subtract),
               reads=[posf, frac], writes=[frac])
            for which, shift in ((0, 0.25), (1, 0.0)):
                op("dve", lambda shift=shift: nc.vector.tensor_scalar(tab.ap[:], frac.ap[:], shift, None, ALU.add),
                   reads=[frac], writes=[tab])
                op("dve", lambda: nc.vector.tensor_single_scalar(posf.ap[:], tab.ap[:], 0.5, ALU.is_gt),
                   reads=[tab], writes=[posf])
                op("dve", lambda: nc.vector.tensor_tensor(out=tab.ap[:], in0=tab.ap[:], in1=posf.ap[:], op=ALU.subtract),
                   reads=[tab, posf], writes=[tab])
                op("dve", lambda: nc.vector.scalar_tensor_tensor(tab.ap[:], tab.ap[:], -0.5, tab.ap[:], ALU.is_lt, ALU.add),
                   reads=[tab], writes=[tab])
                op("act", lambda: nc.scalar.activation(tab.ap[:], tab.ap[:], AF.Sin, scale=6.283185),
                   reads=[tab], writes=[tab])
                cx.dma("pool", cs_d[which], tab.ap[:], "tab", reads=[tab], writes=cs_c)
            cx.barrier()

        def emit_xn(h_t, sq_t, tmp_t, rs_t, ps_t, xn_t, c, store_h):
            if store_h:
                cx.dma("pool", hT_v[:, :, c * CH:(c + 1) * CH], h_t.ap[:], h_t.name, reads=[h_t], writes=[hT_c[c]])
            rms_fm(h_t, 8, sq_t, tmp_t, rs_t, ps_t, D)
            for k in range(8):
                op("dve", lambda k=k: nc.vector.tensor_tensor(out=xn_t.ap[:, k, :], in0=h_t.ap[:, k, :],
                                                              in1=rs_t.ap[:, :], op=ALU.mult),
                   reads=[h_t, rs_t], writes=[xn_t])
            cx.dma("pool", xnT_v[:, :, c * CH:(c + 1) * CH], xn_t.ap[:], xn_t.name, reads=[xn_t], writes=[xnT_c[c]])

        def run_staggered(gen_fn, n):
            active = {0: gen_fn(0)}
            nxt = 1
            want = False
            while active:
                for i_ in sorted(active):
                    try:
                        if next(active[i_]) == "half":
                            want = True
                    except StopIteration:
                        del active[i_]
                if want and nxt < n and (nxt - 2) not in active:
                    active[nxt] = gen_fn(nxt)
                    nxt += 1
                    want = False

        with contextlib.ExitStack() as ph:
            Z2 = range(2)
            xin = [[T(sb(ph, "xin%d_%d" % (z, i), [128, D], F32), "xin%d_%d" % (z, i)) for i in range(2)] for z in Z2]
            hb = [T(sb(ph, "hb%d" % i, [128, 8, CH], F32), "hb%d" % i) for i in Z2]
            sqb = [T(sb(ph, "sqb%d" % i, [128, 8, CH], BF16), "sqb%d" % i) for i in Z2]
            tmpb = [T(sb(ph, "tmpb%d" % i, [128, CH], F32), "tmpb%d" % i) for i in Z2]
            rsb = [T(sb(ph, "rsb%d" % i, [128, CH], F32), "rsb%d" % i) for i in Z2]
            xnb = [T(sb(ph, "xnb%d" % i, [128, 8, CH], BF16), "xnb%d" % i) for i in Z2]

            def pro_gen(c):
                z = c % 2
                banks = [PS[4 * z + i] for i in range(4)]
                h_t = hb[z]
                for j in range(4):
                    tl = c * 4 + j
                    xi = xin[z][j % 2]
                    cx.dma("sp", xi.ap[:], x_d[tl * 128:(tl + 1) * 128, :], xi.name, writes=[xi])
                    yield
                    for half in range(2):
                        pt = banks[(j * 2 + half) % 3]
                        for q4 in range(4):
                            k = half * 4 + q4
                            op("pe", lambda: nc.tensor.transpose(
                                pt.ap[:, q4 * 128:(q4 + 1) * 128], xi.ap[:, k * 128:(k + 1) * 128], ident_f.ap[:, :]),
                               reads=[xi], writes=[pt], inc=(q4 == 3))
                        yield
                        src = pt.ap[:, :].rearrange("p (k t) -> p k t", k=4)
                        dst = h_t.ap[:, half * 4:(half + 1) * 4, j * 128:(j + 1) * 128]
                        if half == 0:
                            op("act", lambda: nc.scalar.copy(dst, src), reads=[pt], writes=[h_t])
                        else:
                            op("dve", lambda: nc.vector.tensor_copy(dst, src), reads=[pt], writes=[h_t])
                        yield
                yield "half"
                cx.dma("pool", hT_v[:, :, c * CH:(c + 1) * CH], h_t.ap[:], h_t.name, reads=[h_t], writes=[hT_c[c]])
                op("act", lambda: nc.scalar.activation(sqb[z].ap[:, 0:4, :], h_t.ap[:, 0:4, :], AF.Square),
                   reads=[h_t], writes=[sqb[z]])
                yield
                op("pool", lambda: nc.gpsimd.tensor_tensor(out=sqb[z].ap[:, 4:8, :], in0=h_t.ap[:, 4:8, :],
                                                           in1=h_t.ap[:, 4:8, :], op=ALU.mult),
                   reads=[h_t], writes=[sqb[z]])
                for _ in range(5):
                    yield
                for k in range(8):
                    op("pe", lambda: nc.tensor.matmul(banks[3].ap[:, :], ones_b.ap[:, :], sqb[z].ap[:, k, :],
                                                      start=(k == 0), stop=(k == 7)),
                       reads=[sqb[z]], writes=[banks[3]], inc=(k == 7))
                yield
                op("act", lambda: nc.scalar.activation(tmpb[z].ap[:, :], banks[3].ap[:, :], AF.Ln,
                                                        bias=epsb.ap[:, :], scale=1.0 / D),
                   reads=[banks[3]], writes=[tmpb[z]])
                yield
                op("act", lambda: nc.scalar.activation(rsb[z].ap[:, :], tmpb[z].ap[:, :], AF.Exp, scale=-0.5),
                   reads=[tmpb[z]], writes=[rsb[z]])
                yield
                for k in range(8):
                    op("dve", lambda: nc.vector.tensor_tensor(out=xnb[z].ap[:, k, :], in0=h_t.ap[:, k, :],
                                                              in1=rsb[z].ap[:, :], op=ALU.mult),
                       reads=[h_t, rsb[z]], writes=[xnb[z]])
                    yield
                cx.dma("pool", xnT_v[:, :, c * CH:(c + 1) * CH], xnb[z].ap[:], xnb[z].name, reads=[xnb[z]], writes=[xnT_c[c]])
                yield

            run_staggered(pro_gen, NCH)
            cx.barrier()

        if stop_after == "prologue":
            n_layers = 0

        for l in range(n_layers):
            last = (l == L - 1)
            with contextlib.ExitStack() as p1:
                KT = [T(sb(p1, "KT%d" % c, [128, 4, CH], BF16), "KT%d" % c) for c in range(NCH)]
                krT = [T(sb(p1, "krT%d" % c, [128, CH], BF16), "krT%d" % c) for c in range(NCH)]
                Vt = [T(sb(p1, "V%d" % c, [128, 4, 512], BF16), "V%d" % c) for c in range(NCH)]
                xnb = [T(sb(p1, "xnc%d" % i, [128, 8, CH], BF16), "xnc%d" % i) for i in range(2)]
                csb = [[T(sb(p1, "cs%d_%d" % (w, i), [128, CH], F32), "cs%d_%d" % (w, i)) for i in range(2)]
                       for w in range(2)]
                with contextlib.ExitStack() as p1a:
                    stg = [T(sb(p1a, "stgA%d" % i, [128, 1024], F32), "stgA%d" % i) for i in range(2)]
                    wa = T(sb(p1a, "wa", [128, 8, 512], BF16), "wa")
                    wukv = T(sb(p1a, "wukv", [128, 2, 1024], BF16), "wukv")
                    ckv = T(sb(p1a, "ckv", [128, 2, CH], BF16), "ckv")
                    sq = T(sb(p1a, "sqkv", [128, 2, CH], BF16), "sqkv")
                    tmp = T(sb(p1a, "tmpkv", [128, CH], F32), "tmpkv")
                    rs = T(sb(p1a, "rskv", [128, CH], F32), "rskv")
                    tmpt = T(sb(p1a, "tmpt", [128, 4], F32), "tmpt")
                    rst = T(sb(p1a, "rst", [128, 4], F32), "rst")
                    t1 = T(sb(p1a, "t1", [128, CH], F32), "t1")
                    t2 = T(sb(p1a, "t2", [128, CH], F32), "t2")
                    load_weight(stg, wa_d[l], 8, 512, wa, lambda k: gmix.ap[:, l, k:k + 1],
                                neg_cols=[(384, 416), (448, 480)])
                    load_weight(stg, wukv_d[l], 2, 1024, wukv, lambda k: gkv.ap[:, l, k:k + 1])
                    for c in (range(NCH) if 'loop' not in SKIP else []):
                        xn = xnb[c % 2]
                        cs0, cs1 = csb[0][c % 2], csb[1][c % 2]
                        cx.dma("sp", xn.ap[:], xnT_v[:, :, c * CH:(c + 1) * CH], xn.name, reads=[xnT_c[c]], writes=[xn])
                        cx.dma("sp", cs0.ap[:], cs_d[0, :, c * CH:(c + 1) * CH], cs0.name, reads=[cs_c[c]], writes=[cs0])
                        cx.dma("sp", cs1.ap[:], cs_d[1, :, c * CH:(c + 1) * CH], cs1.name, reads=[cs_c[c]], writes=[cs1])
                        if 'ckv' in SKIP:
                            continue
                        for m in range(2):
                            pt = PS[m]
                            for k in range(8):
                                op("pe", lambda m=m, k=k, pt=pt: nc.tensor.matmul(
                                    pt.ap[:, :], wa.ap[:, k, m * 128:(m + 1) * 128], xn.ap[:, k, :],
                                    start=(k == 0), stop=(k == 7)), reads=[wa, xn], writes=[pt], inc=(k == 7))
                            if 'cp' not in SKIP:
                                op("dve", lambda m=m, pt=pt: nc.vector.tensor_copy(ckv.ap[:, m, :], pt.ap[:, :]), reads=[pt], writes=[ckv])
                            if 'sq' not in SKIP:
                                op("act", lambda m=m, pt=pt: nc.scalar.activation(sq.ap[:, m, :], pt.ap[:, :], AF.Square),
                                   reads=[pt], writes=[sq])
                        if 'ss' in SKIP:
                            continue
                        for m in range(2):
                            op("pe", lambda m=m: nc.tensor.matmul(PS[2].ap[:, :], ones_b.ap[:, :], sq.ap[:, m, :],
                                                                  start=(m == 0), stop=(m == 1)),
                               reads=[sq], writes=[PS[2]], inc=(m == 1))
                        rstd_from_psum(PS[2], PS[2].ap[:, :], 256, tmp, rs)
                        for j in (range(4) if 'rst' not in SKIP else []):
                            for m in range(2):
                                op("pe", lambda m=m, j=j: nc.tensor.matmul(
                                    PS[3].ap[:, j:j + 1], sq.ap[:, m, j * 128:(j + 1) * 128], ones_b.ap[:, 0:1],
                                    start=(m == 0), stop=(m == 1)), reads=[sq], writes=[PS[3]],
                                   inc=(m == 1 and j == 3))
                        if 'rst' not in SKIP:
                            rstd_from_psum(PS[3], PS[3].ap[:, 0:4], 256, tmpt, rst, cols=4)
                        for h in (range(4) if 'kn' not in SKIP else []):
                            pt = PS[4 + (h % 2)]
                            for m in range(2):
                                op("pe", lambda m=m, h=h, pt=pt: nc.tensor.matmul(
                                    pt.ap[:, :], wukv.ap[:, m, h * 128:(h + 1) * 128], ckv.ap[:, m, :],
                                    start=(m == 0), stop=(m == 1)), reads=[wukv, ckv], writes=[pt], inc=(m == 1))
                            op("dve", lambda h=h, pt=pt: nc.vector.tensor_tensor(
                                out=KT[c].ap[:, h, :], in0=pt.ap[:, :], in1=rs.ap[:, :], op=ALU.mult),
                               reads=[pt, rs], writes=[KT[c]])
                        for j in (range(4) if 'v' not in SKIP else []):
                            pt = PS[6 + (j % 2)]
                            for m in range(2):
                                op("pe", lambda m=m, j=j, pt=pt: nc.tensor.matmul(
                                    pt.ap[:, :], ckv.ap[:, m, j * 128:(j + 1) * 128], wukv.ap[:, m, 512:1024],
                                    start=(m == 0), stop=(m == 1)), reads=[wukv, ckv], writes=[pt], inc=(m == 1))
                            op("act", lambda j=j, pt=pt: nc.scalar.activation(
                                Vt[c].ap[:, j, :], pt.ap[:, :], AF.Copy, scale=rst.ap[:, j:j + 1]),
                               reads=[pt, rst], writes=[Vt[c]])
                        for w in (range(2) if 'kr' not in SKIP else []):
                            pt = PS[w]
                            for k in range(8):
                                op("pe", lambda w=w, k=k, pt=pt: nc.tensor.matmul(
                                    pt.ap[:, :], wa.ap[:, k, 256 + w * 128:256 + (w + 1) * 128], xn.ap[:, k, :],
                                    start=(k == 0), stop=(k == 7)), reads=[wa, xn], writes=[pt], inc=(k == 7))
                        if 'kr' not in SKIP:
                            op("dve", lambda: nc.vector.tensor_tensor(out=t1.ap[:, :], in0=PS[0].ap[:, :], in1=cs0.ap[:, :], op=ALU.mult),
                               reads=[PS[0], cs0], writes=[t1])
                            op("dve", lambda: nc.vector.tensor_tensor(out=t2.ap[:, :], in0=PS[1].ap[:, :], in1=cs1.ap[:, :], op=ALU.mult),
                               reads=[PS[1], cs1], writes=[t2])
                            op("pool", lambda: nc.gpsimd.tensor_tensor(out=krT[c].ap[:, :], in0=t1.ap[:, :], in1=t2.ap[:, :], op=ALU.add),
                               reads=[t1, t2], writes=[krT[c]])
                    cx.barrier()

                if stop_after == "p1a":
                    break
                with contextlib.ExitStack() as p1b:
                    wb = T(sb(p1b, "wb", [128, 8, 896], BF16), "wb")
                    wuq = T(sb(p1b, "wuq", [128, 3, 1024], BF16), "wuq")
                    with contextlib.ExitStack() as p1bw:
                        stg = [T(sb(p1bw, "stgB%d" % i, [128, 1024], F32), "stgB%d" % i) for i in range(2)]
                        load_weight(stg, wb_d[l], 8, 896, wb, lambda k: gmix.ap[:, l, k:k + 1])
                        load_weight(stg, wuq_d[l], 3, 1024, wuq, lambda k: gq.ap[:, l, k:k + 1],
                                    neg_cols=[(768, 800), (832, 864), (896, 928), (960, 992)])
                        cx.barrier()
                    Z2 = range(2)
                    cqs = [T(sb(p1b, "cq%d" % z, [128, 3, CH], BF16), "cq%d" % z) for z in Z2]
                    sqs = [T(sb(p1b, "sqq%d" % z, [128, 3, CH], BF16), "sqq%d" % z) for z in Z2]
                    rss = [T(sb(p1b, "rsq%d" % z, [128, CH], F32), "rsq%d" % z) for z in Z2]
                    tmps = rss
                    qns = [T(sb(p1b, "qn%d" % z, [128, 4, CH], BF16), "qn%d" % z) for z in Z2]
                    qrs = [T(sb(p1b, "qr%d" % z, [128, 4, CH], BF16), "qr%d" % z) for z in Z2]
                    gas = [T(sb(p1b, "ga%d" % z, [128, 4, CH], BF16), "ga%d" % z) for z in Z2]
                    t1s = [T(sb(p1b, "t1b%d" % z, [128, CH], F32), "t1b%d" % z) for z in Z2]
                    t2s = [T(sb(p1b, "t2b%d" % z, [128, CH], F32), "t2b%d" % z) for z in Z2]
                    for z in Z2:
                        op("pool", lambda: nc.gpsimd.memset(qrs[z].ap[:], 0.0), writes=[qrs[z]])
                    accd = T(sb(p1b, "accd", [128, CH], F32), "accd")
                    ptb = [T(sb(p1b, "pt%d" % i, [128, CH], BF16), "pt%d" % i) for i in range(4)]
                    rinv = T(sb(p1b, "rinv", [128, CH], F32), "rinv")
                    ym = [T(sb(p1b, "ym%d" % i, [128, 4, CH], BF16), "ym%d" % i) for i in range(2)]

                    def qprep(c):
                        z = c % 2
                        xn = xnb[z]
                        cs0, cs1 = csb[0][z], csb[1][z]
                        cq, sq, tmp, rs, qn, qr, ga, t1, t2 = cqs[z], sqs[z], tmps[z], rss[z], qns[z], qrs[z], gas[z], t1s[z], t2s[z]
                        cx.dma("sp", xn.ap[:], xnT_v[:, :, c * CH:(c + 1) * CH], xn.name, reads=[xnT_c[c]], writes=[xn])
                        cx.dma("sp", cs0.ap[:], cs_d[0, :, c * CH:(c + 1) * CH], cs0.name, reads=[cs_c[c]], writes=[cs0])
                        cx.dma("sp", cs1.ap[:], cs_d[1, :, c * CH:(c + 1) * CH], cs1.name, reads=[cs_c[c]], writes=[cs1])
                        yield
                        for m in range(3):
                            pt = PS[0]
                            for k in range(8):
                                op("pe", lambda: nc.tensor.matmul(
                                    pt.ap[:, :], wb.ap[:, k, m * 128:(m + 1) * 128], xn.ap[:, k, :],
                                    start=(k == 0), stop=(k == 7)), reads=[wb, xn], writes=[pt], inc=(k == 7))
                                if k % 4 == 3:
                                    yield
                            op("dve", lambda: nc.vector.tensor_copy(cq.ap[:, m, :], pt.ap[:, :]), reads=[pt], writes=[cq])
                            yield
                            op("act", lambda: nc.scalar.activation(sq.ap[:, m, :], pt.ap[:, :], AF.Square),
                               reads=[pt], writes=[sq])
                            yield
                        for m in range(3):
                            op("pe", lambda: nc.tensor.matmul(PS[0].ap[:, :], ones_b.ap[:, :], sq.ap[:, m, :],
                                                              start=(m == 0), stop=(m == 2)),
                               reads=[sq], writes=[PS[0]], inc=(m == 2))
                        yield
                        op("act", lambda: nc.scalar.activation(tmp.ap[:, :], PS[0].ap[:, :], AF.Ln, bias=epsb.ap[:, :], scale=1.0 / 384),
                           reads=[PS[0]], writes=[tmp])
                        yield
                        op("act", lambda: nc.scalar.activation(rs.ap[:, :], tmp.ap[:, :], AF.Exp, scale=-0.5), reads=[tmp], writes=[rs])
                        yield
                        for h in range(4):
                            pt = PS[0]
                            for m in range(3):
                                op("pe", lambda: nc.tensor.matmul(
                                    pt.ap[:, :], wuq.ap[:, m, h * 128:(h + 1) * 128], cq.ap[:, m, :],
                                    start=(m == 0), stop=(m == 2)), reads=[wuq, cq], writes=[pt], inc=(m == 2))
                            yield
                            op("dve", lambda: nc.vector.tensor_tensor(
                                out=qn.ap[:, h, :], in0=pt.ap[:, :], in1=rs.ap[:, :], op=ALU.mult),
                               reads=[pt, rs], writes=[qn])
                            yield
                        for pr in range(2):
                            for w in range(2):
                                pt = PS[0]
                                for m in range(3):
                                    col = 512 + w * 256 + pr * 128
                                    op("pe", lambda: nc.tensor.matmul(
                                        pt.ap[:, :], wuq.ap[:, m, col:col + 128], cq.ap[:, m, :],
                                        start=(m == 0), stop=(m == 2)), reads=[wuq, cq], writes=[pt], inc=(m == 2))
                                yield
                                tw, csw = (t1, cs0) if w == 0 else (t2, cs1)
                                op("dve", lambda: nc.vector.tensor_tensor(out=tw.ap[:, :], in0=pt.ap[:, :], in1=csw.ap[:, :], op=ALU.mult),
                                   reads=[pt, csw], writes=[tw])
                                yield
                            op("pool", lambda: nc.gpsimd.tensor_tensor(out=t1.ap[:, :], in0=t1.ap[:, :], in1=t2.ap[:, :], op=ALU.add),
                               reads=[t1, t2], writes=[t1])
                            yield
                            for hh in range(2):
                                hr = slice(hh * 64, (hh + 1) * 64)
                                op("pool", lambda: nc.gpsimd.tensor_tensor(
                                    out=qr.ap[hr, pr * 2 + hh, :], in0=t1.ap[hr, :], in1=rs.ap[hr, :], op=ALU.mult),
                                   reads=[t1, rs], writes=[qr])
                                yield
                        for m in range(4):
                            pt = PS[0]
                            for k in range(8):
                                op("pe", lambda: nc.tensor.matmul(
                                    pt.ap[:, :], wb.ap[:, k, 384 + m * 128:384 + (m + 1) * 128], xn.ap[:, k, :],
                                    start=(k == 0), stop=(k == 7)), reads=[wb, xn], writes=[pt], inc=(k == 7))
                                if k % 4 == 3:
                                    yield
                            op("act", lambda: nc.scalar.activation(ga.ap[:, m, :], pt.ap[:, :], AF.Silu),
                               reads=[pt], writes=[ga])
                            yield

                    accds = [accd, T(sb(p1b, "accd2", [128, CH], F32), "accd2")]
                    rgt = T(sb(p1b, "rgt", [128, CH], F32), "rgt")
                    psum2 = T(sb(p1b, "psum2", [128, CH], BF16), "psum2")

                    def attn_all():
                        tiles = [(c, h, kt) for c in range(NCH) for h in range(4) for kt in range(32)]
                        n = len(tiles)

                        def s_mm(i):
                            c, h, kt = tiles[i]
                            z = c % 2
                            pt = PS[2 + (i % 4)]
                            kc, ko = kt // 4, (kt % 4) * 128
                            op("pe", lambda: nc.tensor.matmul(
                                pt.ap[:, :], KT[kc].ap[:, h, ko:ko + 128], qns[z].ap[:, h, :], start=True, stop=False),
                               reads=[KT[kc], qns[z]], writes=[pt], inc=False)
                            op("pe", lambda: nc.tensor.matmul(
                                pt.ap[:, :], krT[kc].ap[:, ko:ko + 128], qrs[z].ap[:, h, :], start=False, stop=True),
                               reads=[krT[kc], qrs[z]], writes=[pt], inc=True)

                        def epilogue(c, h):
                            z = c % 2
                            g = c * 4 + h
                            po, acc, prs, ymc, ga = PS[6 + (g % 2)], accds[g % 2], PS[1], ym[z], gas[z]
                            op("pe", lambda: nc.tensor.matmul(prs.ap[:, :], ones_f.ap[:, :], acc.ap[:, :], start=True, stop=True),
                               reads=[acc], writes=[prs])
                            op("act", lambda: nc.scalar.activation(rinv.ap[:, :], prs.ap[:, :], AF.Ln), reads=[prs], writes=[rinv])
                            op("act", lambda: nc.scalar.activation(rinv.ap[:, :], rinv.ap[:, :], AF.Exp, scale=-1.0),
                               reads=[rinv], writes=[rinv])
                            op("pool", lambda: nc.gpsimd.tensor_tensor(out=rgt.ap[:, :], in0=rinv.ap[:, :], in1=ga.ap[:, h, :], op=ALU.mult),
                               reads=[rinv, ga], writes=[rgt])
                            op("dve", lambda: nc.vector.tensor_tensor(out=ymc.ap[:, h, :], in0=po.ap[:, :], in1=rgt.ap[:, :], op=ALU.mult),
                               reads=[po, rgt], writes=[ymc])
                            if h == 3:
                                cx.dma("pool", yT_v[:, 0:4, c * CH:(c + 1) * CH], ymc.ap[:], ymc.name, reads=[ymc], writes=[yT_c[c]])

                        pending = []
                        yield ("chunk", 0)
                        s_mm(0)
                        s_mm(1)
                        for i0 in range(0, n, 2):
                            c, h, kt0 = tiles[i0]
                            if h == 0 and kt0 == 0 and c > 0:
                                yield ("chunk", c)
                            g = c * 4 + h
                            po, acc = PS[6 + (g % 2)], accds[g % 2]
                            pbs = [ptb[i0 % 4], ptb[(i0 + 1) % 4]]
                            for u in range(2):
                                pt = PS[2 + ((i0 + u) % 4)]
                                op("act", lambda: nc.scalar.activation(pbs[u].ap[:, :], pt.ap[:, :], AF.Exp, scale=SCALE),
                                   reads=[pt], writes=[pbs[u]])
                            for u in range(2):
                                if i0 + 2 + u < n:
                                    s_mm(i0 + 2 + u)
                            for u in range(2):
                                kt = kt0 + u
                                kc, kj = kt // 4, kt % 4
                                rd = [Vt[kc], pbs[0], pbs[1]] if u == 0 else [Vt[kc], pbs[1]]
                                op("pe", lambda: nc.tensor.matmul(
                                    po.ap[:, :], Vt[kc].ap[:, kj, h * 128:(h + 1) * 128], pbs[u].ap[:, :],
                                    start=(kt == 0), stop=(kt == 31)), reads=rd, writes=[po], inc=True)
                            op("dve", lambda: nc.vector.tensor_tensor(out=psum2.ap[:, :], in0=pbs[0].ap[:, :], in1=pbs[1].ap[:, :], op=ALU.add),
                               reads=[pbs[0], pbs[1]], writes=[psum2])
                            if kt0 == 0:
                                op("dve", lambda: nc.vector.tensor_copy(acc.ap[:, :], psum2.ap[:, :]), reads=[psum2], writes=[acc])
                            else:
                                op("dve", lambda: nc.vector.tensor_tensor(out=acc.ap[:, :], in0=acc.ap[:, :], in1=psum2.ap[:, :], op=ALU.add),
                                   reads=[psum2, acc], writes=[acc])
                            if kt0 + 1 == 31:
                                pending.append((i0 + 4, c, h))
                            while pending and pending[0][0] <= i0:
                                _, c_, h_ = pending.pop(0)
                                epilogue(c_, h_)
                            yield None
                            yield None
                        for _, c_, h_ in pending:
                            epilogue(c_, h_)
                        yield None

                    for _ in qprep(0):
                        pass
                    gens = [attn_all()]
                    while gens:
                        nx = []
                        for g_ in gens:
                            try:
                                tag = next(g_)
                                nx.append(g_)
                                if tag is not None and tag[0] == "chunk" and tag[1] + 1 < NCH:
                                    nx.append(qprep(tag[1] + 1))
                            except StopIteration:
                                pass
                        gens = nx
                    cx.barrier()
            cx.barrier()
            if stop_after == "p1b":
                break
            with contextlib.ExitStack() as p2:
                gqT = [T(sb(p2, "gqT%d" % c, [128, 2, CH], BF16), "gqT%d" % c) for c in range(NCH)]
                gkT = [T(sb(p2, "gkT%d" % c, [128, 2, CH], BF16), "gkT%d" % c) for c in range(NCH)]
                gkt = [T(sb(p2, "gkt%d" % c, [128, 4, 256], BF16), "gkt%d" % c) for c in range(NCH)]
                gvt = [T(sb(p2, "gvt%d" % c, [128, 4, 512], BF16), "gvt%d" % c) for c in range(NCH)]
                ggt = [T(sb(p2, "ggt%d" % c, [128, 4, 512], BF16), "ggt%d" % c) for c in range(NCH)]
                lrT = [T(sb(p2, "lr%d" % c, [48, CH], BF16), "lr%d" % c) for c in range(NCH)]
                with contextlib.ExitStack() as p2a:
                    stg = [T(sb(p2a, "stgG%d" % i, [128, 1824], F32), "stgG%d" % i) for i in range(2)]
                    wg = T(sb(p2a, "wg", [128, 8, 1824], BF16), "wg")
                    xnb = [T(sb(p2a, "xng%d" % i, [128, 8, CH], BF16), "xng%d" % i) for i in range(2)]
                    load_weight(stg, wg_d[l], 8, 1824, wg, lambda k: gmix.ap[:, l, k:k + 1])
                    for c in range(NCH):
                        xn = xnb[c % 2]
                        cx.dma("sp", xn.ap[:], xnT_v[:, :, c * CH:(c + 1) * CH], xn.name, reads=[xnT_c[c]], writes=[xn])
                        for which, dst in ((0, gqT[c]), (1, gkT[c])):
                            for pr in range(2):
                                pt = PS[(which * 2 + pr) % 4]
                                col = which * 256 + pr * 128
                                for k in range(8):
                                    op("pe", lambda k=k, pt=pt, col=col: nc.tensor.matmul(
                                        pt.ap[:, :], wg.ap[:, k, col:col + 128], xn.ap[:, k, :],
                                        start=(k == 0), stop=(k == 7)), reads=[wg, xn], writes=[pt], inc=(k == 7))
                                if which == 0:
                                    op("act", lambda pt=pt, dst=dst, pr=pr: nc.scalar.mul(dst.ap[:, pr, :], pt.ap[:, :], 0.125),
                                       reads=[pt], writes=[dst])
                                else:
                                    op("dve", lambda pt=pt, dst=dst, pr=pr: nc.vector.tensor_copy(dst.ap[:, pr, :], pt.ap[:, :]),
                                       reads=[pt], writes=[dst])
                        for dr in range(2):
                            pt = PS[4 + dr]
                            col = 512 + dr * 16
                            for k in range(8):
                                op("pe", lambda k=k, pt=pt, col=col: nc.tensor.matmul(
                                    pt.ap[32 * dr:32 * dr + 16, :], wg.ap[:, k, col:col + 16], xn.ap[:, k, :],
                                    start=(k == 0), stop=(k == 7)), reads=[wg, xn], writes=[pt], inc=(k == 7))
                            op("dve", lambda pt=pt, dr=dr: nc.vector.tensor_copy(lrT[c].ap[32 * dr:32 * dr + 16, :],
                                                                                 pt.ap[32 * dr:32 * dr + 16, :]),
                               reads=[pt], writes=[lrT[c]])
                        for j in range(4):
                            tk = slice(j * 128, (j + 1) * 128)
                            pk_, pv_, pg_ = PS[6], PS[7], PS[j % 4]
                            for k in range(8):
                                op("pe", lambda k=k: nc.tensor.matmul(
                                    pk_.ap[:, 0:256], xn.ap[:, k, tk], wg.ap[:, k, 544:800],
                                    start=(k == 0), stop=(k == 7)), reads=[wg, xn], writes=[pk_], inc=(k == 7))
                            op("dve", lambda j=j: nc.vector.tensor_copy(gkt[c].ap[:, j, :], pk_.ap[:, 0:256]),
                               reads=[pk_], writes=[gkt[c]])
                            for k in range(8):
                                op("pe", lambda k=k: nc.tensor.matmul(
                                    pv_.ap[:, :], xn.ap[:, k, tk], wg.ap[:, k, 800:1312],
                                    start=(k == 0), stop=(k == 7)), reads=[wg, xn], writes=[pv_], inc=(k == 7))
                            op("act", lambda j=j: nc.scalar.copy(gvt[c].ap[:, j, :], pv_.ap[:, :]),
                               reads=[pv_], writes=[gvt[c]])
                            for k in range(8):
                                op("pe", lambda k=k, pg_=pg_: nc.tensor.matmul(
                                    pg_.ap[:, :], xn.ap[:, k, tk], wg.ap[:, k, 1312:1824],
                                    start=(k == 0), stop=(k == 7)), reads=[wg, xn], writes=[pg_], inc=(k == 7))
                            op("act", lambda j=j, pg_=pg_: nc.scalar.activation(ggt[c].ap[:, j, :], pg_.ap[:, :], AF.Silu),
                               reads=[pg_], writes=[ggt[c]])
                    cx.barrier()
                if stop_after == "p2a":
                    break
                with contextlib.ExitStack() as p2b:
                    opart = [T(sb(p2b, "opart%d" % t, [128, 512], BF16), "opart%d" % t) for t in range(32)]
                    D2 = range(2)
                    S2 = range(2)
                    ez = [T(sb(p2b, "ez%d" % d, [128, 256], F32), "ez%d" % d) for d in D2]
                    spt = ez
                    Eq = [T(sb(p2b, "Eq%d" % d, [128, 2, 128], F32), "Eq%d" % d) for d in D2]
                    Ek = [T(sb(p2b, "Ek%d" % d, [128, 2, 128], F32), "Ek%d" % d) for d in D2]
                    Eqi = [T(sb(p2b, "Eqi%d" % d, [128, 2, 128], F32), "Eqi%d" % d) for d in D2]
                    Es = [T(sb(p2b, "Es%d" % d, [128, 256], F32), "Es%d" % d) for d in D2]
                    dec = [[T(sb(p2b, "dec%d_%d" % (d, z), [128, 2, 2], F32), "dec%d_%d" % (d, z)) for z in S2] for d in D2]
                    qin = [[T(sb(p2b, "qin%d_%d" % (d, z), [128, 2, 128], BF16), "qin%d_%d" % (d, z)) for z in S2] for d in D2]
                    kin = [[T(sb(p2b, "kin%d_%d" % (d, z), [128, 4, 128], BF16), "kin%d_%d" % (d, z)) for z in S2] for d in D2]
                    qint = [[T(sb(p2b, "qint%d_%d" % (d, z), [128, 4, 128], BF16), "qint%d_%d" % (d, z)) for z in S2] for d in D2]
                    kst = [[[T(sb(p2b, "kst%d_%d_%d" % (d, z, cc), [128, 256], BF16), "kst%d_%d_%d" % (d, z, cc))
                             for cc in range(2)] for z in S2] for d in D2]
                    Am = [[T(sb(p2b, "Am%d_%d" % (d, cc), [128, 256], BF16), "Am%d_%d" % (d, cc)) for cc in range(2)] for d in D2]
                    Sf = [T(sb(p2b, "Sf%d" % d, [128, 2, 128], F32), "Sf%d" % d) for d in D2]
                    Sb = [T(sb(p2b, "Sb%d" % d, [128, 2, 128], BF16), "Sb%d" % d) for d in D2]
                    hm = T(sb(p2b, "hm", [128, 2], F32), "hm")
                    ot = T(sb(p2b, "ot", [128, 512], F32), "ot")
                    osq = T(sb(p2b, "osq", [128, 512], F32), "osq")
                    ss = T(sb(p2b, "ssg", [128, 4], F32), "ssg")
                    ssr = T(sb(p2b, "ssr", [128, 4], F32), "ssr")
                    rg = T(sb(p2b, "rg", [128, 4], F32), "rg")
                    yt = T(sb(p2b, "yt", [128, 512], BF16), "yt")
                    ygT = T(sb(p2b, "ygT", [128, 4, 128], BF16), "ygT")
                    op("pool", lambda: nc.gpsimd.memset(hm.ap[:], 0.0), writes=[hm])
                    op("pool", lambda: nc.gpsimd.memset(hm.ap[0:64, 0:1], 1.0), writes=[hm])
                    op("pool", lambda: nc.gpsimd.memset(hm.ap[64:128, 1:2], 1.0), writes=[hm])
                    for d in D2:
                        op("pool", lambda: nc.gpsimd.memset(Sf[d].ap[:], 0.0), writes=[Sf[d]])
                        op("pool", lambda: nc.gpsimd.memset(Sb[d].ap[:], 0.0), writes=[Sb[d]])
                        for cc in range(2):
                            op("pool", lambda: nc.gpsimd.memset(Am[d][cc].ap[:], 0.0), writes=[Am[d][cc]])
                            for z in S2:
                                op("pool", lambda: nc.gpsimd.memset(kst[d][z][cc].ap[:], 0.0), writes=[kst[d][z][cc]])
                    PREP = [PS[0], PS[1]]
                    PA = [PS[2], PS[5]]
                    PO = [PS[3], PS[4]]
                    PKS = [PS[6], PS[7]]

                    def prep(d, t):
                        z = t % 2
                        c, j = t // 4, t % 4
                        tk = slice(j * 128, (j + 1) * 128)
                        lr_rows = slice(32 * d, 32 * d + 16)
                        lastcol = 63 if d == 0 else 0
                        pp = PREP[d]
                        bgx = bgh[d]
                        op("pe", lambda: nc.tensor.matmul(pp.ap[:, 0:256], lrT[c].ap[lr_rows, tk], wg_b.ap[lr_rows, l, :],
                                                          start=True, stop=False),
                           reads=[lrT[c], wg_b], writes=[pp], inc=False)
                        op("pe", lambda: nc.tensor.matmul(pp.ap[:, 0:256], ones_b.ap[0:1, :], bgx.ap[0:1, 0, l * 256:(l + 1) * 256],
                                                          start=False, stop=False),
                           reads=[bgx], writes=[pp], inc=False)
                        op("pe", lambda: nc.tensor.matmul(pp.ap[:, 0:256], ones_b.ap[0:1, :], bgx.ap[0:1, 1, l * 256:(l + 1) * 256],
                                                          start=False, stop=True),
                           reads=[bgx], writes=[pp])
                        yield
                        op("act", lambda: nc.scalar.activation(ez[d].ap[:, :], pp.ap[:, 0:256], AF.Exp, scale=-1.0),
                           reads=[pp], writes=[ez[d]])
                        yield
                        op("act", lambda: nc.scalar.activation(spt[d].ap[:, :], ez[d].ap[:, :], AF.Ln, bias=1.0),
                           reads=[ez[d]], writes=[spt[d]])
                        yield
                        for w in range(2):
                            for pr in range(2):
                                op("pe", lambda: nc.tensor.matmul(
                                    pp.ap[:, w * 256 + pr * 128:w * 256 + (pr + 1) * 128], spt[d].ap[:, pr * 128:(pr + 1) * 128],
                                    tri.ap[:, d * 3 + w, :], start=True, stop=True),
                                   reads=[spt[d], tri], writes=[pp], inc=(w == 1 and pr == 1))
                        yield
                        op("pe", lambda: nc.tensor.matmul(PA[d].ap[:, 256:512], tri.ap[:, d * 3 + 2, :], spt[d].ap[:, :],
                                                          start=True, stop=True),
                           reads=[spt[d], tri], writes=[PA[d]])
                        yield
                        op("act", lambda: nc.scalar.activation(
                            Eq[d].ap[:], pp.ap[:, 256:512].rearrange("p (a i) -> p a i", a=2), AF.Exp),
                           reads=[pp], writes=[Eq[d]])
                        yield
                        op("act", lambda: nc.scalar.activation(
                            Ek[d].ap[:], pp.ap[:, 256:512].rearrange("p (a i) -> p a i", a=2), AF.Exp, scale=-1.0),
                           reads=[pp], writes=[Ek[d]])
                        yield
                        op("act", lambda: nc.scalar.activation(
                            Eqi[d].ap[:], pp.ap[:, 0:256].rearrange("p (a i) -> p a i", a=2), AF.Exp),
                           reads=[pp], writes=[Eqi[d]])
                        yield
                        op("act", lambda: nc.scalar.activation(
                            dec[d][z].ap[:].rearrange("p a c -> p (a c)").rearrange("p (q o) -> p q o", o=1),
                            pp.ap[:, 0:256].rearrange("p (q i) -> p q i", i=64)[:, :, lastcol:lastcol + 1], AF.Exp),
                           reads=[pp], writes=[dec[d][z]])
                        yield
                        op("act", lambda: nc.scalar.activation(Es[d].ap[:, :], PA[d].ap[:, 256:512], AF.Exp),
                           reads=[PA[d]], writes=[Es[d]])
                        yield
                        op("dve", lambda: nc.vector.tensor_tensor(out=qin[d][z].ap[:], in0=gqT[c].ap[:, :, tk], in1=Eq[d].ap[:], op=ALU.mult),
                           reads=[gqT[c], Eq[d]], writes=[qin[d][z]])
                        yield
                        for h in range(4):
                            pr, hh = h // 2, h % 2
                            op("dve", lambda: nc.vector.scalar_tensor_tensor(
                                kin[d][z].ap[:, h, :], gkT[c].ap[:, pr, tk], hm.ap[:, hh:hh + 1], Ek[d].ap[:, pr, :],
                                ALU.mult, ALU.mult), reads=[gkT[c], Ek[d], hm], writes=[kin[d][z]])
                            yield
                            op("dve", lambda: nc.vector.scalar_tensor_tensor(
                                qint[d][z].ap[:, h, :], gqT[c].ap[:, pr, tk], hm.ap[:, hh:hh + 1], Eqi[d].ap[:, pr, :],
                                ALU.mult, ALU.mult), reads=[gqT[c], Eqi[d], hm], writes=[qint[d][z]])
                            yield
                        for cc in range(2):
                            rws = slice(cc * 64, (cc + 1) * 64)
                            op("pool", lambda: nc.gpsimd.tensor_tensor(
                                out=kst[d][z][cc].ap[rws, :], in0=gkt[c].ap[rws, j, :], in1=Es[d].ap[rws, :], op=ALU.mult),
                               reads=[gkt[c], Es[d]], writes=[kst[d][z][cc]])
                            yield

                    def scan(d, t, s_):
                        z = t % 2
                        c, j = t // 4, t % 4
                        po, pa = PO[d], PA[d]
                        PK = PKS[d]
                        PX = PKS[d]
                        for cc in ((0, 1) if d == 0 else (1, 0)):
                            rows = slice(cc * 64, (cc + 1) * 64)
                            cols = slice(cc * 64, (cc + 1) * 64)
                            for h in range(4):
                                pr = h // 2
                                op("pe", lambda: nc.tensor.matmul(
                                    pa.ap[rows, h * 64:(h + 1) * 64], kin[d][z].ap[:, h, cols], qin[d][z].ap[:, pr, cols],
                                    start=True, stop=True), reads=[kin[d][z], qin[d][z]], writes=[pa], inc=(h == 3))
                            yield
                            op("dve", lambda: nc.vector.tensor_tensor(out=Am[d][cc].ap[rows, :], in0=pa.ap[rows, 0:256],
                                                                      in1=mask.ap[rows, d, :], op=ALU.mult),
                               reads=[pa, mask], writes=[Am[d][cc]])
                            yield
                            for h in range(4):
                                pr = h // 2
                                op("pe", lambda: nc.tensor.matmul(
                                    po.ap[rows, h * 128:(h + 1) * 128], Am[d][cc].ap[:, h * 64:(h + 1) * 64],
                                    gvt[c].ap[:, j, h * 128:(h + 1) * 128], start=True, stop=False),
                                   reads=[Am[d][cc], gvt[c]], writes=[po], inc=False)
                                op("pe", lambda: nc.tensor.matmul(
                                    po.ap[rows, h * 128:(h + 1) * 128], qint[d][z].ap[:, h, cols],
                                    Sb[d].ap[:, pr, :], start=False, stop=True),
                                   reads=[qint[d][z], Sb[d]], writes=[po], inc=(h == 3))
                            yield
                            for h in range(4):
                                pr, hh = h // 2, h % 2
                                hr = slice(hh * 64, (hh + 1) * 64)
                                op("pe", lambda: nc.tensor.matmul(
                                    PK.ap[hr, pr * 128:(pr + 1) * 128], kst[d][z][cc].ap[:, h * 64:(h + 1) * 64],
                                    gvt[c].ap[:, j, h * 128:(h + 1) * 128], start=True, stop=True),
                                   reads=[kst[d][z][cc], gvt[c]], writes=[PK], inc=(h == 3))
                            yield
                            for pr in range(2):
                                op("dve", lambda: nc.vector.scalar_tensor_tensor(
                                    Sf[d].ap[:, pr, :], Sf[d].ap[:, pr, :], dec[d][z].ap[:, pr, cc:cc + 1],
                                    PK.ap[:, pr * 128:(pr + 1) * 128], ALU.mult, ALU.add),
                                   reads=[Sf[d], dec[d][z], PK], writes=[Sf[d]])
                            yield
                            op("act", lambda: nc.scalar.copy(Sb[d].ap[:], Sf[d].ap[:]), reads=[Sf[d]], writes=[Sb[d]])
                            yield
                        if s_ < 16:
                            op("act", lambda: nc.scalar.copy(opart[t].ap[:, :], po.ap[:, :]), reads=[po], writes=[opart[t]])
                            yield
                            return
                        op("dve", lambda: nc.vector.tensor_tensor(out=ot.ap[:, :], in0=po.ap[:, :], in1=opart[t].ap[:, :], op=ALU.add),
                           reads=[po, opart[t]], writes=[ot])
                        op("pool", lambda: nc.gpsimd.tensor_tensor(out=osq.ap[:, :], in0=ot.ap[:, :], in1=ot.ap[:, :], op=ALU.mult),
                           reads=[ot], writes=[osq])
                        op("dve", lambda: nc.vector.reduce_sum(ss.ap[:, :], osq.ap[:, :].rearrange("p (h v) -> p h v", h=4), axis=AX.X),
                           reads=[osq], writes=[ss])
                        op("act", lambda: nc.scalar.activation(ssr.ap[:, :], ss.ap[:, :], AF.Ln, bias=epsb.ap[:, :], scale=1.0 / 128),
                           reads=[ss], writes=[ssr])
                        op("act", lambda: nc.scalar.activation(rg.ap[:, :], ssr.ap[:, :], AF.Exp, scale=-0.5), reads=[ssr], writes=[rg])
                        for h in range(4):
                            hs = slice(h * 128, (h + 1) * 128)
                            op("dve", lambda: nc.vector.scalar_tensor_tensor(
                                yt.ap[:, hs], ot.ap[:, hs], rg.ap[:, h:h + 1], ggt[c].ap[:, j, hs], ALU.mult, ALU.mult),
                               reads=[ot, rg, ggt[c]], writes=[yt])
                            pview = PX.ap[:, :].bitcast(BF16)
                        for h in range(4):
                            op("pe", lambda: nc.tensor.transpose(pview[:, h * 128:(h + 1) * 128],
                                                                 yt.ap[:, h * 128:(h + 1) * 128], ident_b.ap[:, :]),
                               reads=[yt], writes=[PX], inc=(h == 3))
                        op("act", lambda: nc.scalar.copy(ygT.ap[:].rearrange("p h t -> p (h t)"), pview[:, 0:512]),
                           reads=[PX], writes=[ygT])
                        cx.dma("sp", yT_v[:, 4:8, t * 128:(t + 1) * 128], ygT.ap[:], ygT.name, reads=[ygT], writes=[yT_c[c]])
                        yield

                    def run_zipped(gens, weights=None):
                        gens = list(gens)
                        weights = list(weights) if weights else [1] * len(gens)
                        while gens:
                            nxt, nw = [], []
                            for g, w in zip(gens, weights):
                                alive = True
                                for _ in range(w):
                                    try:
                                        next(g)
                                    except StopIteration:
                                        alive = False
                                        break
                                if alive:
                                    nxt.append(g)
                                    nw.append(w)
                            gens, weights = nxt, nw

                    run_zipped([prep(0, 0), prep(1, 31)])
                    for s_ in range(32):
                        gens = [scan(0, s_, s_), scan(1, 31 - s_, s_)]
                        wts = [1, 1]
                        if s_ + 1 < 32:
                            gens += [prep(0, s_ + 1), prep(1, 30 - s_)]
                            wts += [1, 1]
                        run_zipped(gens, wts)
                    cx.barrier()
            cx.barrier()
            if stop_after == "p2b":
                break
            with contextlib.ExitStack() as p3:
                stg = [T(sb(p3, "stgO%d" % i, [128, 1024], F32), "stgO%d" % i) for i in range(2)]
                wout = T(sb(p3, "wout", [128, 8, 1024], BF16), "wout")
                wpg = T(sb(p3, "wpg", [128, 8, 1024], BF16), "wpg")
                wpp = T(sb(p3, "wpp", [128, 2, 1024], BF16), "wpp")
                Z2 = range(2)
                yb = [T(sb(p3, "yb%d" % i, [128, 8, CH], BF16), "yb%d" % i) for i in Z2]
                hb = [T(sb(p3, "hb3_%d" % i, [128, 8, CH], F32), "hb3_%d" % i) for i in range(3)]
                pin = [[T(sb(p3, "pin%d_%d" % (z, i), [128, 256], F32), "pin%d_%d" % (z, i)) for i in range(4)] for z in Z2]
                pTb = [T(sb(p3, "pT%d" % z, [128, 2, CH], BF16), "pT%d" % z) for z in Z2]
                sqbs = [T(sb(p3, "sqb3_%d" % z, [128, 8, CH], BF16), "sqb3_%d" % z) for z in Z2]
                tmpbs = [T(sb(p3, "tmpb3_%d" % z, [128, CH], F32), "tmpb3_%d" % z) for z in Z2]
                rsbs = [T(sb(p3, "rsb3_%d" % z, [128, CH], F32), "rsb3_%d" % z) for z in Z2]
                hns = [T(sb(p3, "hn%d" % z, [128, 8, CH], BF16), "hn%d" % z) for z in Z2]
                sigs = [[T(sb(p3, "sig%d_%d" % (z, i), [128, CH], F32), "sig%d_%d" % (z, i)) for i in range(2)] for z in Z2]
                if not last:
                    xnbs = [T(sb(p3, "xno%d" % i, [128, 8, CH], BF16), "xno%d" % i) for i in Z2]
                else:
                    otiles = [[T(sb(p3, "otile%d_%d" % (z, i), [128, D], F32), "otile%d_%d" % (z, i)) for i in range(2)] for z in Z2]
                load_weight(stg, wout_d[l], 8, 1024, wout, lambda k: (None if k < 4 else goutn.ap[:, l:l + 1]))
                load_weight(stg, wpg_d[l], 8, 1024, wpg, lambda k: gple.ap[:, l, k:k + 1])
                load_weight(stg, wpp_d[l], 2, 1024, wpp, lambda k: None)

                def rms_gen(h_, sq_t, tmp_t, rs_t, ps_t):
                    op("act", lambda: nc.scalar.activation(sq_t.ap[:, 0:4, :], h_.ap[:, 0:4, :], AF.Square),
                       reads=[h_], writes=[sq_t])
                    yield
                    op("act", lambda: nc.scalar.activation(sq_t.ap[:, 4:8, :], h_.ap[:, 4:8, :], AF.Square),
                       reads=[h_], writes=[sq_t])
                    for _ in range(5):
                        yield
                    for k in range(8):
                        op("pe", lambda: nc.tensor.matmul(ps_t.ap[:, :], ones_b.ap[:, :], sq_t.ap[:, k, :],
                                                          start=(k == 0), stop=(k == 7)),
                           reads=[sq_t], writes=[ps_t], inc=(k == 7))
                    yield
                    op("act", lambda: nc.scalar.activation(tmp_t.ap[:, :], ps_t.ap[:, :], AF.Ln,
                                                            bias=epsb.ap[:, :], scale=1.0 / D),
                       reads=[ps_t], writes=[tmp_t])
                    yield
                    op("act", lambda: nc.scalar.activation(rs_t.ap[:, :], tmp_t.ap[:, :], AF.Exp, scale=-0.5),
                       reads=[tmp_t], writes=[rs_t])
                    yield

                def chunk_gen(c):
                    z = c % 2
                    B0, B1, B2, B3 = PS[4 * z], PS[4 * z + 1], PS[4 * z + 2], PS[4 * z + 3]
                    y_, h_, pT_, hn = yb[z], hb[c % 3], pTb[z], hns[z]
                    sqb, tmpb, rsb = sqbs[z], tmpbs[z], rsbs[z]

                    def load_h(cc):
                        cx.dma("sp", hb[cc % 3].ap[:], hT_v[:, :, cc * CH:(cc + 1) * CH], hb[cc % 3].name,
                               reads=[hT_c[cc]], writes=[hb[cc % 3]])

                    def loads(cc):
                        cx.dma("sp", yb[cc % 2].ap[:], yT_v[:, :, cc * CH:(cc + 1) * CH], yb[cc % 2].name,
                               reads=[yT_c[cc]], writes=[yb[cc % 2]])
                        for jt in range(4):
                            tl = cc * 4 + jt
                            pt_ = pin[cc % 2][jt]
                            cx.dma("sp", pt_.ap[:], p_d[l, tl * 128:(tl + 1) * 128, :], pt_.name, writes=[pt_])

                    if c < 2:
                        loads(c)
                    if c == 0:
                        load_h(0)
                        load_h(1)
                        load_h(2)
                    yield
                    for m in range(8):
                        pt = B0 if m % 2 == 0 else B1
                        for k in range(8):
                            op("pe", lambda: nc.tensor.matmul(
                                pt.ap[:, :], wout.ap[:, k, m * 128:(m + 1) * 128], y_.ap[:, k, :],
                                start=(k == 0), stop=(k == 7)), reads=[wout, y_], writes=[pt], inc=(k == 7))
                        yield
                        op("dve", lambda: nc.vector.tensor_tensor(out=h_.ap[:, m, :], in0=pt.ap[:, :],
                                                                  in1=h_.ap[:, m, :], op=ALU.add),
                           reads=[pt, h_], writes=[h_])
                        yield
                    for fc in range(2):
                        for jt in range(4):
                            op("pe", lambda: nc.tensor.transpose(
                                B2.ap[:, jt * 128:(jt + 1) * 128], pin[z][jt].ap[:, fc * 128:(fc + 1) * 128], ident_f.ap[:, :]),
                               reads=[pin[z][jt]], writes=[B2], inc=(jt == 3))
                        yield
                        op("act", lambda: nc.scalar.copy(pT_.ap[:, fc, :], B2.ap[:, :]), reads=[B2], writes=[pT_])
                        yield
                    if c + 2 < NCH:
                        loads(c + 2)
                    yield "half"
                    yield from rms_gen(h_, sqb, tmpb, rsb, B3)
                    for k in range(8):
                        op("dve", lambda: nc.vector.tensor_tensor(out=hn.ap[:, k, :], in0=h_.ap[:, k, :],
                                                                  in1=rsb.ap[:, :], op=ALU.mult),
                           reads=[h_, rsb], writes=[hn])
                        yield
                    for m in range(8):
                        pg_ = B0 if m % 2 == 0 else B1
                        pp_ = B2
                        sg = sigs[z][m % 2]
                        for k in range(8):
                            op("pe", lambda: nc.tensor.matmul(
                                pg_.ap[:, :], wpg.ap[:, k, m * 128:(m + 1) * 128], hn.ap[:, k, :],
                                start=(k == 0), stop=(k == 7)), reads=[wpg, hn], writes=[pg_], inc=(k == 7))
                        yield
                        op("act", lambda: nc.scalar.activation(sg.ap[:, :], pg_.ap[:, :], AF.Sigmoid),
                           reads=[pg_], writes=[sg])
                        yield
                        for fc in range(2):
                            op("pe", lambda: nc.tensor.matmul(
                                pp_.ap[:, :], wpp.ap[:, fc, m * 128:(m + 1) * 128], pT_.ap[:, fc, :],
                                start=(fc == 0), stop=(fc == 1)), reads=[wpp, pT_], writes=[pp_], inc=(fc == 1))
                        yield
                        op("dve", lambda: nc.vector.tensor_tensor(out=sg.ap[:, :], in0=pp_.ap[:, :],
                                                                  in1=sg.ap[:, :], op=ALU.mult),
                           reads=[pp_, sg], writes=[sg])
                        yield
                        op("dve", lambda: nc.vector.tensor_tensor(out=h_.ap[:, m, :], in0=h_.ap[:, m, :],
                                                                  in1=sg.ap[:, :], op=ALU.add),
                           reads=[h_, sg], writes=[h_])
                        yield
                    if not last:
                        xn_t = xnbs[z]
                        cx.dma("pool", hT_v[:, :, c * CH:(c + 1) * CH], h_.ap[:], h_.name, reads=[h_], writes=[hT_c[c]])
                        yield from rms_gen(h_, sqb, tmpb, rsb, B3)
                        for k in range(8):
                            op("dve", lambda: nc.vector.tensor_tensor(out=xn_t.ap[:, k, :], in0=h_.ap[:, k, :],
                                                                      in1=rsb.ap[:, :], op=ALU.mult),
                               reads=[h_, rsb], writes=[xn_t])
                            yield
                        cx.dma("pool", xnT_v[:, :, c * CH:(c + 1) * CH], xn_t.ap[:], xn_t.name, reads=[xn_t], writes=[xnT_c[c]])
                        yield
                    else:
                        yield from rms_gen(h_, sqb, tmpb, rsb, B3)
                        for k in range(8):
                            op("dve", lambda: nc.vector.scalar_tensor_tensor(
                                h_.ap[:, k, :], h_.ap[:, k, :], gfin.ap[:, k:k + 1], rsb.ap[:, :], ALU.mult, ALU.mult),
                               reads=[h_, gfin, rsb], writes=[h_])
                            yield
                        banks = [B0, B1, B2, B3]
                        for jt in range(4):
                            tl = c * 4 + jt
                            ot_ = otiles[z][jt % 2]
                            for half in range(2):
                                pt = banks[(jt * 2 + half) % 4]
                                for q4 in range(4):
                                    k = half * 4 + q4
                                    op("pe", lambda: nc.tensor.transpose(
                                        pt.ap[:, q4 * 128:(q4 + 1) * 128], h_.ap[:, k, jt * 128:(jt + 1) * 128], ident_f.ap[:, :]),
                                       reads=[h_], writes=[pt], inc=(q4 == 3))
                                yield
                                if half == 0:
                                    op("act", lambda: nc.scalar.copy(ot_.ap[:, 0:512], pt.ap[:, :]), reads=[pt], writes=[ot_])
                                else:
                                    op("dve", lambda: nc.vector.tensor_copy(ot_.ap[:, 512:1024], pt.ap[:, :]), reads=[pt], writes=[ot_])
                                yield
                            cx.dma("pool", out_d[tl * 128:(tl + 1) * 128, :], ot_.ap[:], ot_.name, reads=[ot_])
                            yield

                    if c + 3 < NCH:
                        load_h(c + 3)
                        yield

                active = {0: chunk_gen(0)}
                nxt_c = 1
                want_start = False
                while active:
                    for c_ in sorted(active):
                        try:
                            tag = next(active[c_])
                            if tag == "half":
                                want_start = True
                        except StopIteration:
                            del active[c_]
                    if want_start and nxt_c < NCH and (nxt_c - 2) not in active:
                        active[nxt_c] = chunk_gen(nxt_c)
                        nxt_c += 1
                        want_start = False
                cx.barrier()

        cx.barrier()
    return nc, cx.n_inst


def _kchunk(w):
    K, C = w.shape
    return np.ascontiguousarray(w.reshape(K // 128, 128, C).transpose(1, 0, 2))


def _rot_cols(w):
    return np.concatenate([w[:, 32:64], w[:, 0:32]], axis=1)


def prep_shared(inp):
    f = lambda a: np.asarray(a, dtype=np.float32)
    w_in = f(inp["w_in"])
    w_uq = f(inp["w_uq"])
    w_ukv = f(inp["w_ukv"])
    sh = {}
    o = np.cumsum([0, 384, 256, 64, 512, 256, 256, 512, 16, 16, 512])
    wa, wb, wg, wuq, wukv = [], [], [], [], []
    for l in range(L):
        W = w_in[l]
        cq, ckv, kr, gate_a, gq_, gk_, gv_, lrf, lrb, gate_g = [W[:, o[i]:o[i + 1]] for i in range(10)]
        krr = _rot_cols(kr)
        wa.append(_kchunk(np.concatenate([ckv, kr, kr, krr, krr], axis=1)))
        wb.append(_kchunk(np.concatenate([cq, gate_a], axis=1)))
        wg.append(_kchunk(np.concatenate([gq_, gk_, lrf, lrb, gk_, gv_, gate_g], axis=1)))
        U = w_uq[l].reshape(384, 4, 192)
        nope = U[:, :, :128].reshape(384, 512)
        rope = U[:, :, 128:]
        rope_cat = rope.reshape(384, 256)
        rot_cat = np.concatenate([_rot_cols(rope[:, h, :]) for h in range(4)], axis=1)
        wuq.append(_kchunk(np.concatenate([nope, rope_cat, rot_cat], axis=1)))
        KV = w_ukv[l].reshape(256, 4, 256)
        kn = KV[:, :, :128].reshape(256, 512)
        vv = KV[:, :, 128:].reshape(256, 512)
        wukv.append(_kchunk(np.concatenate([kn, vv], axis=1)))
    sh["wa"] = np.stack(wa)
    sh["wb"] = np.stack(wb)
    sh["wg"] = np.stack(wg)
    sh["wuq"] = np.stack(wuq)
    sh["wukv"] = np.stack(wukv)
    sh["wout"] = np.stack([_kchunk(f(inp["w_out"])[l]) for l in range(L)])
    sh["wpg"] = np.stack([_kchunk(f(inp["w_ple_gate"])[l]) for l in range(L)])
    sh["wpp"] = np.stack([_kchunk(f(inp["w_ple_proj"])[l]) for l in range(L)])

    def pk(g, nk):
        return np.ascontiguousarray(g.reshape(L, nk, 128).transpose(2, 0, 1))
    sh["gmix"] = pk(f(inp["ln_mix"]), 8)
    sh["gq"] = pk(f(inp["mla_q_norm"]), 3)
    sh["gkv"] = pk(f(inp["mla_kv_norm"]), 2)
    sh["goutn"] = np.ascontiguousarray(f(inp["gla_out_norm"]).T)
    sh["gple"] = pk(f(inp["ple_norm"]), 8)
    sh["gfin"] = np.ascontiguousarray(f(inp["final_norm"]).reshape(8, 128).T)
    sh["wgf"] = np.ascontiguousarray(f(inp["gla_w_gate_fwd"]).transpose(1, 0, 2))
    sh["wgb"] = np.ascontiguousarray(f(inp["gla_w_gate_bwd"]).transpose(1, 0, 2))
    sh["bgf"] = np.ascontiguousarray(f(inp["gla_b_gate_fwd"]).reshape(1, L * 256))
    sh["bgb"] = np.ascontiguousarray(f(inp["gla_b_gate_bwd"]).reshape(1, L * 256))
    sh["ident"] = np.eye(128, dtype=np.float32)
    j = np.arange(128)[:, None]
    i = np.arange(128)[None, :]
    same = (j // 64) == (i // 64)
    v = np.float32(-1.0 / 16.0)
    tri = np.zeros((128, 6, 128), np.float32)
    Tf = np.where(same & (j <= i), v, 0).astype(np.float32)
    Tb = np.where(same & (j >= i), v, 0).astype(np.float32)
    reff = (np.arange(128) // 64) * 64 + 31
    refb = (np.arange(128) // 64) * 64 + 32
    tri[:, 0, :] = Tf
    tri[:, 1, :] = Tf - Tf[:, reff]
    tri[:, 2, :] = np.where(same & (j > i), v, 0)
    tri[:, 3, :] = Tb
    tri[:, 4, :] = Tb - Tb[:, refb]
    tri[:, 5, :] = np.where(same & (j < i), v, 0)
    sh["tri"] = tri
    jl = (np.arange(128) % 64)[:, None]
    il = np.arange(64)[None, :]
    mk = np.zeros((128, 2, 4, 64), np.float32)
    mk[:, 0, :, :] = (jl <= il).astype(np.float32)[:, None, :]
    mk[:, 1, :, :] = (jl >= il).astype(np.float32)[:, None, :]
    sh["mask"] = mk.reshape(128, 2, 256)
    half = 32
    inv = (10000.0 ** (-np.arange(half, dtype=np.float32) / half)).astype(np.float32)
    invf = (inv.astype(np.float64) / (2 * np.pi)).astype(np.float32)
    sh["invf"] = np.ascontiguousarray(np.tile(invf, 4).reshape(128, 1))
    return sh


def make_in_maps(inp):
    sh = prep_shared(inp)
    x = np.asarray(inp["x"], dtype=np.float32)
    p = np.asarray(inp["p"], dtype=np.float32)
    pos = np.asarray(inp["positions"], dtype=np.int32)
    maps = []
    for b in range(8):
        m = dict(sh)
        m["x"] = np.ascontiguousarray(x[b])
        m["p"] = np.ascontiguousarray(p[:, b])
        m["pos"] = np.ascontiguousarray(pos[b:b + 1])
        maps.append(m)
    return maps


def kernel(**inputs):
    nc, _ = build()
    maps = make_in_maps(inputs)
    res = run_bass_kernel_spmd(nc, maps, core_ids=list(range(8)))
    return np.stack([np.asarray(r["out"], dtype=np.float32) for r in res.results], axis=0)
```

```python
import contextlib
import numpy as np
import concourse.bass as bass
import concourse.mybir as mybir
from concourse.bass_utils import run_bass_kernel_spmd

F32 = mybir.dt.float32
BF16 = mybir.dt.bfloat16
I32 = mybir.dt.int32
AF = mybir.ActivationFunctionType
ALU = mybir.AluOpType
AX = mybir.AxisListType

S = 4096
D = 1024
NCH = 8
CH = 512
L = 2
EPS = 1e-6
SCALE = float((128 + 64) ** -0.5)
SAME_ENGINE_SYNC = True
import os
SKIP = set(os.environ.get('DBG_SKIP', '').split(','))


class T:
    __slots__ = ("ap", "name", "w", "r", "psum")

    def __init__(self, ap, name, psum=False):
        self.ap = ap
        self.name = name
        self.w = None
        self.r = []
        self.psum = psum

    def __getitem__(self, i):
        return self.ap[i]


class Ctx:
    def __init__(self, nc, stack):
        self.nc = nc
        self.stack = stack
        self.eng = {}
        for name, h in [("pe", nc.tensor), ("act", nc.scalar), ("dve", nc.vector),
                        ("pool", nc.gpsimd), ("sp", nc.sync)]:
            sem = stack.enter_context(nc.semaphore("s_" + name))
            self.eng[name] = dict(h=h, sem=sem, cnt=0, waited={}, name=name)
        self.dsem = {}
        self.n_inst = 0

    def _sem(self, key):
        if key in self.eng:
            return self.eng[key]["sem"]
        return self.dsem[key][0]

    def _need(self, e, stamp, acc):
        key, val = stamp
        if key == e["name"]:
            if key == "pe" or not SAME_ENGINE_SYNC:
                return
        if e["waited"].get(key, 0) >= val:
            return
        if acc.get(key, 0) < val:
            acc[key] = val

    def _deps(self, e, reads, writes):
        acc = {}
        for t in reads:
            if t.w is not None:
                self._need(e, t.w, acc)
            if t.psum:
                for st in t.r:
                    self._need(e, st, acc)
        me = e["name"]
        for t in writes:
            if t.w is not None and t.w[0] != me:
                self._need(e, t.w, acc)
            for st in t.r:
                if st[0] != me:
                    self._need(e, st, acc)
        for key, val in acc.items():
            e["h"].wait_ge(self._sem(key), val)
            e["waited"][key] = val

    def _mark(self, stamp, reads, writes):
        for t in reads:
            if t.psum:
                t.w = stamp
                t.r = []
                continue
            t.r.append(stamp)
            if len(t.r) > 64:
                best = {}
                for k, v in t.r:
                    if best.get(k, 0) < v:
                        best[k] = v
                t.r = list(best.items())
        for t in writes:
            t.w = stamp
            t.r = []

    def op(self, engname, fn, reads=(), writes=(), inc=True):
        e = self.eng[engname]
        self._deps(e, reads, writes)
        ins = fn()
        self.n_inst += 1
        if inc:
            e["cnt"] += 1
            ins.then_inc(e["sem"], 1)
            stamp = (engname, e["cnt"])
        else:
            stamp = (engname, e["cnt"] + 1)
        self._mark(stamp, reads, writes)
        return ins

    def dma(self, q, out, in_, key, reads=(), writes=()):
        e = self.eng[q]
        if q == "pool":
            key = key + "_sw"
        self._deps(e, reads, writes)
        if key not in self.dsem:
            sem = self.stack.enter_context(self.nc.semaphore("d_" + key))
            self.dsem[key] = [sem, 0]
        d = self.dsem[key]
        d[1] += 16
        e["h"].dma_start(out=out, in_=in_).then_inc(d[0], 16)
        self.n_inst += 1
        self._mark((key, d[1]), reads, writes)

    def barrier(self):
        for en, e in self.eng.items():
            for xn, x in self.eng.items():
                if xn != en and x["cnt"] > 0 and e["waited"].get(xn, 0) < x["cnt"]:
                    e["h"].wait_ge(x["sem"], x["cnt"])
                    e["waited"][xn] = x["cnt"]
            for key, d in self.dsem.items():
                if d[1] > 0 and e["waited"].get(key, 0) < d[1]:
                    e["h"].wait_ge(d[0], d[1])
                    e["waited"][key] = d[1]


def build(n_layers=L, debug=False, stop_after=None):
    nc = bass.Bass("TRN2", target_bir_lowering=False)
    dt = nc.dram_tensor

    def din(name, shape, dtype=F32):
        return dt(name, list(shape), dtype, kind="ExternalInput").ap()

    x_d = din("x", [S, D])
    p_d = din("p", [L, S, 256])
    pos_d = din("pos", [1, S], I32)
    wa_d = din("wa", [L, 128, 8, 512])
    wb_d = din("wb", [L, 128, 8, 896])
    wg_d = din("wg", [L, 128, 8, 1824])
    wuq_d = din("wuq", [L, 128, 3, 1024])
    wukv_d = din("wukv", [L, 128, 2, 1024])
    wout_d = din("wout", [L, 128, 8, 1024])
    wpg_d = din("wpg", [L, 128, 8, 1024])
    wpp_d = din("wpp", [L, 128, 2, 1024])
    gmix_d = din("gmix", [128, L, 8])
    gq_d = din("gq", [128, L, 3])
    gkv_d = din("gkv", [128, L, 2])
    goutn_d = din("goutn", [128, L])
    gple_d = din("gple", [128, L, 8])
    gfin_d = din("gfin", [128, 8])
    wgf_d = din("wgf", [16, L, 256])
    wgb_d = din("wgb", [16, L, 256])
    bgf_d = din("bgf", [1, L * 256])
    bgb_d = din("bgb", [1, L * 256])
    ident_d = din("ident", [128, 128])
    tri_d = din("tri", [128, 6, 128])
    mask_d = din("mask", [128, 2, 256])
    invf_d = din("invf", [128, 2])

    out_d = dt("out", [S, D], F32, kind="ExternalOutput").ap()
    skind = "ExternalOutput" if debug else "Internal"
    hT_d = dt("hT_s", [D, S], F32, kind=skind).ap()
    xnT_d = dt("xnT_s", [D, S], BF16, kind=skind).ap()
    yT_d = dt("yT_s", [D, S], BF16, kind=skind).ap()
    cs_d = dt("cs_s", [2, 128, S], F32, kind=skind).ap()

    hT_v = hT_d.rearrange("(k p) t -> p k t", p=128)
    xnT_v = xnT_d.rearrange("(k p) t -> p k t", p=128)
    yT_v = yT_d.rearrange("(k p) t -> p k t", p=128)

    with contextlib.ExitStack() as stack:
        cx = Ctx(nc, stack)
        op = cx.op

        uid = [0]

        def sb(st, name, shape, dtype):
            uid[0] += 1
            return st.enter_context(nc.sbuf_tensor("sb%d_%s" % (uid[0], name), list(shape), dtype))

        ident_f = T(sb(stack, "ident_f", [128, 128], F32), "ident_f")
        ident_b = T(sb(stack, "ident_b", [128, 128], BF16), "ident_b")
        ones_b = T(sb(stack, "ones_b", [128, 128], BF16), "ones_b")
        ones_f = T(sb(stack, "ones_f", [128, 128], F32), "ones_f")
        tri = T(sb(stack, "tri", [128, 6, 128], F32), "tri")
        mask = T(sb(stack, "mask", [128, 2, 256], F32), "mask")
        gmix = T(sb(stack, "gmix", [128, L, 8], F32), "gmix")
        gq = T(sb(stack, "gq", [128, L, 3], F32), "gq")
        gkv = T(sb(stack, "gkv", [128, L, 2], F32), "gkv")
        goutn = T(sb(stack, "goutn", [128, L], F32), "goutn")
        gple = T(sb(stack, "gple", [128, L, 8], F32), "gple")
        gfin = T(sb(stack, "gfin", [128, 8], F32), "gfin")
        invf = T(sb(stack, "invf", [128, 2], F32), "invf")
        wgs = T(sb(stack, "wgs", [48, L, 256], F32), "wgs")
        wg_b = T(sb(stack, "wg_b", [48, L, 256], BF16), "wg_b")
        bgs = [T(sb(stack, "bgs%d" % d, [1, L * 256], F32), "bgs%d" % d) for d in range(2)]
        bgh = [T(sb(stack, "bgh%d" % d, [1, 2, L * 256], BF16), "bgh%d" % d) for d in range(2)]
        bgt = T(sb(stack, "bgt", [1, L * 256], F32), "bgt")
        epsb = T(sb(stack, "epsb", [128, 1], F32), "epsb")
        PS = [T(stack.enter_context(nc.psum_tensor("ps%d" % i, [128, 512], F32)), "ps%d" % i, psum=True)
              for i in range(8)]

        for t, d in [(ident_f, ident_d), (tri, tri_d), (mask, mask_d), (gmix, gmix_d), (gq, gq_d),
                     (gkv, gkv_d), (goutn, goutn_d), (gple, gple_d), (gfin, gfin_d), (invf, invf_d),
]:
            cx.dma("sp", t.ap[:], d, "const", writes=[t])
        cx.dma("sp", wgs.ap[0:16], wgf_d, "const", writes=[wgs])
        cx.dma("sp", wgs.ap[32:48], wgb_d, "const", writes=[wgs])
        cx.dma("sp", bgs[0].ap[:], bgf_d, "const", writes=[bgs[0]])
        cx.dma("sp", bgs[1].ap[:], bgb_d, "const", writes=[bgs[1]])
        cx.barrier()
        op("dve", lambda: nc.vector.memset(ones_b.ap[:], 1.0), writes=[ones_b])
        op("dve", lambda: nc.vector.memset(ones_f.ap[:], 1.0), writes=[ones_f])
        for d in range(2):
            op("dve", lambda d=d: nc.vector.tensor_copy(bgh[d].ap[0:1, 0, :], bgs[d].ap[:, :]), reads=[bgs[d]], writes=[bgh[d]])
            op("dve", lambda d=d: nc.vector.tensor_tensor(out=bgt.ap[:, :], in0=bgs[d].ap[:, :], in1=bgh[d].ap[0:1, 0, :], op=ALU.subtract),
               reads=[bgs[d], bgh[d]], writes=[bgt])
            op("dve", lambda d=d: nc.vector.tensor_copy(bgh[d].ap[0:1, 1, :], bgt.ap[:, :]), reads=[bgt], writes=[bgh[d]])
        op("dve", lambda: nc.vector.memset(epsb.ap[:], EPS), writes=[epsb])
        op("dve", lambda: nc.vector.tensor_copy(ident_b.ap[:], ident_f.ap[:]), reads=[ident_f], writes=[ident_b])
        op("dve", lambda: nc.vector.tensor_copy(wg_b.ap[0:16], wgs.ap[0:16]), reads=[wgs], writes=[wg_b])
        op("dve", lambda: nc.vector.tensor_copy(wg_b.ap[32:48], wgs.ap[32:48]), reads=[wgs], writes=[wg_b])
        cx.barrier()

        hT_c = [T(None, "hTd%d" % c) for c in range(NCH)]
        xnT_c = [T(None, "xnTd%d" % c) for c in range(NCH)]
        yT_c = [T(None, "yTd%d" % c) for c in range(NCH)]
        cs_c = [T(None, "csd%d" % c) for c in range(NCH)]

        def rstd_from_psum(ps_t, ps_ap, n_feat, sq_t, rs_t, parts=128, cols=CH):
            op("act", lambda: nc.scalar.activation(sq_t.ap[0:parts, 0:cols], ps_ap, AF.Ln,
                                                    bias=epsb.ap[0:parts, :], scale=1.0 / n_feat),
               reads=[ps_t], writes=[sq_t])
            op("act", lambda: nc.scalar.activation(rs_t.ap[0:parts, 0:cols], sq_t.ap[0:parts, 0:cols], AF.Exp, scale=-0.5),
               reads=[sq_t], writes=[rs_t])

        def rms_fm(h_t, nk, sq_t, tmp_t, rs_t, ps_t, n_feat):
            op("pool", lambda: nc.gpsimd.tensor_tensor(out=sq_t.ap[:, 0:nk, :], in0=h_t.ap[:, 0:nk, :],
                                                       in1=h_t.ap[:, 0:nk, :], op=ALU.mult),
               reads=[h_t], writes=[sq_t])
            for k in range(nk):
                op("pe", lambda k=k: nc.tensor.matmul(ps_t.ap[:, :], ones_b.ap[:, :], sq_t.ap[:, k, :],
                                                      start=(k == 0), stop=(k == nk - 1)),
                   reads=[sq_t], writes=[ps_t], inc=(k == nk - 1))
            rstd_from_psum(ps_t, ps_t.ap[:, :], n_feat, tmp_t, rs_t)

        def load_weight(st_list, w_dram_l, nk, ncols, dst, gain_ap_fn, neg_cols=None, q="sp"):
            for k in range(nk):
                stg = st_list[k % len(st_list)]
                cx.dma(q, stg.ap[:, 0:ncols], w_dram_l[:, k, :], stg.name, writes=[stg])
                g = gain_ap_fn(k)
                if g is None:
                    op("dve", lambda k=k, stg=stg: nc.vector.tensor_copy(dst.ap[:, k, :], stg.ap[:, 0:ncols]),
                       reads=[stg], writes=[dst])
                elif k % 2 == 1:
                    op("act", lambda k=k, stg=stg, g=g: nc.scalar.activation(
                        dst.ap[:, k, :], stg.ap[:, 0:ncols], AF.Copy, scale=g),
                       reads=[stg], writes=[dst])
                    if neg_cols is not None:
                        for (a, b) in neg_cols:
                            op("dve", lambda k=k, a=a, b=b: nc.vector.tensor_scalar(
                                dst.ap[:, k, a:b], dst.ap[:, k, a:b], -1.0, None, ALU.mult),
                               reads=[dst], writes=[dst])
                else:
                    op("dve", lambda k=k, stg=stg, g=g: nc.vector.tensor_scalar(
                        dst.ap[:, k, :], stg.ap[:, 0:ncols], g, None, ALU.mult),
                       reads=[stg], writes=[dst])
                    if neg_cols is not None:
                        for (a, b) in neg_cols:
                            op("dve", lambda k=k, a=a, b=b: nc.vector.tensor_scalar(
                                dst.ap[:, k, a:b], dst.ap[:, k, a:b], -1.0, None, ALU.mult),
                               reads=[dst], writes=[dst])

        with contextlib.ExitStack() as ph:
            posi = T(sb(ph, "posi", [128, S], I32), "posi")
            posf = T(sb(ph, "posf", [128, S], F32), "posf")
            tri_i = T(sb(ph, "tri_i", [128, S], I32), "tri_i")
            frac = T(sb(ph, "frac", [128, S], F32), "frac")
            tab = T(sb(ph, "tab", [128, S], F32), "tab")
            cx.dma("sp", posi.ap[:], pos_d.partition_broadcast(128), "posi", writes=[posi])
            op("dve", lambda: nc.vector.tensor_copy(posf.ap[:], posi.ap[:]), reads=[posi], writes=[posf])
            op("dve", lambda: nc.vector.tensor_scalar(frac.ap[:], posf.ap[:], invf.ap[:, 1:2], None, ALU.mult),
               reads=[posf, invf], writes=[frac])
            op("dve", lambda: nc.vector.scalar_tensor_tensor(posf.ap[:], posf.ap[:], invf.ap[:, 0:1], frac.ap[:], ALU.mult, ALU.add),
               reads=[posf, invf, frac], writes=[posf])
            op("dve", lambda: nc.vector.tensor_copy(tri_i.ap[:], posf.ap[:]), reads=[posf], writes=[tri_i])
            op("dve", lambda: nc.vector.tensor_copy(frac.ap[:], tri_i.ap[:]), reads=[tri_i], writes=[frac])
            op("dve", lambda: nc.vector.tensor_tensor(out=frac.ap[:], in0=posf.ap[:], in1=frac.ap[:], op=ALU.subtract),
               reads=[posf, frac], writes=[frac])
            for which, shift in ((0, 0.25), (1, 0.0)):
                op("dve", lambda shift=shift: nc.vector.tensor_scalar(tab.ap[:], frac.ap[:], shift, None, ALU.add),
                   reads=[frac], writes=[tab])
                op("dve", lambda: nc.vector.tensor_single_scalar(posf.ap[:], tab.ap[:], 0.5, ALU.is_gt),
                   reads=[tab], writes=[posf])
                op("dve", lambda: nc.vector.tensor_tensor(out=tab.ap[:], in0=tab.ap[:], in1=posf.ap[:], op=ALU.subtract),
                   reads=[tab, posf], writes=[tab])
                op("dve", lambda: nc.vector.scalar_tensor_tensor(tab.ap[:], tab.ap[:], -0.5, tab.ap[:], ALU.is_lt, ALU.add),
                   reads=[tab], writes=[tab])
                op("act", lambda: nc.scalar.activation(tab.ap[:], tab.ap[:], AF.Sin, scale=6.283185),
                   reads=[tab], writes=[tab])
                cx.dma("pool", cs_d[which], tab.ap[:], "tab", reads=[tab], writes=cs_c)
            cx.barrier()

        def emit_xn(h_t, sq_t, tmp_t, rs_t, ps_t, xn_t, c, store_h):
            if store_h:
                cx.dma("pool", hT_v[:, :, c * CH:(c + 1) * CH], h_t.ap[:], h_t.name, reads=[h_t], writes=[hT_c[c]])
            rms_fm(h_t, 8, sq_t, tmp_t, rs_t, ps_t, D)
            for k in range(8):
                op("dve", lambda k=k: nc.vector.tensor_tensor(out=xn_t.ap[:, k, :], in0=h_t.ap[:, k, :],
                                                              in1=rs_t.ap[:, :], op=ALU.mult),
                   reads=[h_t, rs_t], writes=[xn_t])
            cx.dma("pool", xnT_v[:, :, c * CH:(c + 1) * CH], xn_t.ap[:], xn_t.name, reads=[xn_t], writes=[xnT_c[c]])

        def run_staggered(gen_fn, n):
            active = {0: gen_fn(0)}
            nxt = 1
            want = False
            while active:
                for i_ in sorted(active):
                    try:
                        if next(active[i_]) == "half":
                            want = True
                    except StopIteration:
                        del active[i_]
                if want and nxt < n and (nxt - 2) not in active:
                    active[nxt] = gen_fn(nxt)
                    nxt += 1
                    want = False

        with contextlib.ExitStack() as ph:
            Z2 = range(2)
            xin = [[T(sb(ph, "xin%d_%d" % (z, i), [128, D], F32), "xin%d_%d" % (z, i)) for i in range(2)] for z in Z2]
            hb = [T(sb(ph, "hb%d" % i, [128, 8, CH], F32), "hb%d" % i) for i in Z2]
            sqb = [T(sb(ph, "sqb%d" % i, [128, 8, CH], BF16), "sqb%d" % i) for i in Z2]
            tmpb = [T(sb(ph, "tmpb%d" % i, [128, CH], F32), "tmpb%d" % i) for i in Z2]
            rsb = [T(sb(ph, "rsb%d" % i, [128, CH], F32), "rsb%d" % i) for i in Z2]
            xnb = [T(sb(ph, "xnb%d" % i, [128, 8, CH], BF16), "xnb%d" % i) for i in Z2]

            def pro_gen(c):
                z = c % 2
                banks = [PS[4 * z + i] for i in range(4)]
                h_t = hb[z]
                for j in range(4):
                    tl = c * 4 + j
                    xi = xin[z][j % 2]
                    cx.dma("sp", xi.ap[:], x_d[tl * 128:(tl + 1) * 128, :], xi.name, writes=[xi])
                    yield
                    for half in range(2):
                        pt = banks[(j * 2 + half) % 3]
                        for q4 in range(4):
                            k = half * 4 + q4
                            op("pe", lambda: nc.tensor.transpose(
                                pt.ap[:, q4 * 128:(q4 + 1) * 128], xi.ap[:, k * 128:(k + 1) * 128], ident_f.ap[:, :]),
                               reads=[xi], writes=[pt], inc=(q4 == 3))
                        yield
                        src = pt.ap[:, :].rearrange("p (k t) -> p k t", k=4)
                        dst = h_t.ap[:, half * 4:(half + 1) * 4, j * 128:(j + 1) * 128]
                        if half == 0:
                            op("act", lambda: nc.scalar.copy(dst, src), reads=[pt], writes=[h_t])
                        else:
                            op("dve", lambda: nc.vector.tensor_copy(dst, src), reads=[pt], writes=[h_t])
                        yield
                yield "half"
                cx.dma("pool", hT_v[:, :, c * CH:(c + 1) * CH], h_t.ap[:], h_t.name, reads=[h_t], writes=[hT_c[c]])
                op("act", lambda: nc.scalar.activation(sqb[z].ap[:, 0:4, :], h_t.ap[:, 0:4, :], AF.Square),
                   reads=[h_t], writes=[sqb[z]])
                yield
                op("pool", lambda: nc.gpsimd.tensor_tensor(out=sqb[z].ap[:, 4:8, :], in0=h_t.ap[:, 4:8, :],
                                                           in1=h_t.ap[:, 4:8, :], op=ALU.mult),
                   reads=[h_t], writes=[sqb[z]])
                for _ in range(5):
                    yield
                for k in range(8):
                    op("pe", lambda: nc.tensor.matmul(banks[3].ap[:, :], ones_b.ap[:, :], sqb[z].ap[:, k, :],
                                                      start=(k == 0), stop=(k == 7)),
                       reads=[sqb[z]], writes=[banks[3]], inc=(k == 7))
                yield
                op("act", lambda: nc.scalar.activation(tmpb[z].ap[:, :], banks[3].ap[:, :], AF.Ln,
                                                        bias=epsb.ap[:, :], scale=1.0 / D),
                   reads=[banks[3]], writes=[tmpb[z]])
                yield
                op("act", lambda: nc.scalar.activation(rsb[z].ap[:, :], tmpb[z].ap[:, :], AF.Exp, scale=-0.5),
                   reads=[tmpb[z]], writes=[rsb[z]])
                yield
                for k in range(8):
                    op("dve", lambda: nc.vector.tensor_tensor(out=xnb[z].ap[:, k, :], in0=h_t.ap[:, k, :],
                                                              in1=rsb[z].ap[:, :], op=ALU.mult),
                       reads=[h_t, rsb[z]], writes=[xnb[z]])
                    yield
                cx.dma("pool", xnT_v[:, :, c * CH:(c + 1) * CH], xnb[z].ap[:], xnb[z].name, reads=[xnb[z]], writes=[xnT_c[c]])
                yield

            run_staggered(pro_gen, NCH)
            cx.barrier()

        if stop_after == "prologue":
            n_layers = 0

        for l in range(n_layers):
            last = (l == L - 1)
            with contextlib.ExitStack() as p1:
                KT = [T(sb(p1, "KT%d" % c, [128, 4, CH], BF16), "KT%d" % c) for c in range(NCH)]
                krT = [T(sb(p1, "krT%d" % c, [128, CH], BF16), "krT%d" % c) for c in range(NCH)]
                Vt = [T(sb(p1, "V%d" % c, [128, 4, 512], BF16), "V%d" % c) for c in range(NCH)]
                xnb = [T(sb(p1, "xnc%d" % i, [128, 8, CH], BF16), "xnc%d" % i) for i in range(2)]
                csb = [[T(sb(p1, "cs%d_%d" % (w, i), [128, CH], F32), "cs%d_%d" % (w, i)) for i in range(2)]
                       for w in range(2)]
                with contextlib.ExitStack() as p1a:
                    stg = [T(sb(p1a, "stgA%d" % i, [128, 1024], F32), "stgA%d" % i) for i in range(2)]
                    wa = T(sb(p1a, "wa", [128, 8, 512], BF16), "wa")
                    wukv = T(sb(p1a, "wukv", [128, 2, 1024], BF16), "wukv")
                    ckv = T(sb(p1a, "ckv", [128, 2, CH], BF16), "ckv")
                    sq = T(sb(p1a, "sqkv", [128, 2, CH], BF16), "sqkv")
                    tmp = T(sb(p1a, "tmpkv", [128, CH], F32), "tmpkv")
                    rs = T(sb(p1a, "rskv", [128, CH], F32), "rskv")
                    tmpt = T(sb(p1a, "tmpt", [128, 4], F32), "tmpt")
                    rst = T(sb(p1a, "rst", [128, 4], F32), "rst")
                    t1 = T(sb(p1a, "t1", [128, CH], F32), "t1")
                    t2 = T(sb(p1a, "t2", [128, CH], F32), "t2")
                    load_weight(stg, wa_d[l], 8, 512, wa, lambda k: gmix.ap[:, l, k:k + 1],
                                neg_cols=[(384, 416), (448, 480)])
                    load_weight(stg, wukv_d[l], 2, 1024, wukv, lambda k: gkv.ap[:, l, k:k + 1])
                    for c in (range(NCH) if 'loop' not in SKIP else []):
                        xn = xnb[c % 2]
                        cs0, cs1 = csb[0][c % 2], csb[1][c % 2]
                        cx.dma("sp", xn.ap[:], xnT_v[:, :, c * CH:(c + 1) * CH], xn.name, reads=[xnT_c[c]], writes=[xn])
                        cx.dma("sp", cs0.ap[:], cs_d[0, :, c * CH:(c + 1) * CH], cs0.name, reads=[cs_c[c]], writes=[cs0])
                        cx.dma("sp", cs1.ap[:], cs_d[1, :, c * CH:(c + 1) * CH], cs1.name, reads=[cs_c[c]], writes=[cs1])
                        if 'ckv' in SKIP:
                            continue
                        for m in range(2):
                            pt = PS[m]
                            for k in range(8):
                                op("pe", lambda m=m, k=k, pt=pt: nc.tensor.matmul(
                                    pt.ap[:, :], wa.ap[:, k, m * 128:(m + 1) * 128], xn.ap[:, k, :],
                                    start=(k == 0), stop=(k == 7)), reads=[wa, xn], writes=[pt], inc=(k == 7))
                            if 'cp' not in SKIP:
                                op("dve", lambda m=m, pt=pt: nc.vector.tensor_copy(ckv.ap[:, m, :], pt.ap[:, :]), reads=[pt], writes=[ckv])
                            if 'sq' not in SKIP:
                                op("act", lambda m=m, pt=pt: nc.scalar.activation(sq.ap[:, m, :], pt.ap[:, :], AF.Square),
                                   reads=[pt], writes=[sq])
                        if 'ss' in SKIP:
                            continue
                        for m in range(2):
                            op("pe", lambda m=m: nc.tensor.matmul(PS[2].ap[:, :], ones_b.ap[:, :], sq.ap[:, m, :],
                                                                  start=(m == 0), stop=(m == 1)),
                               reads=[sq], writes=[PS[2]], inc=(m == 1))
                        rstd_from_psum(PS[2], PS[2].ap[:, :], 256, tmp, rs)
                        for j in (range(4) if 'rst' not in SKIP else []):
                            for m in range(2):
                                op("pe", lambda m=m, j=j: nc.tensor.matmul(
                                    PS[3].ap[:, j:j + 1], sq.ap[:, m, j * 128:(j + 1) * 128], ones_b.ap[:, 0:1],
                                    start=(m == 0), stop=(m == 1)), reads=[sq], writes=[PS[3]],
                                   inc=(m == 1 and j == 3))
                        if 'rst' not in SKIP:
                            rstd_from_psum(PS[3], PS[3].ap[:, 0:4], 256, tmpt, rst, cols=4)
                        for h in (range(4) if 'kn' not in SKIP else []):
                            pt = PS[4 + (h % 2)]
                            for m in range(2):
                                op("pe", lambda m=m, h=h, pt=pt: nc.tensor.matmul(
                                    pt.ap[:, :], wukv.ap[:, m, h * 128:(h + 1) * 128], ckv.ap[:, m, :],
                                    start=(m == 0), stop=(m == 1)), reads=[wukv, ckv], writes=[pt], inc=(m == 1))
                            op("dve", lambda h=h, pt=pt: nc.vector.tensor_tensor(
                                out=KT[c].ap[:, h, :], in0=pt.ap[:, :], in1=rs.ap[:, :], op=ALU.mult),
                               reads=[pt, rs], writes=[KT[c]])
                        for j in (range(4) if 'v' not in SKIP else []):
                            pt = PS[6 + (j % 2)]
                            for m in range(2):
                                op("pe", lambda m=m, j=j, pt=pt: nc.tensor.matmul(
                                    pt.ap[:, :], ckv.ap[:, m, j * 128:(j + 1) * 128], wukv.ap[:, m, 512:1024],
                                    start=(m == 0), stop=(m == 1)), reads=[wukv, ckv], writes=[pt], inc=(m == 1))
                            op("act", lambda j=j, pt=pt: nc.scalar.activation(
                                Vt[c].ap[:, j, :], pt.ap[:, :], AF.Copy, scale=rst.ap[:, j:j + 1]),
                               reads=[pt, rst], writes=[Vt[c]])
                        for w in (range(2) if 'kr' not in SKIP else []):
                            pt = PS[w]
                            for k in range(8):
                                op("pe", lambda w=w, k=k, pt=pt: nc.tensor.matmul(
                                    pt.ap[:, :], wa.ap[:, k, 256 + w * 128:256 + (w + 1) * 128], xn.ap[:, k, :],
                                    start=(k == 0), stop=(k == 7)), reads=[wa, xn], writes=[pt], inc=(k == 7))
                        if 'kr' not in SKIP:
                            op("dve", lambda: nc.vector.tensor_tensor(out=t1.ap[:, :], in0=PS[0].ap[:, :], in1=cs0.ap[:, :], op=ALU.mult),
                               reads=[PS[0], cs0], writes=[t1])
                            op("dve", lambda: nc.vector.tensor_tensor(out=t2.ap[:, :], in0=PS[1].ap[:, :], in1=cs1.ap[:, :], op=ALU.mult),
                               reads=[PS[1], cs1], writes=[t2])
                            op("pool", lambda: nc.gpsimd.tensor_tensor(out=krT[c].ap[:, :], in0=t1.ap[:, :], in1=t2.ap[:, :], op=ALU.add),
                               reads=[t1, t2], writes=[krT[c]])
                    cx.barrier()

                if stop_after == "p1a":
                    break
                with contextlib.ExitStack() as p1b:
                    wb = T(sb(p1b, "wb", [128, 8, 896], BF16), "wb")
                    wuq = T(sb(p1b, "wuq", [128, 3, 1024], BF16), "wuq")
                    with contextlib.ExitStack() as p1bw:
                        stg = [T(sb(p1bw, "stgB%d" % i, [128, 1024], F32), "stgB%d" % i) for i in range(2)]
                        load_weight(stg, wb_d[l], 8, 896, wb, lambda k: gmix.ap[:, l, k:k + 1])
                        load_weight(stg, wuq_d[l], 3, 1024, wuq, lambda k: gq.ap[:, l, k:k + 1],
                                    neg_cols=[(768, 800), (832, 864), (896, 928), (960, 992)])
                        cx.barrier()
                    Z2 = range(2)
                    cqs = [T(sb(p1b, "cq%d" % z, [128, 3, CH], BF16), "cq%d" % z) for z in Z2]
                    sqs = [T(sb(p1b, "sqq%d" % z, [128, 3, CH], BF16), "sqq%d" % z) for z in Z2]
                    rss = [T(sb(p1b, "rsq%d" % z, [128, CH], F32), "rsq%d" % z) for z in Z2]
                    tmps = rss
                    qns = [T(sb(p1b, "qn%d" % z, [128, 4, CH], BF16), "qn%d" % z) for z in Z2]
                    qrs = [T(sb(p1b, "qr%d" % z, [128, 4, CH], BF16), "qr%d" % z) for z in Z2]
                    gas = [T(sb(p1b, "ga%d" % z, [128, 4, CH], BF16), "ga%d" % z) for z in Z2]
                    t1s = [T(sb(p1b, "t1b%d" % z, [128, CH], F32), "t1b%d" % z) for z in Z2]
                    t2s = [T(sb(p1b, "t2b%d" % z, [128, CH], F32), "t2b%d" % z) for z in Z2]
                    for z in Z2:
                        op("pool", lambda: nc.gpsimd.memset(qrs[z].ap[:], 0.0), writes=[qrs[z]])
                    accd = T(sb(p1b, "accd", [128, CH], F32), "accd")
                    ptb = [T(sb(p1b, "pt%d" % i, [128, CH], BF16), "pt%d" % i) for i in range(4)]
                    rinv = T(sb(p1b, "rinv", [128, CH], F32), "rinv")
                    ym = [T(sb(p1b, "ym%d" % i, [128, 4, CH], BF16), "ym%d" % i) for i in range(2)]

                    def qprep(c):
                        z = c % 2
                        xn = xnb[z]
                        cs0, cs1 = csb[0][z], csb[1][z]
                        cq, sq, tmp, rs, qn, qr, ga, t1, t2 = cqs[z], sqs[z], tmps[z], rss[z], qns[z], qrs[z], gas[z], t1s[z], t2s[z]
                        cx.dma("sp", xn.ap[:], xnT_v[:, :, c * CH:(c + 1) * CH], xn.name, reads=[xnT_c[c]], writes=[xn])
                        cx.dma("sp", cs0.ap[:], cs_d[0, :, c * CH:(c + 1) * CH], cs0.name, reads=[cs_c[c]], writes=[cs0])
                        cx.dma("sp", cs1.ap[:], cs_d[1, :, c * CH:(c + 1) * CH], cs1.name, reads=[cs_c[c]], writes=[cs1])
                        yield
                        for m in range(3):
                            pt = PS[0]
                            for k in range(8):
                                op("pe", lambda: nc.tensor.matmul(
                                    pt.ap[:, :], wb.ap[:, k, m * 128:(m + 1) * 128], xn.ap[:, k, :],
                                    start=(k == 0), stop=(k == 7)), reads=[wb, xn], writes=[pt], inc=(k == 7))
                                if k % 4 == 3:
                                    yield
                            op("dve", lambda: nc.vector.tensor_copy(cq.ap[:, m, :], pt.ap[:, :]), reads=[pt], writes=[cq])
                            yield
                            op("act", lambda: nc.scalar.activation(sq.ap[:, m, :], pt.ap[:, :], AF.Square),
                               reads=[pt], writes=[sq])
                            yield
                        for m in range(3):
                            op("pe", lambda: nc.tensor.matmul(PS[0].ap[:, :], ones_b.ap[:, :], sq.ap[:, m, :],
                                                              start=(m == 0), stop=(m == 2)),
                               reads=[sq], writes=[PS[0]], inc=(m == 2))
                        yield
                        op("act", lambda: nc.scalar.activation(tmp.ap[:, :], PS[0].ap[:, :], AF.Ln, bias=epsb.ap[:, :], scale=1.0 / 384),
                           reads=[PS[0]], writes=[tmp])
                        yield
                        op("act", lambda: nc.scalar.activation(rs.ap[:, :], tmp.ap[:, :], AF.Exp, scale=-0.5), reads=[tmp], writes=[rs])
                        yield
                        for h in range(4):
                            pt = PS[0]
                            for m in range(3):
                                op("pe", lambda: nc.tensor.matmul(
                                    pt.ap[:, :], wuq.ap[:, m, h * 128:(h + 1) * 128], cq.ap[:, m, :],
                                    start=(m == 0), stop=(m == 2)), reads=[wuq, cq], writes=[pt], inc=(m == 2))
                            yield
                            op("dve", lambda: nc.vector.tensor_tensor(
                                out=qn.ap[:, h, :], in0=pt.ap[:, :], in1=rs.ap[:, :], op=ALU.mult),
                               reads=[pt, rs], writes=[qn])
                            yield
                        for pr in range(2):
                            for w in range(2):
                                pt = PS[0]
                                for m in range(3):
                                    col = 512 + w * 256 + pr * 128
                                    op("pe", lambda: nc.tensor.matmul(
                                        pt.ap[:, :], wuq.ap[:, m, col:col + 128], cq.ap[:, m, :],
                                        start=(m == 0), stop=(m == 2)), reads=[wuq, cq], writes=[pt], inc=(m == 2))
                                yield
                                tw, csw = (t1, cs0) if w == 0 else (t2, cs1)
                                op("dve", lambda: nc.vector.tensor_tensor(out=tw.ap[:, :], in0=pt.ap[:, :], in1=csw.ap[:, :], op=ALU.mult),
                                   reads=[pt, csw], writes=[tw])
                                yield
                            op("pool", lambda: nc.gpsimd.tensor_tensor(out=t1.ap[:, :], in0=t1.ap[:, :], in1=t2.ap[:, :], op=ALU.add),
                               reads=[t1, t2], writes=[t1])
                            yield
                            for hh in range(2):
                                hr = slice(hh * 64, (hh + 1) * 64)
                                op("pool", lambda: nc.gpsimd.tensor_tensor(
                                    out=qr.ap[hr, pr * 2 + hh, :], in0=t1.ap[hr, :], in1=rs.ap[hr, :], op=ALU.mult),
                                   reads=[t1, rs], writes=[qr])
                                yield
                        for m in range(4):
                            pt = PS[0]
                            for k in range(8):
                                op("pe", lambda: nc.tensor.matmul(
                                    pt.ap[:, :], wb.ap[:, k, 384 + m * 128:384 + (m + 1) * 128], xn.ap[:, k, :],
                                    start=(k == 0), stop=(k == 7)), reads=[wb, xn], writes=[pt], inc=(k == 7))
                                if k % 4 == 3:
                                    yield
                            op("act", lambda: nc.scalar.activation(ga.ap[:, m, :], pt.ap[:, :], AF.Silu),
                               reads=[pt], writes=[ga])
                            yield

                    accds = [accd, T(sb(p1b, "accd2", [128, CH], F32), "accd2")]
                    rgt = T(sb(p1b, "rgt", [128, CH], F32), "rgt")
                    psum2 = T(sb(p1b, "psum2", [128, CH], BF16), "psum2")

                    def attn_all():
                        tiles = [(c, h, kt) for c in range(NCH) for h in range(4) for kt in range(32)]
                        n = len(tiles)

                        def s_mm(i):
                            c, h, kt = tiles[i]
                            z = c % 2
                            pt = PS[2 + (i % 4)]
                            kc, ko = kt // 4, (kt % 4) * 128
                            op("pe", lambda: nc.tensor.matmul(
                                pt.ap[:, :], KT[kc].ap[:, h, ko:ko + 128], qns[z].ap[:, h, :], start=True, stop=False),
                               reads=[KT[kc], qns[z]], writes=[pt], inc=False)
                            op("pe", lambda: nc.tensor.matmul(
                                pt.ap[:, :], krT[kc].ap[:, ko:ko + 128], qrs[z].ap[:, h, :], start=False, stop=True),
                               reads=[krT[kc], qrs[z]], writes=[pt], inc=True)

                        def epilogue(c, h):
                            z = c % 2
                            g = c * 4 + h
                            po, acc, prs, ymc, ga = PS[6 + (g % 2)], accds[g % 2], PS[1], ym[z], gas[z]
                            op("pe", lambda: nc.tensor.matmul(prs.ap[:, :], ones_f.ap[:, :], acc.ap[:, :], start=True, stop=True),
                               reads=[acc], writes=[prs])
                            op("act", lambda: nc.scalar.activation(rinv.ap[:, :], prs.ap[:, :], AF.Ln), reads=[prs], writes=[rinv])
                            op("act", lambda: nc.scalar.activation(rinv.ap[:, :], rinv.ap[:, :], AF.Exp, scale=-1.0),
                               reads=[rinv], writes=[rinv])
                            op("pool", lambda: nc.gpsimd.tensor_tensor(out=rgt.ap[:, :], in0=rinv.ap[:, :], in1=ga.ap[:, h, :], op=ALU.mult),
                               reads=[rinv, ga], writes=[rgt])
                            op("dve", lambda: nc.vector.tensor_tensor(out=ymc.ap[:, h, :], in0=po.ap[:, :], in1=rgt.ap[:, :], op=ALU.mult),
                               reads=[po, rgt], writes=[ymc])
                            if h == 3:
                                cx.dma("pool", yT_v[:, 0:4, c * CH:(c + 1) * CH], ymc.ap[:], ymc.name, reads=[ymc], writes=[yT_c[c]])

                        pending = []
                        yield ("chunk", 0)
                        s_mm(0)
                        s_mm(1)
                        for i0 in range(0, n, 2):
                            c, h, kt0 = tiles[i0]
                            if h == 0 and kt0 == 0 and c > 0:
                                yield ("chunk", c)
                            g = c * 4 + h
                            po, acc = PS[6 + (g % 2)], accds[g % 2]
                            pbs = [ptb[i0 % 4], ptb[(i0 + 1) % 4]]
                            for u in range(2):
                                pt = PS[2 + ((i0 + u) % 4)]
                                op("act", lambda: nc.scalar.activation(pbs[u].ap[:, :], pt.ap[:, :], AF.Exp, scale=SCALE),
                                   reads=[pt], writes=[pbs[u]])
                            for u in range(2):
                                if i0 + 2 + u < n:
                                    s_mm(i0 + 2 + u)
                            for u in range(2):
                                kt = kt0 + u
                                kc, kj = kt // 4, kt % 4
                                rd = [Vt[kc], pbs[0], pbs[1]] if u == 0 else [Vt[kc], pbs[1]]
                                op("pe", lambda: nc.tensor.matmul(
                                    po.ap[:, :], Vt[kc].ap[:, kj, h * 128:(h + 1) * 128], pbs[u].ap[:, :],
                                    start=(kt == 0), stop=(kt == 31)), reads=rd, writes=[po], inc=True)
                            op("dve", lambda: nc.vector.tensor_tensor(out=psum2.ap[:, :], in0=pbs[0].ap[:, :], in1=pbs[1].ap[:, :], op=ALU.add),
                               reads=[pbs[0], pbs[1]], writes=[psum2])
                            if kt0 == 0:
                                op("dve", lambda: nc.vector.tensor_copy(acc.ap[:, :], psum2.ap[:, :]), reads=[psum2], writes=[acc])
                            else:
                                op("dve", lambda: nc.vector.tensor_tensor(out=acc.ap[:, :], in0=acc.ap[:, :], in1=psum2.ap[:, :], op=ALU.add),
                                   reads=[psum2, acc], writes=[acc])
                            if kt0 + 1 == 31:
                                pending.append((i0 + 4, c, h))
                            while pending and pending[0][0] <= i0:
                                _, c_, h_ = pending.pop(0)
                                epilogue(c_, h_)
                            yield None
                            yield None
                        for _, c_, h_ in pending:
                            epilogue(c_, h_)
                        yield None

                    for _ in qprep(0):
                        pass
                    gens = [attn_all()]
                    while gens:
                        nx = []
                        for g_ in gens:
                            try:
                                tag = next(g_)
                                nx.append(g_)
                                if tag is not None and tag[0] == "chunk" and tag[1] + 1 < NCH:
                                    nx.append(qprep(tag[1] + 1))
                            except StopIteration:
                                pass
                        gens = nx
                    cx.barrier()
            cx.barrier()
            if stop_after == "p1b":
                break
            with contextlib.ExitStack() as p2:
                gqT = [T(sb(p2, "gqT%d" % c, [128, 2, CH], BF16), "gqT%d" % c) for c in range(NCH)]
                gkT = [T(sb(p2, "gkT%d" % c, [128, 2, CH], BF16), "gkT%d" % c) for c in range(NCH)]
                gkt = [T(sb(p2, "gkt%d" % c, [128, 4, 256], BF16), "gkt%d" % c) for c in range(NCH)]
                gvt = [T(sb(p2, "gvt%d" % c, [128, 4, 512], BF16), "gvt%d" % c) for c in range(NCH)]
                ggt = [T(sb(p2, "ggt%d" % c, [128, 4, 512], BF16), "ggt%d" % c) for c in range(NCH)]
                lrT = [T(sb(p2, "lr%d" % c, [48, CH], BF16), "lr%d" % c) for c in range(NCH)]
                with contextlib.ExitStack() as p2a:
                    stg = [T(sb(p2a, "stgG%d" % i, [128, 1824], F32), "stgG%d" % i) for i in range(2)]
                    wg = T(sb(p2a, "wg", [128, 8, 1824], BF16), "wg")
                    xnb = [T(sb(p2a, "xng%d" % i, [128, 8, CH], BF16), "xng%d" % i) for i in range(2)]
                    load_weight(stg, wg_d[l], 8, 1824, wg, lambda k: gmix.ap[:, l, k:k + 1])
                    for c in range(NCH):
                        xn = xnb[c % 2]
                        cx.dma("sp", xn.ap[:], xnT_v[:, :, c * CH:(c + 1) * CH], xn.name, reads=[xnT_c[c]], writes=[xn])
                        for which, dst in ((0, gqT[c]), (1, gkT[c])):
                            for pr in range(2):
                                pt = PS[(which * 2 + pr) % 4]
                                col = which * 256 + pr * 128
                                for k in range(8):
                                    op("pe", lambda k=k, pt=pt, col=col: nc.tensor.matmul(
                                        pt.ap[:, :], wg.ap[:, k, col:col + 128], xn.ap[:, k, :],
                                        start=(k == 0), stop=(k == 7)), reads=[wg, xn], writes=[pt], inc=(k == 7))
                                if which == 0:
                                    op("act", lambda pt=pt, dst=dst, pr=pr: nc.scalar.mul(dst.ap[:, pr, :], pt.ap[:, :], 0.125),
                                       reads=[pt], writes=[dst])
                                else:
                                    op("dve", lambda pt=pt, dst=dst, pr=pr: nc.vector.tensor_copy(dst.ap[:, pr, :], pt.ap[:, :]),
                                       reads=[pt], writes=[dst])
                        for dr in range(2):
                            pt = PS[4 + dr]
                            col = 512 + dr * 16
                            for k in range(8):
                                op("pe", lambda k=k, pt=pt, col=col: nc.tensor.matmul(
                                    pt.ap[32 * dr:32 * dr + 16, :], wg.ap[:, k, col:col + 16], xn.ap[:, k, :],
                                    start=(k == 0), stop=(k == 7)), reads=[wg, xn], writes=[pt], inc=(k == 7))
                            op("dve", lambda pt=pt, dr=dr: nc.vector.tensor_copy(lrT[c].ap[32 * dr:32 * dr + 16, :],
                                                                                 pt.ap[32 * dr:32 * dr + 16, :]),
                               reads=[pt], writes=[lrT[c]])
                        for j in range(4):
                            tk = slice(j * 128, (j + 1) * 128)
                            pk_, pv_, pg_ = PS[6], PS[7], PS[j % 4]
                            for k in range(8):
                                op("pe", lambda k=k: nc.tensor.matmul(
                                    pk_.ap[:, 0:256], xn.ap[:, k, tk], wg.ap[:, k, 544:800],
                                    start=(k == 0), stop=(k == 7)), reads=[wg, xn], writes=[pk_], inc=(k == 7))
                            op("dve", lambda j=j: nc.vector.tensor_copy(gkt[c].ap[:, j, :], pk_.ap[:, 0:256]),
                               reads=[pk_], writes=[gkt[c]])
                            for k in range(8):
                                op("pe", lambda k=k: nc.tensor.matmul(
                                    pv_.ap[:, :], xn.ap[:, k, tk], wg.ap[:, k, 800:1312],
                                    start=(k == 0), stop=(k == 7)), reads=[wg, xn], writes=[pv_], inc=(k == 7))
                            op("act", lambda j=j: nc.scalar.copy(gvt[c].ap[:, j, :], pv_.ap[:, :]),
                               reads=[pv_], writes=[gvt[c]])
                            for k in range(8):
                                op("pe", lambda k=k, pg_=pg_: nc.tensor.matmul(
                                    pg_.ap[:, :], xn.ap[:, k, tk], wg.ap[:, k, 1312:1824],
                                    start=(k == 0), stop=(k == 7)), reads=[wg, xn], writes=[pg_], inc=(k == 7))
                            op("act", lambda j=j, pg_=pg_: nc.scalar.activation(ggt[c].ap[:, j, :], pg_.ap[:, :], AF.Silu),
                               reads=[pg_], writes=[ggt[c]])
                    cx.barrier()
                if stop_after == "p2a":
                    break
                with contextlib.ExitStack() as p2b:
                    opart = [T(sb(p2b, "opart%d" % t, [128, 512], BF16), "opart%d" % t) for t in range(32)]
                    D2 = range(2)
                    S2 = range(2)
                    ez = [T(sb(p2b, "ez%d" % d, [128, 256], F32), "ez%d" % d) for d in D2]
                    spt = ez
                    Eq = [T(sb(p2b, "Eq%d" % d, [128, 2, 128], F32), "Eq%d" % d) for d in D2]
                    Ek = [T(sb(p2b, "Ek%d" % d, [128, 2, 128], F32), "Ek%d" % d) for d in D2]
                    Eqi = [T(sb(p2b, "Eqi%d" % d, [128, 2, 128], F32), "Eqi%d" % d) for d in D2]
                    Es = [T(sb(p2b, "Es%d" % d, [128, 256], F32), "Es%d" % d) for d in D2]
                    dec = [[T(sb(p2b, "dec%d_%d" % (d, z), [128, 2, 2], F32), "dec%d_%d" % (d, z)) for z in S2] for d in D2]
                    qin = [[T(sb(p2b, "qin%d_%d" % (d, z), [128, 2, 128], BF16), "qin%d_%d" % (d, z)) for z in S2] for d in D2]
                    kin = [[T(sb(p2b, "kin%d_%d" % (d, z), [128, 4, 128], BF16), "kin%d_%d" % (d, z)) for z in S2] for d in D2]
                    qint = [[T(sb(p2b, "qint%d_%d" % (d, z), [128, 4, 128], BF16), "qint%d_%d" % (d, z)) for z in S2] for d in D2]
                    kst = [[[T(sb(p2b, "kst%d_%d_%d" % (d, z, cc), [128, 256], BF16), "kst%d_%d_%d" % (d, z, cc))
                             for cc in range(2)] for z in S2] for d in D2]
                    Am = [[T(sb(p2b, "Am%d_%d" % (d, cc), [128, 256], BF16), "Am%d_%d" % (d, cc)) for cc in range(2)] for d in D2]
                    Sf = [T(sb(p2b, "Sf%d" % d, [128, 2, 128], F32), "Sf%d" % d) for d in D2]
                    Sb = [T(sb(p2b, "Sb%d" % d, [128, 2, 128], BF16), "Sb%d" % d) for d in D2]
                    hm = T(sb(p2b, "hm", [128, 2], F32), "hm")
                    ot = T(sb(p2b, "ot", [128, 512], F32), "ot")
                    osq = T(sb(p2b, "osq", [128, 512], F32), "osq")
                    ss = T(sb(p2b, "ssg", [128, 4], F32), "ssg")
                    ssr = T(sb(p2b, "ssr", [128, 4], F32), "ssr")
                    rg = T(sb(p2b, "rg", [128, 4], F32), "rg")
                    yt = T(sb(p2b, "yt", [128, 512], BF16), "yt")
                    ygT = T(sb(p2b, "ygT", [128, 4, 128], BF16), "ygT")
                    op("pool", lambda: nc.gpsimd.memset(hm.ap[:], 0.0), writes=[hm])
                    op("pool", lambda: nc.gpsimd.memset(hm.ap[0:64, 0:1], 1.0), writes=[hm])
                    op("pool", lambda: nc.gpsimd.memset(hm.ap[64:128, 1:2], 1.0), writes=[hm])
                    for d in D2:
                        op("pool", lambda: nc.gpsimd.memset(Sf[d].ap[:], 0.0), writes=[Sf[d]])
                        op("pool", lambda: nc.gpsimd.memset(Sb[d].ap[:], 0.0), writes=[Sb[d]])
                        for cc in range(2):
                            op("pool", lambda: nc.gpsimd.memset(Am[d][cc].ap[:], 0.0), writes=[Am[d][cc]])
                            for z in S2:
                                op("pool", lambda: nc.gpsimd.memset(kst[d][z][cc].ap[:], 0.0), writes=[kst[d][z][cc]])
                    PREP = [PS[0], PS[1]]
                    PA = [PS[2], PS[5]]
                    PO = [PS[3], PS[4]]
                    PKS = [PS[6], PS[7]]

                    def prep(d, t):
                        z = t % 2
                        c, j = t // 4, t % 4
                        tk = slice(j * 128, (j + 1) * 128)
                        lr_rows = slice(32 * d, 32 * d + 16)
                        lastcol = 63 if d == 0 else 0
                        pp = PREP[d]
                        bgx = bgh[d]
                        op("pe", lambda: nc.tensor.matmul(pp.ap[:, 0:256], lrT[c].ap[lr_rows, tk], wg_b.ap[lr_rows, l, :],
                                                          start=True, stop=False),
                           reads=[lrT[c], wg_b], writes=[pp], inc=False)
                        op("pe", lambda: nc.tensor.matmul(pp.ap[:, 0:256], ones_b.ap[0:1, :], bgx.ap[0:1, 0, l * 256:(l + 1) * 256],
                                                          start=False, stop=False),
                           reads=[bgx], writes=[pp], inc=False)
                        op("pe", lambda: nc.tensor.matmul(pp.ap[:, 0:256], ones_b.ap[0:1, :], bgx.ap[0:1, 1, l * 256:(l + 1) * 256],
                                                          start=False, stop=True),
                           reads=[bgx], writes=[pp])
                        yield
                        op("act", lambda: nc.scalar.activation(ez[d].ap[:, :], pp.ap[:, 0:256], AF.Exp, scale=-1.0),
                           reads=[pp], writes=[ez[d]])
                        yield
                        op("act", lambda: nc.scalar.activation(spt[d].ap[:, :], ez[d].ap[:, :], AF.Ln, bias=1.0),
                           reads=[ez[d]], writes=[spt[d]])
                        yield
                        for w in range(2):
                            for pr in range(2):
                                op("pe", lambda: nc.tensor.matmul(
                                    pp.ap[:, w * 256 + pr * 128:w * 256 + (pr + 1) * 128], spt[d].ap[:, pr * 128:(pr + 1) * 128],
                                    tri.ap[:, d * 3 + w, :], start=True, stop=True),
                                   reads=[spt[d], tri], writes=[pp], inc=(w == 1 and pr == 1))
                        yield
                        op("pe", lambda: nc.tensor.matmul(PA[d].ap[:, 256:512], tri.ap[:, d * 3 + 2, :], spt[d].ap[:, :],
                                                          start=True, stop=True),
                           reads=[spt[d], tri], writes=[PA[d]])
                        yield
                        op("act", lambda: nc.scalar.activation(
                            Eq[d].ap[:], pp.ap[:, 256:512].rearrange("p (a i) -> p a i", a=2), AF.Exp),
                           reads=[pp], writes=[Eq[d]])
                        yield
                        op("act", lambda: nc.scalar.activation(
                            Ek[d].ap[:], pp.ap[:, 256:512].rearrange("p (a i) -> p a i", a=2), AF.Exp, scale=-1.0),
                           reads=[pp], writes=[Ek[d]])
                        yield
                        op("act", lambda: nc.scalar.activation(
                            Eqi[d].ap[:], pp.ap[:, 0:256].rearrange("p (a i) -> p a i", a=2), AF.Exp),
                           reads=[pp], writes=[Eqi[d]])
                        yield
                        op("act", lambda: nc.scalar.activation(
                            dec[d][z].ap[:].rearrange("p a c -> p (a c)").rearrange("p (q o) -> p q o", o=1),
                            pp.ap[:, 0:256].rearrange("p (q i) -> p q i", i=64)[:, :, lastcol:lastcol + 1], AF.Exp),
                           reads=[pp], writes=[dec[d][z]])
                        yield
                        op("act", lambda: nc.scalar.activation(Es[d].ap[:, :], PA[d].ap[:, 256:512], AF.Exp),
                           reads=[PA[d]], writes=[Es[d]])
                        yield
                        op("dve", lambda: nc.vector.tensor_tensor(out=qin[d][z].ap[:], in0=gqT[c].ap[:, :, tk], in1=Eq[d].ap[:], op=ALU.mult),
                           reads=[gqT[c], Eq[d]], writes=[qin[d][z]])
                        yield
                        for h in range(4):
                            pr, hh = h // 2, h % 2
                            op("dve", lambda: nc.vector.scalar_tensor_tensor(
                                kin[d][z].ap[:, h, :], gkT[c].ap[:, pr, tk], hm.ap[:, hh:hh + 1], Ek[d].ap[:, pr, :],
                                ALU.mult, ALU.mult), reads=[gkT[c], Ek[d], hm], writes=[kin[d][z]])
                            yield
                            op("dve", lambda: nc.vector.scalar_tensor_tensor(
                                qint[d][z].ap[:, h, :], gqT[c].ap[:, pr, tk], hm.ap[:, hh:hh + 1], Eqi[d].ap[:, pr, :],
                                ALU.mult, ALU.mult), reads=[gqT[c], Eqi[d], hm], writes=[qint[d][z]])
                            yield
                        for cc in range(2):
                            rws = slice(cc * 64, (cc + 1) * 64)
                            op("pool", lambda: nc.gpsimd.tensor_tensor(
                                out=kst[d][z][cc].ap[rws, :], in0=gkt[c].ap[rws, j, :], in1=Es[d].ap[rws, :], op=ALU.mult),
                               reads=[gkt[c], Es[d]], writes=[kst[d][z][cc]])
                            yield

                    def scan(d, t, s_):
                        z = t % 2
                        c, j = t // 4, t % 4
                        po, pa = PO[d], PA[d]
                        PK = PKS[d]
                        PX = PKS[d]
                        for cc in ((0, 1) if d == 0 else (1, 0)):
                            rows = slice(cc * 64, (cc + 1) * 64)
                            cols = slice(cc * 64, (cc + 1) * 64)
                            for h in range(4):
                                pr = h // 2
                                op("pe", lambda: nc.tensor.matmul(
                                    pa.ap[rows, h * 64:(h + 1) * 64], kin[d][z].ap[:, h, cols], qin[d][z].ap[:, pr, cols],
                                    start=True, stop=True), reads=[kin[d][z], qin[d][z]], writes=[pa], inc=(h == 3))
                            yield
                            op("dve", lambda: nc.vector.tensor_tensor(out=Am[d][cc].ap[rows, :], in0=pa.ap[rows, 0:256],
                                                                      in1=mask.ap[rows, d, :], op=ALU.mult),
                               reads=[pa, mask], writes=[Am[d][cc]])
                            yield
                            for h in range(4):
                                pr = h // 2
                                op("pe", lambda: nc.tensor.matmul(
                                    po.ap[rows, h * 128:(h + 1) * 128], Am[d][cc].ap[:, h * 64:(h + 1) * 64],
                                    gvt[c].ap[:, j, h * 128:(h + 1) * 128], start=True, stop=False),
                                   reads=[Am[d][cc], gvt[c]], writes=[po], inc=False)
                                op("pe", lambda: nc.tensor.matmul(
                                    po.ap[rows, h * 128:(h + 1) * 128], qint[d][z].ap[:, h, cols],
                                    Sb[d].ap[:, pr, :], start=False, stop=True),
                                   reads=[qint[d][z], Sb[d]], writes=[po], inc=(h == 3))
                            yield
                            for h in range(4):
                                pr, hh = h // 2, h % 2
                                hr = slice(hh * 64, (hh + 1) * 64)
                                op("pe", lambda: nc.tensor.matmul(
                                    PK.ap[hr, pr * 128:(pr + 1) * 128], kst[d][z][cc].ap[:, h * 64:(h + 1) * 64],
                                    gvt[c].ap[:, j, h * 128:(h + 1) * 128], start=True, stop=True),
                                   reads=[kst[d][z][cc], gvt[c]], writes=[PK], inc=(h == 3))
                            yield
                            for pr in range(2):
                                op("dve", lambda: nc.vector.scalar_tensor_tensor(
                                    Sf[d].ap[:, pr, :], Sf[d].ap[:, pr, :], dec[d][z].ap[:, pr, cc:cc + 1],
                                    PK.ap[:, pr * 128:(pr + 1) * 128], ALU.mult, ALU.add),
                                   reads=[Sf[d], dec[d][z], PK], writes=[Sf[d]])
                            yield
                            op("act", lambda: nc.scalar.copy(Sb[d].ap[:], Sf[d].ap[:]), reads=[Sf[d]], writes=[Sb[d]])
                            yield
                        if s_ < 16:
                            op("act", lambda: nc.scalar.copy(opart[t].ap[:, :], po.ap[:, :]), reads=[po], writes=[opart[t]])
                            yield
                            return
                        op("dve", lambda: nc.vector.tensor_tensor(out=ot.ap[:, :], in0=po.ap[:, :], in1=opart[t].ap[:, :], op=ALU.add),
                           reads=[po, opart[t]], writes=[ot])
                        op("pool", lambda: nc.gpsimd.tensor_tensor(out=osq.ap[:, :], in0=ot.ap[:, :], in1=ot.ap[:, :], op=ALU.mult),
                           reads=[ot], writes=[osq])
                        op("dve", lambda: nc.vector.reduce_sum(ss.ap[:, :], osq.ap[:, :].rearrange("p (h v) -> p h v", h=4), axis=AX.X),
                           reads=[osq], writes=[ss])
                        op("act", lambda: nc.scalar.activation(ssr.ap[:, :], ss.ap[:, :], AF.Ln, bias=epsb.ap[:, :], scale=1.0 / 128),
                           reads=[ss], writes=[ssr])
                        op("act", lambda: nc.scalar.activation(rg.ap[:, :], ssr.ap[:, :], AF.Exp, scale=-0.5), reads=[ssr], writes=[rg])
                        for h in range(4):
                            hs = slice(h * 128, (h + 1) * 128)
                            op("dve", lambda: nc.vector.scalar_tensor_tensor(
                                yt.ap[:, hs], ot.ap[:, hs], rg.ap[:, h:h + 1], ggt[c].ap[:, j, hs], ALU.mult, ALU.mult),
                               reads=[ot, rg, ggt[c]], writes=[yt])
                            pview = PX.ap[:, :].bitcast(BF16)
                        for h in range(4):
                            op("pe", lambda: nc.tensor.transpose(pview[:, h * 128:(h + 1) * 128],
                                                                 yt.ap[:, h * 128:(h + 1) * 128], ident_b.ap[:, :]),
                               reads=[yt], writes=[PX], inc=(h == 3))
                        op("act", lambda: nc.scalar.copy(ygT.ap[:].rearrange("p h t -> p (h t)"), pview[:, 0:512]),
                           reads=[PX], writes=[ygT])
                        cx.dma("sp", yT_v[:, 4:8, t * 128:(t + 1) * 128], ygT.ap[:], ygT.name, reads=[ygT], writes=[yT_c[c]])
                        yield

                    def run_zipped(gens, weights=None):
                        gens = list(gens)
                        weights = list(weights) if weights else [1] * len(gens)
                        while gens:
                            nxt, nw = [], []
                            for g, w in zip(gens, weights):
                                alive = True
                                for _ in range(w):
                                    try:
                                        next(g)
                                    except StopIteration:
                                        alive = False
                                        break
                                if alive:
                                    nxt.append(g)
                                    nw.append(w)
                            gens, weights = nxt, nw

                    run_zipped([prep(0, 0), prep(1, 31)])
                    for s_ in range(32):
                        gens = [scan(0, s_, s_), scan(1, 31 - s_, s_)]
                        wts = [1, 1]
                        if s_ + 1 < 32:
                            gens += [prep(0, s_ + 1), prep(1, 30 - s_)]
                            wts += [1, 1]
                        run_zipped(gens, wts)
                    cx.barrier()
            cx.barrier()
            if stop_after == "p2b":
                break
            with contextlib.ExitStack() as p3:
                stg = [T(sb(p3, "stgO%d" % i, [128, 1024], F32), "stgO%d" % i) for i in range(2)]
                wout = T(sb(p3, "wout", [128, 8, 1024], BF16), "wout")
                wpg = T(sb(p3, "wpg", [128, 8, 1024], BF16), "wpg")
                wpp = T(sb(p3, "wpp", [128, 2, 1024], BF16), "wpp")
                Z2 = range(2)
                yb = [T(sb(p3, "yb%d" % i, [128, 8, CH], BF16), "yb%d" % i) for i in Z2]
                hb = [T(sb(p3, "hb3_%d" % i, [128, 8, CH], F32), "hb3_%d" % i) for i in range(3)]
                pin = [[T(sb(p3, "pin%d_%d" % (z, i), [128, 256], F32), "pin%d_%d" % (z, i)) for i in range(4)] for z in Z2]
                pTb = [T(sb(p3, "pT%d" % z, [128, 2, CH], BF16), "pT%d" % z) for z in Z2]
                sqbs = [T(sb(p3, "sqb3_%d" % z, [128, 8, CH], BF16), "sqb3_%d" % z) for z in Z2]
                tmpbs = [T(sb(p3, "tmpb3_%d" % z, [128, CH], F32), "tmpb3_%d" % z) for z in Z2]
                rsbs = [T(sb(p3, "rsb3_%d" % z, [128, CH], F32), "rsb3_%d" % z) for z in Z2]
                hns = [T(sb(p3, "hn%d" % z, [128, 8, CH], BF16), "hn%d" % z) for z in Z2]
                sigs = [[T(sb(p3, "sig%d_%d" % (z, i), [128, CH], F32), "sig%d_%d" % (z, i)) for i in range(2)] for z in Z2]
                if not last:
                    xnbs = [T(sb(p3, "xno%d" % i, [128, 8, CH], BF16), "xno%d" % i) for i in Z2]
                else:
                    otiles = [[T(sb(p3, "otile%d_%d" % (z, i), [128, D], F32), "otile%d_%d" % (z, i)) for i in range(2)] for z in Z2]
                load_weight(stg, wout_d[l], 8, 1024, wout, lambda k: (None if k < 4 else goutn.ap[:, l:l + 1]))
                load_weight(stg, wpg_d[l], 8, 1024, wpg, lambda k: gple.ap[:, l, k:k + 1])
                load_weight(stg, wpp_d[l], 2, 1024, wpp, lambda k: None)

                def rms_gen(h_, sq_t, tmp_t, rs_t, ps_t):
                    op("act", lambda: nc.scalar.activation(sq_t.ap[:, 0:4, :], h_.ap[:, 0:4, :], AF.Square),
                       reads=[h_], writes=[sq_t])
                    yield
                    op("act", lambda: nc.scalar.activation(sq_t.ap[:, 4:8, :], h_.ap[:, 4:8, :], AF.Square),
                       reads=[h_], writes=[sq_t])
                    for _ in range(5):
                        yield
                    for k in range(8):
                        op("pe", lambda: nc.tensor.matmul(ps_t.ap[:, :], ones_b.ap[:, :], sq_t.ap[:, k, :],
                                                          start=(k == 0), stop=(k == 7)),
                           reads=[sq_t], writes=[ps_t], inc=(k == 7))
                    yield
                    op("act", lambda: nc.scalar.activation(tmp_t.ap[:, :], ps_t.ap[:, :], AF.Ln,
                                                            bias=epsb.ap[:, :], scale=1.0 / D),
                       reads=[ps_t], writes=[tmp_t])
                    yield
                    op("act", lambda: nc.scalar.activation(rs_t.ap[:, :], tmp_t.ap[:, :], AF.Exp, scale=-0.5),
                       reads=[tmp_t], writes=[rs_t])
                    yield

                def chunk_gen(c):
                    z = c % 2
                    B0, B1, B2, B3 = PS[4 * z], PS[4 * z + 1], PS[4 * z + 2], PS[4 * z + 3]
                    y_, h_, pT_, hn = yb[z], hb[c % 3], pTb[z], hns[z]
                    sqb, tmpb, rsb = sqbs[z], tmpbs[z], rsbs[z]

                    def load_h(cc):
                        cx.dma("sp", hb[cc % 3].ap[:], hT_v[:, :, cc * CH:(cc + 1) * CH], hb[cc % 3].name,
                               reads=[hT_c[cc]], writes=[hb[cc % 3]])

                    def loads(cc):
                        cx.dma("sp", yb[cc % 2].ap[:], yT_v[:, :, cc * CH:(cc + 1) * CH], yb[cc % 2].name,
                               reads=[yT_c[cc]], writes=[yb[cc % 2]])
                        for jt in range(4):
                            tl = cc * 4 + jt
                            pt_ = pin[cc % 2][jt]
                            cx.dma("sp", pt_.ap[:], p_d[l, tl * 128:(tl + 1) * 128, :], pt_.name, writes=[pt_])

                    if c < 2:
                        loads(c)
                    if c == 0:
                        load_h(0)
                        load_h(1)
                        load_h(2)
                    yield
                    for m in range(8):
                        pt = B0 if m % 2 == 0 else B1
                        for k in range(8):
                            op("pe", lambda: nc.tensor.matmul(
                                pt.ap[:, :], wout.ap[:, k, m * 128:(m + 1) * 128], y_.ap[:, k, :],
                                start=(k == 0), stop=(k == 7)), reads=[wout, y_], writes=[pt], inc=(k == 7))
                        yield
                        op("dve", lambda: nc.vector.tensor_tensor(out=h_.ap[:, m, :], in0=pt.ap[:, :],
                                                                  in1=h_.ap[:, m, :], op=ALU.add),
                           reads=[pt, h_], writes=[h_])
                        yield
                    for fc in range(2):
                        for jt in range(4):
                            op("pe", lambda: nc.tensor.transpose(
                                B2.ap[:, jt * 128:(jt + 1) * 128], pin[z][jt].ap[:, fc * 128:(fc + 1) * 128], ident_f.ap[:, :]),
                               reads=[pin[z][jt]], writes=[B2], inc=(jt == 3))
                        yield
                        op("act", lambda: nc.scalar.copy(pT_.ap[:, fc, :], B2.ap[:, :]), reads=[B2], writes=[pT_])
                        yield
                    if c + 2 < NCH:
                        loads(c + 2)
                    yield "half"
                    yield from rms_gen(h_, sqb, tmpb, rsb, B3)
                    for k in range(8):
                        op("dve", lambda: nc.vector.tensor_tensor(out=hn.ap[:, k, :], in0=h_.ap[:, k, :],
                                                                  in1=rsb.ap[:, :], op=ALU.mult),
                           reads=[h_, rsb], writes=[hn])
                        yield
                    for m in range(8):
                        pg_ = B0 if m % 2 == 0 else B1
                        pp_ = B2
                        sg = sigs[z][m % 2]
                        for k in range(8):
                            op("pe", lambda: nc.tensor.matmul(
                                pg_.ap[:, :], wpg.ap[:, k, m * 128:(m + 1) * 128], hn.ap[:, k, :],
                                start=(k == 0), stop=(k == 7)), reads=[wpg, hn], writes=[pg_], inc=(k == 7))
                        yield
                        op("act", lambda: nc.scalar.activation(sg.ap[:, :], pg_.ap[:, :], AF.Sigmoid),
                           reads=[pg_], writes=[sg])
                        yield
                        for fc in range(2):
                            op("pe", lambda: nc.tensor.matmul(
                                pp_.ap[:, :], wpp.ap[:, fc, m * 128:(m + 1) * 128], pT_.ap[:, fc, :],
                                start=(fc == 0), stop=(fc == 1)), reads=[wpp, pT_], writes=[pp_], inc=(fc == 1))
                        yield
                        op("dve", lambda: nc.vector.tensor_tensor(out=sg.ap[:, :], in0=pp_.ap[:, :],
                                                                  in1=sg.ap[:, :], op=ALU.mult),
                           reads=[pp_, sg], writes=[sg])
                        yield
                        op("dve", lambda: nc.vector.tensor_tensor(out=h_.ap[:, m, :], in0=h_.ap[:, m, :],
                                                                  in1=sg.ap[:, :], op=ALU.add),
                           reads=[h_, sg], writes=[h_])
                        yield
                    if not last:
                        xn_t = xnbs[z]
                        cx.dma("pool", hT_v[:, :, c * CH:(c + 1) * CH], h_.ap[:], h_.name, reads=[h_], writes=[hT_c[c]])
                        yield from rms_gen(h_, sqb, tmpb, rsb, B3)
                        for k in range(8):
                            op("dve", lambda: nc.vector.tensor_tensor(out=xn_t.ap[:, k, :], in0=h_.ap[:, k, :],
                                                                      in1=rsb.ap[:, :], op=ALU.mult),
                               reads=[h_, rsb], writes=[xn_t])
                            yield
                        cx.dma("pool", xnT_v[:, :, c * CH:(c + 1) * CH], xn_t.ap[:], xn_t.name, reads=[xn_t], writes=[xnT_c[c]])
                        yield
                    else:
                        yield from rms_gen(h_, sqb, tmpb, rsb, B3)
                        for k in range(8):
                            op("dve", lambda: nc.vector.scalar_tensor_tensor(
                                h_.ap[:, k, :], h_.ap[:, k, :], gfin.ap[:, k:k + 1], rsb.ap[:, :], ALU.mult, ALU.mult),
                               reads=[h_, gfin, rsb], writes=[h_])
                            yield
                        banks = [B0, B1, B2, B3]
                        for jt in range(4):
                            tl = c * 4 + jt
                            ot_ = otiles[z][jt % 2]
                            for half in range(2):
                                pt = banks[(jt * 2 + half) % 4]
                                for q4 in range(4):
                                    k = half * 4 + q4
                                    op("pe", lambda: nc.tensor.transpose(
                                        pt.ap[:, q4 * 128:(q4 + 1) * 128], h_.ap[:, k, jt * 128:(jt + 1) * 128], ident_f.ap[:, :]),
                                       reads=[h_], writes=[pt], inc=(q4 == 3))
                                yield
                                if half == 0:
                                    op("act", lambda: nc.scalar.copy(ot_.ap[:, 0:512], pt.ap[:, :]), reads=[pt], writes=[ot_])
                                else:
                                    op("dve", lambda: nc.vector.tensor_copy(ot_.ap[:, 512:1024], pt.ap[:, :]), reads=[pt], writes=[ot_])
                                yield
                            cx.dma("pool", out_d[tl * 128:(tl + 1) * 128, :], ot_.ap[:], ot_.name, reads=[ot_])
                            yield

                    if c + 3 < NCH:
                        load_h(c + 3)
                        yield

                active = {0: chunk_gen(0)}
                nxt_c = 1
                want_start = False
                while active:
                    for c_ in sorted(active):
                        try:
                            tag = next(active[c_])
                            if tag == "half":
                                want_start = True
                        except StopIteration:
                            del active[c_]
                    if want_start and nxt_c < NCH and (nxt_c - 2) not in active:
                        active[nxt_c] = chunk_gen(nxt_c)
                        nxt_c += 1
                        want_start = False
                cx.barrier()

        cx.barrier()
    return nc, cx.n_inst


def _kchunk(w):
    K, C = w.shape
    return np.ascontiguousarray(w.reshape(K // 128, 128, C).transpose(1, 0, 2))


def _rot_cols(w):
    return np.concatenate([w[:, 32:64], w[:, 0:32]], axis=1)


def prep_shared(inp):
    f = lambda a: np.asarray(a, dtype=np.float32)
    w_in = f(inp["w_in"])
    w_uq = f(inp["w_uq"])
    w_ukv = f(inp["w_ukv"])
    sh = {}
    o = np.cumsum([0, 384, 256, 64, 512, 256, 256, 512, 16, 16, 512])
    wa, wb, wg, wuq, wukv = [], [], [], [], []
    for l in range(L):
        W = w_in[l]
        cq, ckv, kr, gate_a, gq_, gk_, gv_, lrf, lrb, gate_g = [W[:, o[i]:o[i + 1]] for i in range(10)]
        krr = _rot_cols(kr)
        wa.append(_kchunk(np.concatenate([ckv, kr, kr, krr, krr], axis=1)))
        wb.append(_kchunk(np.concatenate([cq, gate_a], axis=1)))
        wg.append(_kchunk(np.concatenate([gq_, gk_, lrf, lrb, gk_, gv_, gate_g], axis=1)))
        U = w_uq[l].reshape(384, 4, 192)
        nope = U[:, :, :128].reshape(384, 512)
        rope = U[:, :, 128:]
        rope_cat = rope.reshape(384, 256)
        rot_cat = np.concatenate([_rot_cols(rope[:, h, :]) for h in range(4)], axis=1)
        wuq.append(_kchunk(np.concatenate([nope, rope_cat, rot_cat], axis=1)))
        KV = w_ukv[l].reshape(256, 4, 256)
        kn = KV[:, :, :128].reshape(256, 512)
        vv = KV[:, :, 128:].reshape(256, 512)
        wukv.append(_kchunk(np.concatenate([kn, vv], axis=1)))
    sh["wa"] = np.stack(wa)
    sh["wb"] = np.stack(wb)
    sh["wg"] = np.stack(wg)
    sh["wuq"] = np.stack(wuq)
    sh["wukv"] = np.stack(wukv)
    sh["wout"] = np.stack([_kchunk(f(inp["w_out"])[l]) for l in range(L)])
    sh["wpg"] = np.stack([_kchunk(f(inp["w_ple_gate"])[l]) for l in range(L)])
    sh["wpp"] = np.stack([_kchunk(f(inp["w_ple_proj"])[l]) for l in range(L)])

    def pk(g, nk):
        return np.ascontiguousarray(g.reshape(L, nk, 128).transpose(2, 0, 1))
    sh["gmix"] = pk(f(inp["ln_mix"]), 8)
    sh["gq"] = pk(f(inp["mla_q_norm"]), 3)
    sh["gkv"] = pk(f(inp["mla_kv_norm"]), 2)
    sh["goutn"] = np.ascontiguousarray(f(inp["gla_out_norm"]).T)
    sh["gple"] = pk(f(inp["ple_norm"]), 8)
    sh["gfin"] = np.ascontiguousarray(f(inp["final_norm"]).reshape(8, 128).T)
    sh["wgf"] = np.ascontiguousarray(f(inp["gla_w_gate_fwd"]).transpose(1, 0, 2))
    sh["wgb"] = np.ascontiguousarray(f(inp["gla_w_gate_bwd"]).transpose(1, 0, 2))
    sh["bgf"] = np.ascontiguousarray(f(inp["gla_b_gate_fwd"]).reshape(1, L * 256))
    sh["bgb"] = np.ascontiguousarray(f(inp["gla_b_gate_bwd"]).reshape(1, L * 256))
    sh["ident"] = np.eye(128, dtype=np.float32)
    j = np.arange(128)[:, None]
    i = np.arange(128)[None, :]
    same = (j // 64) == (i // 64)
    v = np.float32(-1.0 / 16.0)
    tri = np.zeros((128, 6, 128), np.float32)
    Tf = np.where(same & (j <= i), v, 0).astype(np.float32)
    Tb = np.where(same & (j >= i), v, 0).astype(np.float32)
    reff = (np.arange(128) // 64) * 64 + 31
    refb = (np.arange(128) // 64) * 64 + 32
    tri[:, 0, :] = Tf
    tri[:, 1, :] = Tf - Tf[:, reff]
    tri[:, 2, :] = np.where(same & (j > i), v, 0)
    tri[:, 3, :] = Tb
    tri[:, 4, :] = Tb - Tb[:, refb]
    tri[:, 5, :] = np.where(same & (j < i), v, 0)
    sh["tri"] = tri
    jl = (np.arange(128) % 64)[:, None]
    il = np.arange(64)[None, :]
    mk = np.zeros((128, 2, 4, 64), np.float32)
    mk[:, 0, :, :] = (jl <= il).astype(np.float32)[:, None, :]
    mk[:, 1, :, :] = (jl >= il).astype(np.float32)[:, None, :]
    sh["mask"] = mk.reshape(128, 2, 256)
    half = 32
    inv = (10000.0 ** (-np.arange(half, dtype=np.float32) / half)).astype(np.float32)
    v64 = (10000.0 ** (-np.arange(half, dtype=np.float64) / half)) / (2 * np.pi)
    hi = v64.astype(np.float32)
    lo = (v64 - hi.astype(np.float64)).astype(np.float32)
    sh["invf"] = np.ascontiguousarray(np.stack([np.tile(hi, 4), np.tile(lo, 4)], axis=1))
    return sh


def make_in_maps(inp):
    sh = prep_shared(inp)
    x = np.asarray(inp["x"], dtype=np.float32)
    p = np.asarray(inp["p"], dtype=np.float32)
    pos = np.asarray(inp["positions"], dtype=np.int32)
    maps = []
    for b in range(8):
        m = dict(sh)
        m["x"] = np.ascontiguousarray(x[b])
        m["p"] = np.ascontiguousarray(p[:, b])
        m["pos"] = np.ascontiguousarray(pos[b:b + 1])
        maps.append(m)
    return maps


def kernel(**inputs):
    nc, _ = build()
    maps = make_in_maps(inputs)
    res = run_bass_kernel_spmd(nc, maps, core_ids=list(range(8)))
    return np.stack([np.asarray(r["out"], dtype=np.float32) for r in res.results], axis=0)
```
